# Optimizing a Trainium2 kernel written in Bass

```python
import math
import jax, jax.numpy as jnp
from jax import lax
import numpy as np

D_MODEL = 1024
BATCH = 8
SEQ = 4096
DEPTH = 2

SSM_GROUP_CH = 16
SSM_GROUPS = D_MODEL // 32
SSM_WIDTH = SSM_GROUPS * SSM_GROUP_CH
SSM_STATE = 64
DT_MIN = 1e-3
DT_MAX = 1e-1
EIG_CLIP = 1e-4
HEAD_DIM = 64
ATTN_HEADS = D_MODEL // 128
ATTN_WIDTH = ATTN_HEADS * HEAD_DIM
Q_BLOCK = 128
N_IN = SSM_WIDTH + 3 * ATTN_WIDTH + ATTN_HEADS + 2 * D_MODEL
D_FF = ((8 * D_MODEL + 3 * 256 - 1) // (3 * 256)) * 256
N_MOD = 6
RMS_EPS = 1e-6

kernel_name = "hybrid_s5_fox_gated_block"


def rmsnorm(x, g):
    xf = x.astype(jnp.float32)
    r = lax.rsqrt(jnp.mean(xf * xf, axis=-1, keepdims=True) + RMS_EPS)
    return (xf * r * g.astype(jnp.float32)).astype(x.dtype)


def _linear_recurrence(e1, e2):
    a1, b1 = e1
    a2, b2 = e2
    return a1 * a2, a2 * b1 + b2


def s5_branch(u, lam_re, lam_im, log_dt, b_re, b_im, c_re, c_im, d_skip, w_glu, b_glu):
    dtype = u.dtype
    bsz, s, _ = u.shape
    f32 = jnp.float32
    uf = u.astype(f32).reshape(bsz, s, SSM_GROUPS, SSM_GROUP_CH)
    lam = lax.complex(jnp.minimum(lam_re.astype(f32), -EIG_CLIP), lam_im.astype(f32))
    dt = jnp.exp(log_dt.astype(f32))[:, None]
    lam_bar = jnp.exp(lam * dt)
    b = lax.complex(b_re.astype(f32), b_im.astype(f32))
    b_bar = ((lam_bar - 1.0) / lam)[..., None] * b
    bu = jnp.einsum('bsgh,gph->bsgp', uf, b_bar)
    a = jnp.broadcast_to(lam_bar, bu.shape)
    _, states = lax.associative_scan(_linear_recurrence, (a, bu), axis=1)
    cm = lax.complex(c_re.astype(f32), c_im.astype(f32))
    y = jnp.real(jnp.einsum('bsgp,ghp->bsgh', states, cm))
    y = y + d_skip.astype(f32).reshape(SSM_GROUPS, SSM_GROUP_CH) * uf
    y = y.reshape(bsz, s, SSM_WIDTH).astype(dtype)
    z = jax.nn.gelu(y)
    return z * jax.nn.sigmoid(z @ w_glu + b_glu)


def forgetting_attention(q, k, v, f_logit, b_f):
    bsz, s, _ = q.shape
    nb = s // Q_BLOCK
    f32 = jnp.float32
    q = q.reshape(bsz, s, ATTN_HEADS, HEAD_DIM).transpose(0, 2, 1, 3)
    k = k.reshape(bsz, s, ATTN_HEADS, HEAD_DIM).transpose(0, 2, 1, 3)
    v = v.reshape(bsz, s, ATTN_HEADS, HEAD_DIM).transpose(0, 2, 1, 3)
    log_f = jax.nn.log_sigmoid(f_logit.astype(f32) + b_f.astype(f32))
    cum = jnp.cumsum(log_f, axis=1).transpose(0, 2, 1)
    q_blocks = q.reshape(bsz, ATTN_HEADS, nb, Q_BLOCK, HEAD_DIM).transpose(2, 0, 1, 3, 4)
    cum_blocks = cum.reshape(bsz, ATTN_HEADS, nb, Q_BLOCK).transpose(2, 0, 1, 3)
    k_pos = jnp.arange(s)
    scale = HEAD_DIM ** -0.5

    def one_block(args):
        qb, cb, i = args
        logits = jnp.einsum('bhqd,bhkd->bhqk', qb, k).astype(f32) * scale
        logits = logits + cb[..., None] - cum[:, :, None, :]
        q_pos = i * Q_BLOCK + jnp.arange(Q_BLOCK)
        logits = jnp.where(k_pos[None, :] <= q_pos[:, None], logits, -jnp.inf)
        p = jax.nn.softmax(logits, axis=-1).astype(v.dtype)
        return jnp.einsum('bhqk,bhkd->bhqd', p, v)

    out = lax.map(one_block, (q_blocks, cum_blocks, jnp.arange(nb)))
    return out.transpose(1, 0, 3, 2, 4).reshape(bsz, s, ATTN_WIDTH)


def setup_inputs(seed: int = 0) -> dict:
    key = jax.random.key(seed)
    ks = jax.random.split(key, 32)
    f32 = jnp.float32
    nrm = lambda k, shape, s: jax.random.normal(k, shape, f32) * s
    L, D, G, P, H = DEPTH, D_MODEL, SSM_GROUPS, SSM_STATE, SSM_GROUP_CH
    lam_im0 = jnp.pi * jnp.arange(P, dtype=f32)
    return {
        "x": nrm(ks[0], (BATCH, SEQ, D), 1.0),
        "c": nrm(ks[1], (BATCH, D), 1.0),
        "w_ada": nrm(ks[2], (L, D, N_MOD * D), 0.5 * D ** -0.5),
        "b_ada": nrm(ks[3], (L, N_MOD * D), 0.02),
        "g_pre_mix": 1.0 + nrm(ks[4], (L, D), 0.02),
        "g_post_mix": 1.0 + nrm(ks[5], (L, D), 0.02),
        "g_pre_ffn": 1.0 + nrm(ks[6], (L, D), 0.02),
        "g_post_ffn": 1.0 + nrm(ks[7], (L, D), 0.02),
        "w_in": nrm(ks[8], (L, D, N_IN), D ** -0.5),
        "lam_re": -0.5 + nrm(ks[9], (L, G, P), 0.01),
        "lam_im": lam_im0 + nrm(ks[10], (L, G, P), 0.01),
        "log_dt": jax.random.uniform(ks[11], (L, G), f32, math.log(DT_MIN), math.log(DT_MAX)),
        "b_re": nrm(ks[12], (L, G, P, H), (2 * H) ** -0.5),
        "b_im": nrm(ks[13], (L, G, P, H), (2 * H) ** -0.5),
        "c_re": nrm(ks[14], (L, G, H, P), (2 * P) ** -0.5),
        "c_im": nrm(ks[15], (L, G, H, P), (2 * P) ** -0.5),
        "d_skip": nrm(ks[16], (L, SSM_WIDTH), 1.0),
        "w_glu": nrm(ks[17], (L, SSM_WIDTH, SSM_WIDTH), SSM_WIDTH ** -0.5),
        "b_glu": nrm(ks[18], (L, SSM_WIDTH), 0.02),
        "b_f": jax.random.uniform(ks[19], (L, ATTN_HEADS), f32, 1.0, 5.0),
        "w_pa": nrm(ks[20], (L, SSM_WIDTH, D), SSM_WIDTH ** -0.5),
        "w_pb": nrm(ks[21], (L, ATTN_WIDTH, D), ATTN_WIDTH ** -0.5),
        "w_o": nrm(ks[22], (L, D, D), D ** -0.5),
        "w_ffn_gate": nrm(ks[23], (L, D, D_FF), D ** -0.5),
        "w_ffn_up": nrm(ks[24], (L, D, D_FF), D ** -0.5),
        "w_ffn_down": nrm(ks[25], (L, D_FF, D), D_FF ** -0.5),
    }


def reference(x, c, w_ada, b_ada, g_pre_mix, g_post_mix, g_pre_ffn, g_post_ffn, w_in,
              lam_re, lam_im, log_dt, b_re, b_im, c_re, c_im, d_skip, w_glu, b_glu, b_f,
              w_pa, w_pb, w_o, w_ffn_gate, w_ffn_up, w_ffn_down):
    split_at = np.cumsum([SSM_WIDTH, ATTN_WIDTH, ATTN_WIDTH, ATTN_WIDTH, ATTN_HEADS, D_MODEL]).tolist()
    cond = jax.nn.silu(c)
    for l in range(DEPTH):
        mod = cond @ w_ada[l] + b_ada[l]
        shift_m, scale_m, gate_m, shift_f, scale_f, gate_f = jnp.split(mod[:, None, :], N_MOD, axis=-1)

        h = rmsnorm(x, g_pre_mix[l]) * (1.0 + scale_m) + shift_m
        proj = h @ w_in[l]
        u_ssm, q, k, v, f_logit, g_a, g_b = jnp.split(proj, split_at, axis=-1)
        y_ssm = s5_branch(u_ssm, lam_re[l], lam_im[l], log_dt[l], b_re[l], b_im[l],
                          c_re[l], c_im[l], d_skip[l], w_glu[l], b_glu[l])
        y_att = forgetting_attention(q, k, v, f_logit, b_f[l])
        merged = jax.nn.sigmoid(g_a) * (y_ssm @ w_pa[l]) + jax.nn.sigmoid(g_b) * (y_att @ w_pb[l])
        y = merged @ w_o[l]
        x = x + gate_m * rmsnorm(y, g_post_mix[l])

        h = rmsnorm(x, g_pre_ffn[l]) * (1.0 + scale_f) + shift_f
        y = (jax.nn.silu(h @ w_ffn_gate[l]) * (h @ w_ffn_up[l])) @ w_ffn_down[l]
        x = x + gate_f * rmsnorm(y, g_post_ffn[l])
    return x
```

```python
import math
from contextlib import ExitStack

import numpy as np
import concourse.bass as bass
import concourse.mybir as mybir
from concourse.bass_utils import run_bass_kernel_spmd

F32 = mybir.dt.float32
BF16 = mybir.dt.bfloat16
I32 = mybir.dt.int32
ALU = mybir.AluOpType
AF = mybir.ActivationFunctionType

D = 1024
KC = 8
L = 2
DFF = 2816
FC = 22
NIN = 4104
NT = 512
EPS = 1e-6
PI = math.pi


class Buf:
    __slots__ = ("name", "w", "r")

    def __init__(self, name):
        self.name = name
        self.w = {}
        self.r = {}


class DSem:
    def __init__(self, sem):
        self.sem = sem
        self.issued = 0


class Eng:
    def __init__(self, name, h, sem):
        self.name = name
        self.h = h
        self.sem = sem
        self.cnt = 0
        self.waited = {}


class K:
    def __init__(self, nc, es):
        self.nc = nc
        self.es = es
        self.nsem = 0
        self.pe = Eng("pe", nc.tensor, self._sem("pe"))
        self.act = Eng("act", nc.scalar, self._sem("act"))
        self.dve = Eng("dve", nc.vector, self._sem("dve"))
        self.pool = Eng("pool", nc.gpsimd, self._sem("pool"))
        self.sp = Eng("sp", nc.sync, None)
        self.engs = [self.pe, self.act, self.dve, self.pool, self.sp]
        self.dsems = []
        self._occ = {}
        self._dcache = {}

    def _sem(self, name):
        self.nsem += 1
        return self.es.enter_context(self.nc.semaphore("s_%s_%d" % (name, self.nsem)))

    def dsem(self, name="d"):
        occ = self._occ.get(name, 0)
        self._occ[name] = occ + 1
        key = (name, occ)
        if key not in self._dcache:
            d = DSem(self._sem(name))
            self.dsems.append(d)
            self._dcache[key] = d
        return self._dcache[key]

    def layer_begin(self):
        self._occ = {}

    def _wait(self, eng, sem, val, is_dma):
        if (not is_dma) and sem is eng.sem and eng is self.pe:
            return
        key = id(sem)
        if eng.waited.get(key, 0) >= val:
            return
        eng.h.wait_ge(sem, val)
        eng.waited[key] = val

    def _deps(self, eng, reads, writes):
        for b in reads:
            for (sem, val, ds) in b.w.values():
                self._wait(eng, sem, ds.issued if ds is not None else val, ds is not None)
        for b in writes:
            for (sem, val, ds) in b.w.values():
                self._wait(eng, sem, ds.issued if ds is not None else val, ds is not None)
            for (sem, val, ds) in b.r.values():
                self._wait(eng, sem, ds.issued if ds is not None else val, ds is not None)

    def _record(self, tok, reads, writes):
        key = id(tok[0])
        for b in reads:
            b.r[key] = tok
        for b in writes:
            b.w = {key: tok}
            b.r = {}

    def op(self, eng, fn, reads=(), writes=()):
        self._deps(eng, reads, writes)
        ins = fn(eng.h)
        eng.cnt += 1
        ins.then_inc(eng.sem, 1)
        self._record((eng.sem, eng.cnt, None), reads, writes)

    def mm(self, out_buf, out_ap, pairs, reads, tile_position=None, start=True, stop=True):
        eng = self.pe
        self._deps(eng, reads, [out_buf])
        n = len(pairs)
        ins = None
        for i, (lhsT, rhs) in enumerate(pairs):
            kw = {}
            if tile_position is not None:
                kw["tile_position"] = tile_position
            ins = eng.h.matmul(out_ap, lhsT=lhsT, rhs=rhs, start=(start and i == 0),
                               stop=(stop and i == n - 1), **kw)
        eng.cnt += 1
        ins.then_inc(eng.sem, 1)
        self._record((eng.sem, eng.cnt, None), reads, [out_buf])

    def mm_multi(self, out_buf, groups, reads):
        eng = self.pe
        self._deps(eng, reads, [out_buf])
        ins = None
        for (out_ap, pairs, tp) in groups:
            n = len(pairs)
            for i, (lhsT, rhs) in enumerate(pairs):
                kw = {}
                if tp is not None:
                    kw["tile_position"] = tp
                ins = eng.h.matmul(out_ap, lhsT=lhsT, rhs=rhs, start=(i == 0), stop=(i == n - 1), **kw)
        eng.cnt += 1
        ins.then_inc(eng.sem, 1)
        self._record((eng.sem, eng.cnt, None), reads, [out_buf])

    def dma(self, eng, out, in_, ds, reads=(), writes=()):
        self._deps(eng, reads, writes)
        eng.h.dma_start(out=out, in_=in_).then_inc(ds.sem, 16)
        ds.issued += 16
        self._record((ds.sem, ds.issued, ds), reads, writes)

    def barrier(self):
        for e in self.engs:
            for o in self.engs:
                if o.sem is not None and o is not e and o.cnt > 0:
                    self._wait(e, o.sem, o.cnt, False)
            for d in self.dsems:
                if d.issued > 0:
                    self._wait(e, d.sem, d.issued, True)

    def final_wait(self):
        self.barrier()


def build_nc(S, debug=False, nl=L):
    NTL = S // NT
    NCH = S // 8
    KT = S // 128
    LV = int(round(math.log2(NCH)))
    assert 2 ** LV == NCH and NCH <= 512

    nc = bass.Bass("TRN2", target_bir_lowering=False)
    es = ExitStack()
    k = K(nc, es)
    PE, ACT, DVE, POOL, SP = k.pe, k.act, k.dve, k.pool, k.sp

    def din(name, shape, dt=F32):
        return nc.dram_tensor(name, list(shape), dt, kind="ExternalInput").ap()

    okind = "ExternalOutput" if debug else "Internal"

    def dscr(name, shape, dt):
        return nc.dram_tensor(name, list(shape), dt, kind=okind).ap()

    xT_in = din("xT", [D, S])
    cT_in = din("cT", [128, KC])
    w_ada = din("w_ada", [L, D, 6 * D])
    b_adaT = din("b_adaT", [L, 128, 48])
    gT = din("gT", [L, 128, 4, KC])
    w_in = din("w_in", [L, D, NIN])
    lamT = din("lamT", [L, 128, 2, 16])
    ldtT = din("ldtT", [L, 128, 16])
    bT = din("bT", [L, 128, 2, 16, 16])
    cTT = din("cTT", [L, 128, 2, 16, 16])
    dskT = din("dskT", [L, 128, 4])
    w_glu = din("w_glu", [L, 512, 512])
    b_gluT = din("b_gluT", [L, 128, 4])
    b_f = din("b_f", [L, 8, 1])
    w_pa = din("w_pa", [L, 512, D])
    w_pb = din("w_pb", [L, 512, D])
    w_o = din("w_o", [L, D, D])
    w_g = din("w_g", [L, D, DFF])
    w_u = din("w_u", [L, D, DFF])
    w_d = din("w_d", [L, DFF, D])
    tri_in = din("tri", [128, 128])
    ident_in = din("ident", [128, 128])
    bdm_in = din("bdm", [128, 128])

    yT_out = nc.dram_tensor("yT", [D, S], F32, kind="ExternalOutput").ap()
    xa_d = dscr("xa_d", [D, S], F32)
    xb_d = dscr("xb_d", [D, S], F32)
    qT_d = dscr("qT_d", [512, S], BF16)
    kT_d = dscr("kT_d", [512, S], BF16)
    v_d = dscr("v_d", [S, 512], BF16)
    cumq_d = dscr("cumq_d", [8, 3, S], BF16)
    cumk_d = dscr("cumk_d", [8, 3, S], BF16)
    yssm_d = dscr("yssm_d", [512, S], BF16)
    yatt_d = dscr("yatt_d", [512, S], BF16)
    aT_d = dscr("aT_d", [DFF, S], BF16)
    Kw_d = dscr("Kw_d", [L, 128, 8 * 4 * 128], BF16)
    WE_d = dscr("WE_d", [L, 128, 8 * 2 * 4 * 128], BF16)
    WI_d = dscr("WI_d", [L, 128, 16 * 8 * 2 * 32], BF16)
    Ad_d = dscr("Ad_d", [L, 128, 3 * LV * 16], F32)

    uid = [0]

    def sb(stack, name, shape, dt):
        uid[0] += 1
        return stack.enter_context(nc.sbuf_tensor("sb%d_%s" % (uid[0], name), list(shape), dt))

    psum = es.enter_context(nc.psum_tensor("psum", [128, 8, 512], F32))
    banks = [Buf("bank%d" % i) for i in range(8)]
    bank_rr = [0]

    def next_bank(lo=0, hi=8):
        n = hi - lo
        i = lo + (bank_rr[0] % n)
        bank_rr[0] += 1
        return banks[i], psum[:, i, :]

    ones_bf = sb(es, "ones_bf", [128, 128], BF16); B_ones = Buf("ones")
    tri_bf = sb(es, "tri_bf", [128, 128], BF16); B_tri = Buf("tri")
    ident_bf = sb(es, "ident_bf", [128, 128], BF16); B_ident = Buf("ident")
    bdm_f = sb(es, "bdm_f", [128, 128], F32); B_bdm = Buf("bdm")
    epsb = sb(es, "epsb", [128, 1], F32); B_eps = Buf("eps")
    vecs = sb(es, "vecs", [128, L, 6, KC], F32); B_vecs = Buf("vecs")
    ds_const = k.dsem("const")
    ds_const_sw = k.dsem("constsw")

    k.op(DVE, lambda e: e.memset(ones_bf[:], 1.0), writes=[B_ones])
    k.op(DVE, lambda e: e.memset(epsb[:], float(D) * EPS), writes=[B_eps])
    k.dma(POOL, tri_bf[:], tri_in, ds_const_sw, writes=[B_tri])
    k.dma(POOL, ident_bf[:], ident_in, ds_const_sw, writes=[B_ident])
    k.dma(SP, bdm_f[:], bdm_in, ds_const, writes=[B_bdm])

    with ExitStack() as ps_:
        cT = sb(ps_, "cT", [128, KC], F32); B_c = Buf("c")
        cond = sb(ps_, "cond", [128, KC], F32); B_cond = Buf("cond")
        modT = sb(ps_, "modT", [128, L, 48], F32); B_mod = Buf("mod")
        badaS = sb(ps_, "badaS", [128, L, 48], F32); B_bada = Buf("bada")
        gS = sb(ps_, "gS", [128, L, 4, KC], F32); B_g = Buf("g")
        tmpv = sb(ps_, "tmpv", [128, KC], F32); B_tmpv = Buf("tmpv")
        wab = [sb(ps_, "wab%d" % i, [128, KC, 768], F32) for i in range(3)]
        B_wab = [Buf("wab%d" % i) for i in range(3)]
        ds_wab = [k.dsem("wab") for _ in range(3)]
        k.dma(SP, cT[:], cT_in, ds_const, writes=[B_c])
        for l in range(L):
            k.dma(SP, badaS[:, l, :], b_adaT[l], ds_const, writes=[B_bada])
            k.dma(SP, gS[:, l, :, :], gT[l], ds_const, writes=[B_g])
        k.op(ACT, lambda e: e.activation(out=cond[:], in_=cT[:], func=AF.Silu), reads=[B_c], writes=[B_cond])

        Kw = sb(ps_, "KwP", [128, 8, 4, 128], BF16); B_Kw = Buf("KwP")
        WE = sb(ps_, "WEP", [128, 8, 2, 4, 128], BF16); B_WE = Buf("WEP")
        WI = sb(ps_, "WIP", [128, 16, 8, 2, 32], BF16); B_WI = Buf("WIP")
        Ad3 = sb(ps_, "Ad3", [128, 3, LV, 16], F32); B_Ad = Buf("AdP")
        Adr, Adi, Adn = Ad3[:, 0, :, :], Ad3[:, 1, :, :], Ad3[:, 2, :, :]
        ds_p = k.dsem("s5p")
        ds_po = k.dsem("s5po")

        def prep_layer(l):
            lam = sb(ps_, "lam", [128, 2, 16], F32); B_lam = Buf("lam")
            ldt = sb(ps_, "ldt", [128, 16], F32); B_ldt = Buf("ldt")
            bS = sb(ps_, "bS", [128, 2, 16, 16], F32); B_bS = Buf("bS")
            cS = sb(ps_, "cS", [128, 2, 16, 16], F32); B_cS = Buf("cS")
            k.dma(SP, lam[:], lamT[l], ds_p, writes=[B_lam])
            k.dma(SP, ldt[:], ldtT[l], ds_p, writes=[B_ldt])
            k.dma(SP, bS[:], bT[l], ds_p, writes=[B_bS])
            k.dma(SP, cS[:], cTT[l], ds_p, writes=[B_cS])
            NS = 16
            sc = sb(ps_, "sc", [128, NS, 16], F32)
            sci = sb(ps_, "sci", [128, 16], I32)
            B_sc = Buf("sc")

            def V(fn, extra_r=(), extra_w=()):
                k.op(DVE, fn, reads=[B_sc] + list(extra_r), writes=[B_sc] + list(extra_w))

            def A_(fn, extra_r=()):
                k.op(ACT, fn, reads=[B_sc] + list(extra_r), writes=[B_sc])

            DT, LR, AR, TH, MAG, T1, T2, SN, CS, LBR, LBI, FR, FI, T3, T4, T5 = [sc[:, i, :] for i in range(NS)]
            LAMRE, LAMIM = lam[:, 0, :], lam[:, 1, :]
            A_(lambda e: e.activation(out=DT, in_=ldt[:], func=AF.Exp), [B_ldt])
            V(lambda e: e.tensor_scalar(out=LR, in0=LAMRE, scalar1=-1e-4, scalar2=None, op0=ALU.min), [B_lam])
            V(lambda e: e.tensor_tensor(out=AR, in0=LR, in1=DT, op=ALU.mult))
            V(lambda e: e.tensor_tensor(out=TH, in0=LAMIM, in1=DT, op=ALU.mult), [B_lam])
            A_(lambda e: e.activation(out=MAG, in_=AR, func=AF.Exp))

            def range_reduce(dst, shift):
                V(lambda e: e.tensor_scalar(out=T1, in0=TH, scalar1=shift, scalar2=1.0 / (2 * PI), op0=ALU.add,
                                            op1=ALU.mult))
                V(lambda e: e.tensor_copy(out=sci[:], in_=T1))
                V(lambda e: e.tensor_copy(out=T1, in_=sci[:]))
                V(lambda e: e.tensor_scalar(out=T2, in0=TH, scalar1=shift, scalar2=None, op0=ALU.add))
                V(lambda e: e.scalar_tensor_tensor(out=T2, in0=T1, scalar=-2 * PI, in1=T2, op0=ALU.mult, op1=ALU.add))
                V(lambda e: e.tensor_scalar(out=T1, in0=T2, scalar1=PI, scalar2=-2 * PI, op0=ALU.is_gt, op1=ALU.mult))
                V(lambda e: e.tensor_tensor(out=T2, in0=T2, in1=T1, op=ALU.add))
                V(lambda e: e.tensor_scalar(out=T1, in0=T2, scalar1=-PI, scalar2=2 * PI, op0=ALU.is_lt, op1=ALU.mult))
                V(lambda e: e.tensor_tensor(out=T2, in0=T2, in1=T1, op=ALU.add))
                V(lambda e: e.tensor_scalar(out=dst, in0=T2, scalar1=-3.141592, scalar2=3.141592, op0=ALU.max,
                                            op1=ALU.min))

            yield
            range_reduce(T3, 0.0)
            yield
            A_(lambda e: e.activation(out=SN, in_=T3, func=AF.Sin))
            range_reduce(T3, PI / 2)
            A_(lambda e: e.activation(out=CS, in_=T3, func=AF.Sin))
            V(lambda e: e.tensor_tensor(out=LBR, in0=MAG, in1=CS, op=ALU.mult))
            V(lambda e: e.tensor_tensor(out=LBI, in0=MAG, in1=SN, op=ALU.mult))
            V(lambda e: e.tensor_tensor(out=T1, in0=LR, in1=LR, op=ALU.mult))
            V(lambda e: e.tensor_tensor(out=T2, in0=LAMIM, in1=LAMIM, op=ALU.mult), [B_lam])
            V(lambda e: e.tensor_tensor(out=T1, in0=T1, in1=T2, op=ALU.add))
            V(lambda e: e.reciprocal(out=T1, in_=T1))
            V(lambda e: e.tensor_scalar(out=T2, in0=LBR, scalar1=-1.0, scalar2=None, op0=ALU.add))
            V(lambda e: e.tensor_tensor(out=T3, in0=T2, in1=LR, op=ALU.mult))
            V(lambda e: e.tensor_tensor(out=T4, in0=LBI, in1=LAMIM, op=ALU.mult), [B_lam])
            V(lambda e: e.tensor_tensor(out=T3, in0=T3, in1=T4, op=ALU.add))
            V(lambda e: e.tensor_tensor(out=FR, in0=T3, in1=T1, op=ALU.mult))
            V(lambda e: e.tensor_tensor(out=T3, in0=LBI, in1=LR, op=ALU.mult))
            V(lambda e: e.tensor_tensor(out=T4, in0=T2, in1=LAMIM, op=ALU.mult), [B_lam])
            V(lambda e: e.tensor_tensor(out=T3, in0=T3, in1=T4, op=ALU.subtract))
            V(lambda e: e.tensor_tensor(out=FI, in0=T3, in1=T1, op=ALU.mult))
            Pr = sb(ps_, "Pr", [128, 9, 16], F32)
            Pi_ = sb(ps_, "Pi", [128, 9, 16], F32)
            V(lambda e: e.memset(Pr[:, 0, :], 1.0))
            V(lambda e: e.memset(Pi_[:, 0, :], 0.0))

            def cmul(o_r, o_i, a_r, a_i, b_r, b_i):
                V(lambda e: e.tensor_tensor(out=T4, in0=a_r, in1=b_r, op=ALU.mult))
                V(lambda e: e.tensor_tensor(out=T5, in0=a_i, in1=b_i, op=ALU.mult))
                V(lambda e: e.tensor_tensor(out=T3, in0=a_r, in1=b_i, op=ALU.mult))
                V(lambda e: e.tensor_tensor(out=T1, in0=a_i, in1=b_r, op=ALU.mult))
                V(lambda e: e.tensor_tensor(out=o_r, in0=T4, in1=T5, op=ALU.subtract))
                V(lambda e: e.tensor_tensor(out=o_i, in0=T3, in1=T1, op=ALU.add))

            for tau in range(8):
                cmul(Pr[:, tau + 1, :], Pi_[:, tau + 1, :], Pr[:, tau, :], Pi_[:, tau, :], LBR, LBI)
                yield
            V(lambda e: e.tensor_copy(out=Adr[:, 0, :], in_=Pr[:, 8, :]), extra_w=[B_Ad])
            V(lambda e: e.tensor_copy(out=Adi[:, 0, :], in_=Pi_[:, 8, :]), extra_w=[B_Ad])
            for lv in range(LV - 1):
                cmul(Adr[:, lv + 1, :], Adi[:, lv + 1, :], Adr[:, lv, :], Adi[:, lv, :], Adr[:, lv, :], Adi[:, lv, :])
                yield
            V(lambda e: e.tensor_scalar(out=Adn[:], in0=Adi[:], scalar1=-1.0, scalar2=None, op0=ALU.mult),
              extra_w=[B_Ad])
            Bbr = sb(ps_, "Bbr", [128, 16, 16], F32)
            Bbi = sb(ps_, "Bbi", [128, 16, 16], F32)
            W1 = sb(ps_, "W1", [128, 16, 16], F32)
            W2 = sb(ps_, "W2", [128, 16, 16], F32)

            def bc(ap2):
                return ap2.unsqueeze(2).broadcast_to([ap2.shape[0], 16, 16])

            V(lambda e: e.tensor_tensor(out=W1[:], in0=bS[:, 0, :, :], in1=bc(FR), op=ALU.mult), [B_bS])
            V(lambda e: e.tensor_tensor(out=W2[:], in0=bS[:, 1, :, :], in1=bc(FI), op=ALU.mult), [B_bS])
            V(lambda e: e.tensor_tensor(out=Bbr[:], in0=W1[:], in1=W2[:], op=ALU.subtract))
            V(lambda e: e.tensor_tensor(out=W1[:], in0=bS[:, 1, :, :], in1=bc(FR), op=ALU.mult), [B_bS])
            V(lambda e: e.tensor_tensor(out=W2[:], in0=bS[:, 0, :, :], in1=bc(FI), op=ALU.mult), [B_bS])
            V(lambda e: e.tensor_tensor(out=Bbi[:], in0=W1[:], in1=W2[:], op=ALU.add))
            Xp = sb(ps_, "Xp", [128, 8, 2, 512], BF16)
            Yp = sb(ps_, "Yp", [128, 2, 512], BF16)
            V(lambda e: e.memset(Xp[:], 0.0))
            V(lambda e: e.memset(Yp[:], 0.0))
            V(lambda e: e.memset(WI[:], 0.0), extra_w=[B_WI])

            def slot(ap_cols, e_):
                return ap_cols.rearrange("p (q e h) -> p q e h", q=16, e=2)[:, :, e_, :]

            for tau in range(8):
                for e_ in range(2):
                    hs = slice(e_ * 64, (e_ + 1) * 64)
                    pr = bc(Pr[hs, tau, :]); pi = bc(Pi_[hs, tau, :])
                    V(lambda e: e.tensor_tensor(out=W1[hs], in0=Bbr[hs], in1=pr, op=ALU.mult))
                    V(lambda e: e.tensor_tensor(out=W2[hs], in0=Bbi[hs], in1=pi, op=ALU.mult))
                    V(lambda e: e.tensor_tensor(out=slot(Xp[hs, tau, 0, :], e_), in0=W1[hs], in1=W2[hs], op=ALU.subtract))
                    V(lambda e: e.tensor_tensor(out=W1[hs], in0=Bbr[hs], in1=pi, op=ALU.mult))
                    V(lambda e: e.tensor_tensor(out=W2[hs], in0=Bbi[hs], in1=pr, op=ALU.mult))
                    V(lambda e: e.tensor_tensor(out=slot(Xp[hs, tau, 1, :], e_), in0=W1[hs], in1=W2[hs], op=ALU.add))
                    yield
            for e_ in range(2):
                hs = slice(e_ * 64, (e_ + 1) * 64)
                V(lambda e: e.tensor_copy(out=slot(Yp[hs, 0, :], e_), in_=cS[hs, 0, :, :]), [B_cS])
                V(lambda e: e.tensor_scalar(out=slot(Yp[hs, 1, :], e_), in0=cS[hs, 1, :, :], scalar1=-1.0,
                                            scalar2=None, op0=ALU.mult), [B_cS])
                for t_ in range(8):
                    pr = bc(Pr[hs, t_ + 1, :]); pi = bc(Pi_[hs, t_ + 1, :])
                    V(lambda e: e.tensor_tensor(out=W1[hs], in0=cS[hs, 0, :, :], in1=pr, op=ALU.mult), [B_cS])
                    V(lambda e: e.tensor_tensor(out=W2[hs], in0=cS[hs, 1, :, :], in1=pi, op=ALU.mult), [B_cS])
                    V(lambda e: e.tensor_tensor(out=WI[hs, :, t_, 0, e_ * 16:(e_ + 1) * 16], in0=W1[hs], in1=W2[hs],
                                                op=ALU.subtract), extra_w=[B_WI])
                    V(lambda e: e.tensor_tensor(out=W1[hs], in0=cS[hs, 0, :, :], in1=pi, op=ALU.mult), [B_cS])
                    V(lambda e: e.tensor_tensor(out=W2[hs], in0=cS[hs, 1, :, :], in1=pr, op=ALU.mult), [B_cS])
                    V(lambda e: e.tensor_tensor(out=W1[hs], in0=W1[hs], in1=W2[hs], op=ALU.add))
                    V(lambda e: e.tensor_scalar(out=WI[hs, :, t_, 1, e_ * 16:(e_ + 1) * 16], in0=W1[hs], scalar1=-1.0,
                                                scalar2=None, op0=ALU.mult), extra_w=[B_WI])
                    yield
            for tau in range(8):
                bk, bap = next_bank(0, 6)
                k.mm_multi(bk, [(bap[:, ct * 128:(ct + 1) * 128],
                                 [(Xp[:, tau, 0, ct * 128:(ct + 1) * 128], Yp[:, 0, ct * 128:(ct + 1) * 128]),
                                  (Xp[:, tau, 1, ct * 128:(ct + 1) * 128], Yp[:, 1, ct * 128:(ct + 1) * 128])], None)
                                for ct in range(4)], reads=[B_sc])
                for ct in range(4):
                    k.op(DVE, lambda e: e.tensor_tensor(out=Kw[:, tau, ct, :], in0=bap[:, ct * 128:(ct + 1) * 128],
                                                        in1=bdm_f[:], op=ALU.mult),
                         reads=[bk, B_bdm], writes=[B_Kw])
            for s_ in range(8):
                for ri in range(2):
                    bk, bap = next_bank(0, 6)
                    k.mm_multi(bk, [(bap[:, ct * 128:(ct + 1) * 128],
                                     [(Xp[:, 7 - s_, ri, ct * 128:(ct + 1) * 128], ident_bf[:])], None)
                                    for ct in range(4)], reads=[B_sc, B_ident])
                    k.op(ACT, lambda e: e.copy(out=WE[:, s_, ri, :, :].rearrange("p c n -> p (c n)"), in_=bap),
                         reads=[bk], writes=[B_WE])
            yield
            k.dma(ACT, Kw_d[l], Kw[:].rearrange("p a b c -> p (a b c)"), ds_po, reads=[B_Kw])
            k.dma(ACT, WE_d[l], WE[:].rearrange("p a b c d -> p (a b c d)"), ds_po, reads=[B_WE])
            k.dma(ACT, WI_d[l], WI[:].rearrange("p a b c d -> p (a b c d)"), ds_po, reads=[B_WI])
            k.dma(ACT, Ad_d[l], Ad3[:].rearrange("p a b c -> p (a b c)"), ds_po, reads=[B_Ad])
            yield

        def prep_all():
            for l_ in range(nl):
                for _ in prep_layer(l_):
                    yield

        prep_gen = prep_all()

        it = 0
        for l in range(L):
            bk, bap = banks[6 + l % 2], psum[:, 6 + l % 2, :]
            wv = w_ada[l].rearrange("(k p) n -> p k n", p=128)
            for jc in range(8):
                s = it % 3
                it += 1
                k.dma(SP, wab[s][:], wv[:, :, jc * 768:(jc + 1) * 768], ds_wab[s], writes=[B_wab[s]])
                for jj in range(6):
                    j = jc * 6 + jj
                    k.mm(bk, bap[:, j:j + 1],
                         [(wab[s][:, kk, jj * 128:(jj + 1) * 128], cond[:, kk:kk + 1]) for kk in range(KC)],
                         reads=[B_wab[s], B_cond])
                    next(prep_gen, None)
            k.op(DVE, lambda e: e.tensor_tensor(out=modT[:, l, :], in0=bap[:, 0:48], in1=badaS[:, l, :], op=ALU.add),
                 reads=[bk, B_bada], writes=[B_mod])
            for (o, isc, ig) in ((0, 1, 0), (3, 4, 2)):
                k.op(DVE, lambda e: e.tensor_scalar(out=tmpv[:], in0=modT[:, l, isc * 8:(isc + 1) * 8], scalar1=1.0,
                                                    scalar2=32.0, op0=ALU.add, op1=ALU.mult),
                     reads=[B_mod], writes=[B_tmpv])
                k.op(DVE, lambda e: e.tensor_tensor(out=vecs[:, l, o, :], in0=tmpv[:], in1=gS[:, l, ig, :], op=ALU.mult),
                     reads=[B_tmpv, B_g], writes=[B_vecs])
            for (o, ish) in ((1, 0), (4, 3)):
                k.op(DVE, lambda e: e.tensor_copy(out=vecs[:, l, o, :], in_=modT[:, l, ish * 8:(ish + 1) * 8]),
                     reads=[B_mod], writes=[B_vecs])
            for (o, iga, ig) in ((2, 2, 1), (5, 5, 3)):
                k.op(DVE, lambda e: e.tensor_scalar(out=tmpv[:], in0=modT[:, l, iga * 8:(iga + 1) * 8], scalar1=32.0,
                                                    scalar2=None, op0=ALU.mult),
                     reads=[B_mod], writes=[B_tmpv])
                k.op(DVE, lambda e: e.tensor_tensor(out=vecs[:, l, o, :], in0=tmpv[:], in1=gS[:, l, ig, :], op=ALU.mult),
                     reads=[B_tmpv, B_g], writes=[B_vecs])
        for _ in prep_gen:
            pass
        k.barrier()

    def load_w(dst, src2d, kc, c0, ncols, dsm, buf, dcol0=0):
        v = src2d.rearrange("(k p) n -> p k n", p=128)
        step = 1024
        for a in range(0, ncols, step):
            n = min(step, ncols - a)
            k.dma(POOL, dst[:, 0:kc, dcol0 + a:dcol0 + a + n], v[:, :, c0 + a:c0 + a + n], dsm, writes=[buf])

    class WChunks:
        def __init__(self, tile, src2d, kc, c0, ncols, chunk, name):
            self.tile, self.src2d, self.kc, self.c0, self.chunk, self.name = tile, src2d, kc, c0, chunk, name
            self.n = (ncols + chunk - 1) // chunk
            self.ncols = ncols
            self.bufs = [Buf("%s_c%d" % (name, i)) for i in range(self.n)]
            self.ds = [k.dsem("%s_c" % name) for _ in range(self.n)]

        def load(self, i):
            a = i * self.chunk
            n = min(self.chunk, self.ncols - a)
            load_w(self.tile, self.src2d, self.kc, self.c0 + a, n, self.ds[i], self.bufs[i], dcol0=a)

        def buf(self, col):
            return self.bufs[col // self.chunk]

    class NormCtx:
        def __init__(self, stack, tag):
            self.sq = sb(stack, "sq" + tag, [128, KC * NT], BF16); self.B_sq = Buf("sq")
            self.rs = sb(stack, "rs" + tag, [128, NT], F32); self.B_rs = Buf("rs")
            self.tmp = [sb(stack, "ntmp%d%s" % (i, tag), [128, NT], F32) for i in range(2)]
            self.B_tmp = [Buf("ntmp%d" % i) for i in range(2)]
            self.i = 0

        def rstd(self, src, B_src):
            k.op(ACT, lambda e: e.activation(out=self.sq[:], in_=src, func=AF.Square), reads=[B_src], writes=[self.B_sq])
            bk, bap = next_bank()
            k.mm(bk, bap, [(ones_bf[:], self.sq[:, kk * NT:(kk + 1) * NT]) for kk in range(KC)],
                 reads=[B_ones, self.B_sq])
            k.op(ACT, lambda e: e.activation(out=self.rs[:], in_=bap, func=AF.Sqrt, bias=epsb[:, 0:1], scale=1.0),
                 reads=[bk, B_eps], writes=[self.B_rs])
            k.op(DVE, lambda e: e.reciprocal(out=self.rs[:], in_=self.rs[:]), reads=[self.B_rs], writes=[self.B_rs])

        def modulate(self, xt, B_x, hT, B_h, l, ia, ib):
            self.rstd(xt[:], B_x)
            for kk in range(KC):
                t = self.i % 2
                self.i += 1
                tm, Bt = self.tmp[t], self.B_tmp[t]
                k.op(DVE, lambda e: e.tensor_tensor(out=tm[:], in0=xt[:, kk * NT:(kk + 1) * NT], in1=self.rs[:], op=ALU.mult),
                     reads=[B_x, self.B_rs], writes=[Bt])
                k.op(ACT, lambda e: e.activation(out=hT[:, kk, :], in_=tm[:], func=AF.Identity,
                                                 scale=vecs[:, l, ia, kk:kk + 1], bias=vecs[:, l, ib, kk:kk + 1]),
                     reads=[Bt, B_vecs], writes=[B_h])

        def residual(self, yt, B_y, xt, B_x, l, ig):
            self.rstd(yt[:], B_y)
            for kk in range(KC):
                t = self.i % 2
                self.i += 1
                tm, Bt = self.tmp[t], self.B_tmp[t]
                k.op(DVE, lambda e: e.scalar_tensor_tensor(out=tm[:], in0=yt[:, kk * NT:(kk + 1) * NT],
                                                           scalar=vecs[:, l, ig, kk:kk + 1], in1=self.rs[:],
                                                           op0=ALU.mult, op1=ALU.mult),
                     reads=[B_y, self.B_rs, B_vecs], writes=[Bt])
                k.op(POOL, lambda e: e.tensor_tensor(out=xt[:, kk * NT:(kk + 1) * NT], in0=xt[:, kk * NT:(kk + 1) * NT],
                                                     in1=tm[:], op=ALU.add),
                     reads=[Bt, B_x], writes=[B_x])

    def x_view(xd, t):
        return xd.rearrange("(k p) n -> p k n", p=128)[:, :, t * NT:(t + 1) * NT]

    def xt3(xt):
        return xt[:].rearrange("p (k n) -> p k n", k=KC)

    for l in range(nl):
        k.layer_begin()
        x_pre = xT_in if l == 0 else xb_d
        x_mid = xa_d
        x_post = yT_out if l == nl - 1 else xb_d

        with ExitStack() as pm:
            uT = sb(pm, "uT", [128, 4, S], BF16); B_u = Buf("uT")
            with ExitStack() as pa0:
              fT = sb(pa0, "fT", [8, S], F32); B_f = Buf("fT")
              with ExitStack() as pa:
                wA = sb(pa, "wA", [128, KC, 2048], BF16)
                WA = WChunks(wA, w_in[l], KC, 0, 2048, 512, "wA")
                for i_ in range(WA.n):
                    WA.load(i_)
                wF = sb(pa, "wF", [128, KC, 32], BF16); B_wF = Buf("wF")
                wf32 = sb(pa, "wf32", [128, KC, 8], F32); B_wf32 = Buf("wf32"); ds_wf = k.dsem("wf")
                k.dma(SP, wf32[:], w_in[l].rearrange("(k p) n -> p k n", p=128)[:, :, 2048:2056], ds_wf, writes=[B_wf32])
                k.op(DVE, lambda e: e.memset(wF[:], 0.0), writes=[B_wF])
                k.op(DVE, lambda e: e.tensor_copy(out=wF[:, :, 0:8], in_=wf32[:]), reads=[B_wf32], writes=[B_wF])
                xts = [sb(pa, "xtA%d" % i, [128, KC * NT], F32) for i in range(2)]
                B_xt = [Buf("xtA%d" % i) for i in range(2)]
                ds_x = [k.dsem("xA") for _ in range(2)]
                hT = sb(pa, "hTA", [128, KC, NT], BF16); B_h = Buf("hTA")
                stg = [sb(pa, "stgA%d" % i, [128, NT], BF16) for i in range(4)]
                B_stg = [Buf("stgA%d" % i) for i in range(4)]
                ds_stg = [k.dsem("stgA") for _ in range(4)]
                nctx = NormCtx(pa, "A")
                sti = [0]

                def stage_out(bk, bap, dst, eng_i, scale=None):
                    s_ = sti[0] % 4
                    sti[0] += 1
                    if eng_i % 2 == 0:
                        if scale is None:
                            k.op(ACT, lambda e: e.copy(out=stg[s_][:], in_=bap), reads=[bk], writes=[B_stg[s_]])
                        else:
                            k.op(ACT, lambda e: e.activation(out=stg[s_][:], in_=bap, func=AF.Copy, scale=scale),
                                 reads=[bk], writes=[B_stg[s_]])
                    else:
                        if scale is None:
                            k.op(DVE, lambda e: e.tensor_copy(out=stg[s_][:], in_=bap), reads=[bk], writes=[B_stg[s_]])
                        else:
                            k.op(DVE, lambda e: e.tensor_scalar(out=stg[s_][:], in0=bap, scalar1=scale, scalar2=None,
                                                                op0=ALU.mult), reads=[bk], writes=[B_stg[s_]])
                    k.dma(SP, dst, stg[s_][:], ds_stg[s_], reads=[B_stg[s_]])

                hTs = [hT, sb(pa, "hTA2", [128, KC, NT], BF16)]
                B_hs = [B_h, Buf("hTA2")]
                k.dma(SP, xt3(xts[0]), x_view(x_pre, 0), ds_x[0], writes=[B_xt[0]])
                nctx.modulate(xts[0], B_xt[0], hTs[0], B_hs[0], l, 0, 1)
                for t in range(NTL):
                    s = t % 2
                    hT, B_h = hTs[s], B_hs[s]
                    if t + 1 < NTL:
                        k.dma(SP, xt3(xts[1 - s]), x_view(x_pre, t + 1), ds_x[1 - s], writes=[B_xt[1 - s]])
                    tc = slice(t * NT, (t + 1) * NT)
                    ei = 0
                    for m in range(12):
                        if m == 7 and t + 1 < NTL:
                            nctx.modulate(xts[1 - s], B_xt[1 - s], hTs[1 - s], B_hs[1 - s], l, 0, 1)
                        bk, bap = next_bank()
                        k.mm(bk, bap, [(wA[:, kk, m * 128:(m + 1) * 128], hT[:, kk, :]) for kk in range(KC)],
                             reads=[WA.buf(m * 128), B_h])
                        if m < 4:
                            if m % 2 == 0:
                                k.op(ACT, lambda e: e.copy(out=uT[:, m, tc], in_=bap), reads=[bk], writes=[B_u])
                            else:
                                k.op(DVE, lambda e: e.tensor_copy(out=uT[:, m, tc], in_=bap), reads=[bk], writes=[B_u])
                        elif m < 8:
                            stage_out(bk, bap, qT_d[(m - 4) * 128:(m - 3) * 128, tc], ei, scale=0.125); ei += 1
                        else:
                            stage_out(bk, bap, kT_d[(m - 8) * 128:(m - 7) * 128, tc], ei); ei += 1
                    for sub in range(4):
                        bk, bap = next_bank()
                        k.mm(bk, bap, [(hT[:, kk, sub * 128:(sub + 1) * 128], wA[:, kk, 1536:2048]) for kk in range(KC)],
                             reads=[WA.buf(1536), B_h])
                        stage_out(bk, bap, v_d[t * NT + sub * 128:t * NT + (sub + 1) * 128, :], ei); ei += 1
                    bk, bap = next_bank()
                    k.mm(bk, bap[0:32, :], [(wF[:, kk, :], hT[:, kk, :]) for kk in range(KC)], reads=[B_wF, B_h])
                    k.op(DVE, lambda e: e.tensor_copy(out=fT[:, tc], in_=bap[0:8, :]), reads=[bk], writes=[B_f])
                k.barrier()
              with ExitStack() as pa:
                bfS = sb(pa, "bfS", [8, 1], F32); B_bf = Buf("bf")
                onesS = sb(pa, "onesS", [8, S], F32); B_on = Buf("onesS")
                l1 = sb(pa, "l1", [8, S], F32); B_l1 = Buf("l1")
                ncum = sb(pa, "ncum", [8, S], F32); B_nc = Buf("ncum")
                cb = sb(pa, "cb", [8, 3, S], BF16); B_cb = Buf("cb")
                cbn = sb(pa, "cbn", [8, 3, S], BF16); B_cbn = Buf("cbn")
                ds_c = k.dsem("cum")
                k.dma(SP, bfS[:], b_f[l], ds_c, writes=[B_bf])
                k.op(DVE, lambda e: e.tensor_scalar(out=bfS[:], in0=bfS[:], scalar1=-1.0, scalar2=None, op0=ALU.mult),
                     reads=[B_bf], writes=[B_bf])
                k.op(DVE, lambda e: e.memset(onesS[:], 1.0), writes=[B_on])
                k.op(ACT, lambda e: e.activation(out=l1[:], in_=fT[:], func=AF.Exp, scale=-1.0, bias=bfS[:, 0:1]),
                     reads=[B_f, B_bf], writes=[B_l1])
                k.op(ACT, lambda e: e.activation(out=l1[:], in_=l1[:], func=AF.Ln, bias=1.0), reads=[B_l1], writes=[B_l1])
                k.op(DVE, lambda e: e.tensor_tensor_scan(out=ncum[:], data0=onesS[:], data1=l1[:], initial=0.0,
                                                         op0=ALU.mult, op1=ALU.add),
                     reads=[B_on, B_l1], writes=[B_nc])
                for j in range(3):
                    k.op(DVE, lambda e: e.tensor_copy(out=cb[:, j, :], in_=ncum[:]), reads=[B_nc], writes=[B_cb])
                    if j < 2:
                        k.op(DVE, lambda e: e.tensor_tensor(out=ncum[:], in0=ncum[:], in1=cb[:, j, :], op=ALU.subtract),
                             reads=[B_nc, B_cb], writes=[B_nc])
                k.op(DVE, lambda e: e.tensor_scalar(out=cbn[:], in0=cb[:], scalar1=-1.0, scalar2=None, op0=ALU.mult),
                     reads=[B_cb], writes=[B_cbn])
                k.dma(SP, cumk_d.rearrange("h j s -> h (j s)"), cb[:].rearrange("h j s -> h (j s)"), ds_c, reads=[B_cb])
                k.dma(SP, cumq_d.rearrange("h j s -> h (j s)"), cbn[:].rearrange("h j s -> h (j s)"), ds_c, reads=[B_cbn])
                k.barrier()

            with ExitStack() as pb:
                Kw = sb(pb, "Kw", [128, 8, 4, 128], BF16); B_Kw = Buf("Kw")
                WE = sb(pb, "WE", [128, 8, 2, 4, 128], BF16); B_WE = Buf("WE")
                WI = sb(pb, "WI", [128, 16, 8, 2, 32], BF16); B_WI = Buf("WI")
                Sb = sb(pb, "Sb", [128, 16, 2, NCH], BF16); B_Sb = Buf("Sb")
                Ad3 = sb(pb, "Ad3", [128, 3, LV, 16], F32); B_Ad = Buf("Ad")
                Adr, Adi, Adn = Ad3[:, 0, :, :], Ad3[:, 1, :, :], Ad3[:, 2, :, :]
                dsk = sb(pb, "dsk", [128, 4], F32); B_dsk = Buf("dsk")
                bglu = sb(pb, "bglu", [128, 4], F32); B_bglu = Buf("bglu")
                wglu = sb(pb, "wglu", [128, 4, 512], BF16); B_wglu = Buf("wglu"); ds_wglu = k.dsem("wglu")
                ds_p = k.dsem("s5p")
                load_w(wglu, w_glu[l], 4, 0, 512, ds_wglu, B_wglu)
                k.dma(SP, dsk[:], dskT[l], ds_p, writes=[B_dsk])
                k.dma(SP, bglu[:], b_gluT[l], ds_p, writes=[B_bglu])

                k.dma(SP, Kw[:].rearrange("p a b c -> p (a b c)"), Kw_d[l], ds_p, writes=[B_Kw])
                k.dma(SP, WE[:].rearrange("p a b c d -> p (a b c d)"), WE_d[l], ds_p, writes=[B_WE])
                k.dma(SP, WI[:].rearrange("p a b c d -> p (a b c d)"), WI_d[l], ds_p, writes=[B_WI])
                k.dma(SP, Ad3[:].rearrange("p a b c -> p (a b c)"), Ad_d[l], ds_p, writes=[B_Ad])

                with ExitStack() as pc:
                    NSL = 2
                    stt_ = [[sb(pc, "st%d_%d" % (sl, i), [128, NCH], F32) for i in range(4)] for sl in range(NSL)]
                    B_st = [[Buf("st%d_%d" % (sl, i)) for i in range(4)] for sl in range(NSL)]
                    k.op(DVE, lambda e: e.memset(Sb[:], 0.0), writes=[B_Sb])

                    def scan_units():
                        for q in range(16):
                            ct, ql = q // 4, q % 4
                            sl = q % NSL
                            P_ = (stt_[sl][0], stt_[sl][1]); Q_ = (stt_[sl][2], stt_[sl][3])
                            BP = (B_st[sl][0], B_st[sl][1]); BQ = (B_st[sl][2], B_st[sl][3])
                            rows = slice(32 * ql, 32 * ql + 32)
                            for ri in range(2):
                                bk, bap = next_bank(6, 8)
                                k.mm(bk, bap[:, 0:NCH],
                                     [(WE[rows, s_, ri, ct, :], uT[rows, ct, s_:S:8]) for s_ in range(8)],
                                     reads=[B_WE, B_u], tile_position=(32 * ql, 0))
                                k.op(ACT, lambda e: e.copy(out=P_[ri][:], in_=bap[:, 0:NCH]), reads=[bk], writes=[BP[ri]])
                            yield
                            src, dst, Bs, Bd = P_, Q_, BP, BQ
                            for lv in range(LV):
                                d = 1 << lv
                                n = NCH - d
                                ar = Adr[:, lv, q:q + 1]; ai = Adi[:, lv, q:q + 1]; an = Adn[:, lv, q:q + 1]
                                lo_ = d // 2
                                for ri in range(2):
                                    k.op(DVE, lambda e: e.tensor_copy(out=dst[ri][:, lo_:d], in_=src[ri][:, lo_:d]),
                                         reads=[Bs[ri]], writes=[Bd[ri]])
                                k.op(DVE, lambda e: e.scalar_tensor_tensor(out=dst[0][:, d:NCH], in0=src[0][:, 0:n], scalar=ar,
                                                                           in1=src[0][:, d:NCH], op0=ALU.mult, op1=ALU.add),
                                     reads=[Bs[0], B_Ad], writes=[Bd[0]])
                                yield
                                k.op(DVE, lambda e: e.scalar_tensor_tensor(out=dst[0][:, d:NCH], in0=src[1][:, 0:n], scalar=an,
                                                                           in1=dst[0][:, d:NCH], op0=ALU.mult, op1=ALU.add),
                                     reads=[Bs[1], B_Ad, Bd[0]], writes=[Bd[0]])
                                yield
                                k.op(DVE, lambda e: e.scalar_tensor_tensor(out=dst[1][:, d:NCH], in0=src[1][:, 0:n], scalar=ar,
                                                                           in1=src[1][:, d:NCH], op0=ALU.mult, op1=ALU.add),
                                     reads=[Bs[1], B_Ad], writes=[Bd[1]])
                                yield
                                k.op(DVE, lambda e: e.scalar_tensor_tensor(out=dst[1][:, d:NCH], in0=src[0][:, 0:n], scalar=ai,
                                                                           in1=dst[1][:, d:NCH], op0=ALU.mult, op1=ALU.add),
                                     reads=[Bs[0], B_Ad, Bd[1]], writes=[Bd[1]])
                                yield
                                src, dst, Bs, Bd = dst, src, Bd, Bs
                            for ri in range(2):
                                k.op(POOL, lambda e: e.tensor_copy(out=Sb[:, q, ri, 1:NCH], in_=src[ri][:, 0:NCH - 1]),
                                     reads=[Bs[ri]], writes=[B_Sb])
                            yield

                    scan_gen = scan_units()
                    qa = [sb(pc, "qa%d" % i, [128, S], BF16) for i in range(2)]
                    ka = [sb(pc, "ka%d" % i, [128, S], BF16) for i in range(2)]
                    va = [sb(pc, "va%d" % i, [128, KT, 128], BF16) for i in range(2)]
                    B_qa = [Buf("qa%d" % i) for i in range(2)]
                    B_ka = [Buf("ka%d" % i) for i in range(2)]
                    B_va = [Buf("va%d" % i) for i in range(2)]
                    ds_qkv = [k.dsem("qkv") for _ in range(2)]
                    pT = [sb(pc, "pT%d" % i, [128, NT], BF16) for i in range(4)]
                    B_pT = [Buf("pT%d" % i) for i in range(4)]
                    rden = [sb(pc, "rden%d" % i, [128, NT], F32) for i in range(2)]
                    B_rden = [Buf("rden%d" % i) for i in range(2)]
                    yst = [sb(pc, "yst%d" % i, [128, NT], BF16) for i in range(2)]
                    B_yst = [Buf("yst%d" % i) for i in range(2)]
                    ds_yst = [k.dsem("yst") for _ in range(2)]
                    for i in range(2):
                        k.op(POOL, lambda e: e.memset(qa[i][:], 0.0), writes=[B_qa[i]])
                        k.op(POOL, lambda e: e.memset(ka[i][:], 0.0), writes=[B_ka[i]])
                        k.op(POOL, lambda e: e.memset(qa[i][64:70, :], 1.0), writes=[B_qa[i]])
                        k.op(POOL, lambda e: e.memset(ka[i][64:70, :], 1.0), writes=[B_ka[i]])
                    k.op(POOL, lambda e: e.memset(va[0][:, :, 64:128], 1.0), writes=[B_va[0]])
                    k.op(POOL, lambda e: e.memset(va[1][:, :, 0:64], 1.0), writes=[B_va[1]])

                    def load_head(h):
                        s_ = h % 2
                        hr = slice(h * 64, (h + 1) * 64)
                        k.dma(SP, qa[s_][0:64, :], qT_d[hr, :], ds_qkv[s_], writes=[B_qa[s_]])
                        k.dma(SP, qa[s_][67:70, :], cumq_d[h], ds_qkv[s_], writes=[B_qa[s_]])
                        k.dma(SP, ka[s_][0:64, :], kT_d[hr, :], ds_qkv[s_], writes=[B_ka[s_]])
                        k.dma(SP, ka[s_][64:67, :], cumk_d[h], ds_qkv[s_], writes=[B_ka[s_]])
                        vv = v_d.rearrange("(kt p) c -> p kt c", p=128)
                        co = 0 if s_ == 0 else 64
                        for a in range(0, KT, 8):
                            k.dma(SP, va[s_][:, a:a + 8, co:co + 64], vv[:, a:a + 8, hr], ds_qkv[s_], writes=[B_va[s_]])

                    items = []
                    for h in range(8):
                        for j in range(NTL):
                            for i in range(4 * j + 4):
                                items.append((h, j, i))
                    SB_LO, SB_HI = 0, 4
                    obanks = [(banks[4], psum[:, 4, :]), (banks[5], psum[:, 5, :])]
                    pend = []
                    load_head(0)
                    oi = 0
                    for n in range(len(items) + 2):
                        if n < len(items):
                            h, j, i = items[n]
                            s_ = h % 2
                            if j == 0 and i == 2 and h + 1 < 8:
                                load_head(h + 1)
                            r = i - 4 * j
                            c0 = 128 * r if r > 0 else 0
                            bk, bap = next_bank(SB_LO, SB_HI)
                            pi_ = n % 4
                            k.mm(bk, bap[:, c0:NT], [(ka[s_][:, i * 128:(i + 1) * 128], qa[s_][:, j * NT + c0:(j + 1) * NT])],
                                 reads=[B_ka[s_], B_qa[s_]])
                            k.op(ACT, lambda e: e.activation(out=pT[pi_][:, c0:NT], in_=bap[:, c0:NT], func=AF.Exp),
                                 reads=[bk], writes=[B_pT[pi_]])
                            if r >= 0:
                                k.op(POOL, lambda e: e.tensor_tensor(out=pT[pi_][:, c0:c0 + 128], in0=pT[pi_][:, c0:c0 + 128],
                                                                     in1=tri_bf[:], op=ALU.mult),
                                     reads=[B_pT[pi_], B_tri], writes=[B_pT[pi_]])
                            pend.append((h, j, i, c0, pi_))
                            next(scan_gen, None)
                        if n >= 2:
                            h, j, i, c0, pi_ = pend[n - 2]
                            s_ = h % 2
                            last = (i == 4 * j + 3)
                            if i == 0:
                                oi += 1
                            ob, oap = obanks[oi % 2]
                            k.mm(ob, oap[:, c0:NT], [(va[s_][:, i, :], pT[pi_][:, c0:NT])], reads=[B_va[s_], B_pT[pi_]],
                                 start=(i == 0), stop=last)
                            if last:
                                e2 = oi % 2
                                orow = slice(0, 64) if s_ == 0 else slice(64, 128)
                                drow = slice(64, 128) if s_ == 0 else slice(0, 64)
                                k.op(DVE, lambda e: e.reciprocal(out=rden[e2][orow, :], in_=oap[drow, :]), reads=[ob],
                                     writes=[B_rden[e2]])
                                k.op(DVE, lambda e: e.tensor_tensor(out=yst[e2][orow, :], in0=oap[orow, :], in1=rden[e2][orow, :],
                                                                    op=ALU.mult),
                                     reads=[ob, B_rden[e2]], writes=[B_yst[e2]])
                                k.dma(SP, yatt_d[h * 64:(h + 1) * 64, j * NT:(j + 1) * NT], yst[e2][orow, :], ds_yst[e2],
                                      reads=[B_yst[e2]])
                    for _ in scan_gen:
                        pass
                    k.barrier()

                with ExitStack() as pq:
                    zT = sb(pq, "zT", [128, 4, S], BF16); B_z = Buf("zT")
                    isb = [sb(pq, "isb%d" % i, [128, NCH], F32) for i in range(2)]
                    B_isb = [Buf("isb%d" % i) for i in range(2)]
                    y1 = [sb(pq, "y1_%d" % i, [128, NCH], F32) for i in range(2)]
                    B_y1 = [Buf("y1_%d" % i) for i in range(2)]
                    it = 0
                    for t_ in range(8):
                        for ct in range(4):
                            s2 = it % 2
                            it += 1
                            bki, bapi = next_bank()
                            k.mm(bki, bapi[:, 0:NCH],
                                 [(Kw[:, t_ - s_, ct, :], uT[:, ct, s_:S:8]) for s_ in range(t_ + 1)],
                                 reads=[B_Kw, B_u])
                            bke, bape = next_bank()
                            k.mm_multi(bke, [(bape[32 * ql:32 * ql + 32, 0:NCH],
                                              [(WI[:, ct * 4 + ql, t_, 0, :], Sb[:, ct * 4 + ql, 0, :]),
                                               (WI[:, ct * 4 + ql, t_, 1, :], Sb[:, ct * 4 + ql, 1, :])],
                                              (0, 32 * ql)) for ql in range(4)], reads=[B_WI, B_Sb])
                            k.op(ACT, lambda e: e.copy(out=isb[s2][:], in_=bape[:, 0:NCH]), reads=[bke], writes=[B_isb[s2]])
                            k.op(DVE, lambda e: e.scalar_tensor_tensor(out=y1[s2][:], in0=uT[:, ct, t_:S:8],
                                                                       scalar=dsk[:, ct:ct + 1], in1=bapi[:, 0:NCH],
                                                                       op0=ALU.mult, op1=ALU.add),
                                 reads=[B_u, B_dsk, bki], writes=[B_y1[s2]])
                            k.op(POOL, lambda e: e.tensor_tensor(out=y1[s2][:], in0=y1[s2][:], in1=isb[s2][:], op=ALU.add),
                                 reads=[B_y1[s2], B_isb[s2]], writes=[B_y1[s2]])
                            k.op(ACT, lambda e: e.activation(out=zT[:, ct, t_:S:8], in_=y1[s2][:], func=AF.Gelu_apprx_tanh),
                                 reads=[B_y1[s2]], writes=[B_z])
                    sg = [sb(pq, "sg%d" % i, [128, NT], F32) for i in range(2)]
                    B_sg = [Buf("sg%d" % i) for i in range(2)]
                    og = [sb(pq, "og%d" % i, [128, NT], BF16) for i in range(2)]
                    B_og = [Buf("og%d" % i) for i in range(2)]
                    ds_og = [k.dsem("og") for _ in range(2)]
                    it = 0
                    for t in range(NTL):
                        tc = slice(t * NT, (t + 1) * NT)
                        for ct in range(4):
                            s2 = it % 2
                            it += 1
                            bk, bap = next_bank()
                            k.mm(bk, bap, [(wglu[:, kk, ct * 128:(ct + 1) * 128], zT[:, kk, tc]) for kk in range(4)],
                                 reads=[B_wglu, B_z])
                            k.op(ACT, lambda e: e.activation(out=sg[s2][:], in_=bap, func=AF.Sigmoid, bias=bglu[:, ct:ct + 1]),
                                 reads=[bk, B_bglu], writes=[B_sg[s2]])
                            k.op(DVE, lambda e: e.tensor_tensor(out=og[s2][:], in0=sg[s2][:], in1=zT[:, ct, tc], op=ALU.mult),
                                 reads=[B_sg[s2], B_z], writes=[B_og[s2]])
                            k.dma(SP, yssm_d[ct * 128:(ct + 1) * 128, tc], og[s2][:], ds_og[s2], reads=[B_og[s2]])
                    k.barrier()

        with ExitStack() as pd:
            wG = sb(pd, "wG", [128, KC, 2048], BF16)
            wPA = sb(pd, "wPA", [128, 4, D], BF16)
            wPB = sb(pd, "wPB", [128, 4, D], BF16)
            wO = sb(pd, "wO", [128, KC, D], BF16)
            WG = WChunks(wG, w_in[l], KC, 2056, 2048, 512, "wG")
            WPA = WChunks(wPA, w_pa[l], 4, 0, D, 512, "wPA")
            WPB = WChunks(wPB, w_pb[l], 4, 0, D, 512, "wPB")
            WO = WChunks(wO, w_o[l], KC, 0, D, 512, "wO")
            WG.load(0); WG.load(2); WPA.load(0); WPB.load(0)
            WG.load(1); WG.load(3); WPA.load(1); WPB.load(1)
            WO.load(0); WO.load(1)
            xts = [sb(pd, "xtD%d" % i, [128, KC * NT], F32) for i in range(2)]
            B_xt = [Buf("xtD%d" % i) for i in range(2)]
            ds_x = [k.dsem("xD") for _ in range(2)]
            ysa = [sb(pd, "ysa%d" % i, [128, 8, NT], BF16) for i in range(2)]
            B_ysa = [Buf("ysa%d" % i) for i in range(2)]
            hT = sb(pd, "hTD", [128, KC, NT], BF16); B_h = Buf("hTD")
            mg = sb(pd, "mg", [128, KC, NT], BF16); B_mg = Buf("mg")
            yt = sb(pd, "ytD", [128, KC * NT], F32); B_yt = Buf("ytD")
            sga = [sb(pd, "sga%d" % i, [128, NT], F32) for i in range(2)]
            sgb = [sb(pd, "sgb%d" % i, [128, NT], F32) for i in range(2)]
            B_sga = [Buf("sga%d" % i) for i in range(2)]
            B_sgb = [Buf("sgb%d" % i) for i in range(2)]
            nctx = NormCtx(pd, "D")
            nctx2 = NormCtx(pd, "D2")
            hTs = [hT, sb(pd, "hTD2", [128, KC, NT], BF16)]
            B_hs = [B_h, Buf("hTD2")]

            def loadD(t):
                s_ = t % 2
                k.dma(SP, xt3(xts[s_]), x_view(x_pre, t), ds_x[s_], writes=[B_xt[s_]])
                k.dma(SP, ysa[s_][:, 0:4, :], yssm_d.rearrange("(k p) n -> p k n", p=128)[:, :, t * NT:(t + 1) * NT],
                      ds_x[s_], writes=[B_ysa[s_]])
                k.dma(SP, ysa[s_][:, 4:8, :], yatt_d.rearrange("(k p) n -> p k n", p=128)[:, :, t * NT:(t + 1) * NT],
                      ds_x[s_], writes=[B_ysa[s_]])

            def postD(t):
                s_ = t % 2
                nctx2.residual(yt, B_yt, xts[s_], B_xt[s_], l, 2)
                k.dma(SP, x_view(x_mid, t), xt3(xts[s_]), ds_x[s_], reads=[B_xt[s_]])

            def mstepD(t, m):
                s = t % 2
                hT, B_h = hTs[s], B_hs[s]
                s2 = m % 2
                mc = slice(m * 128, (m + 1) * 128)
                bka, bapa = next_bank()
                k.mm(bka, bapa, [(wG[:, kk, mc], hT[:, kk, :]) for kk in range(KC)], reads=[WG.buf(m * 128), B_h])
                k.op(ACT, lambda e: e.activation(out=sga[s2][:], in_=bapa, func=AF.Sigmoid), reads=[bka], writes=[B_sga[s2]])
                bkb, bapb = next_bank()
                k.mm(bkb, bapb, [(wG[:, kk, 1024 + m * 128:1024 + (m + 1) * 128], hT[:, kk, :]) for kk in range(KC)],
                     reads=[WG.buf(1024 + m * 128), B_h])
                k.op(ACT, lambda e: e.activation(out=sgb[s2][:], in_=bapb, func=AF.Sigmoid), reads=[bkb], writes=[B_sgb[s2]])
                bkp, bapp = next_bank()
                k.mm(bkp, bapp, [(wPA[:, kk, mc], ysa[s][:, kk, :]) for kk in range(4)], reads=[WPA.buf(m * 128), B_ysa[s]])
                k.op(DVE, lambda e: e.tensor_tensor(out=sga[s2][:], in0=bapp, in1=sga[s2][:], op=ALU.mult),
                     reads=[bkp, B_sga[s2]], writes=[B_sga[s2]])
                bkq, bapq = next_bank()
                k.mm(bkq, bapq, [(wPB[:, kk, mc], ysa[s][:, 4 + kk, :]) for kk in range(4)], reads=[WPB.buf(m * 128), B_ysa[s]])
                k.op(DVE, lambda e: e.tensor_tensor(out=sgb[s2][:], in0=bapq, in1=sgb[s2][:], op=ALU.mult),
                     reads=[bkq, B_sgb[s2]], writes=[B_sgb[s2]])
                k.op(POOL, lambda e: e.tensor_tensor(out=mg[:, m, :], in0=sga[s2][:], in1=sgb[s2][:], op=ALU.add),
                     reads=[B_sga[s2], B_sgb[s2]], writes=[B_mg])

            def ostepD(t, m):
                mc = slice(m * 128, (m + 1) * 128)
                bk, bap = next_bank()
                k.mm(bk, bap, [(wO[:, kk, mc], mg[:, kk, :]) for kk in range(KC)], reads=[WO.buf(m * 128), B_mg])
                if m % 2 == 0:
                    k.op(ACT, lambda e: e.copy(out=yt[:, m * NT:(m + 1) * NT], in_=bap), reads=[bk], writes=[B_yt])
                else:
                    k.op(DVE, lambda e: e.tensor_copy(out=yt[:, m * NT:(m + 1) * NT], in_=bap), reads=[bk], writes=[B_yt])

            loadD(0)
            nctx.modulate(xts[0], B_xt[0], hTs[0], B_hs[0], l, 0, 1)
            for t in range(NTL):
                for m in range(KC):
                    mstepD(t, m)
                    if m == 1:
                        if t > 0:
                            postD(t - 1)
                        if t + 1 < NTL:
                            loadD(t + 1)
                for m in range(KC):
                    if m == 4 and t + 1 < NTL:
                        s1 = (t + 1) % 2
                        nctx.modulate(xts[s1], B_xt[s1], hTs[s1], B_hs[s1], l, 0, 1)
                    ostepD(t, m)
            postD(NTL - 1)
            k.barrier()

        with ExitStack() as pe1:
            wg = sb(pe1, "wg", [128, KC, DFF], BF16)
            wu = sb(pe1, "wu", [128, KC, DFF], BF16)
            WGt = WChunks(wg, w_g[l], KC, 0, DFF, 512, "wg")
            WUp = WChunks(wu, w_u[l], KC, 0, DFF, 512, "wu")
            for i_ in range(WGt.n):
                WGt.load(i_); WUp.load(i_)
            xts = [sb(pe1, "xtE%d" % i, [128, KC * NT], F32) for i in range(2)]
            B_xt = [Buf("xtE%d" % i) for i in range(2)]
            ds_x = [k.dsem("xE") for _ in range(2)]
            hT = sb(pe1, "hTE", [128, KC, NT], BF16); B_h = Buf("hTE")
            sl_ = [sb(pe1, "sl%d" % i, [128, NT], F32) for i in range(2)]
            B_sl = [Buf("sl%d" % i) for i in range(2)]
            ao = [sb(pe1, "ao%d" % i, [128, NT], BF16) for i in range(4)]
            B_ao = [Buf("ao%d" % i) for i in range(4)]
            ds_ao = [k.dsem("ao") for _ in range(4)]
            nctx = NormCtx(pe1, "E")
            hTs = [hT, sb(pe1, "hTE2", [128, KC, NT], BF16)]
            B_hs = [B_h, Buf("hTE2")]
            k.dma(SP, xt3(xts[0]), x_view(x_mid, 0), ds_x[0], writes=[B_xt[0]])
            nctx.modulate(xts[0], B_xt[0], hTs[0], B_hs[0], l, 3, 4)
            it = 0
            for t in range(NTL):
                s = t % 2
                hT, B_h = hTs[s], B_hs[s]
                if t + 1 < NTL:
                    k.dma(SP, xt3(xts[1 - s]), x_view(x_mid, t + 1), ds_x[1 - s], writes=[B_xt[1 - s]])
                for m in range(FC):
                    if m == 12 and t + 1 < NTL:
                        nctx.modulate(xts[1 - s], B_xt[1 - s], hTs[1 - s], B_hs[1 - s], l, 3, 4)
                    mc = slice(m * 128, (m + 1) * 128)
                    s2 = it % 2
                    s4 = it % 4
                    it += 1
                    bkg, bapg = next_bank()
                    k.mm(bkg, bapg, [(wg[:, kk, mc], hT[:, kk, :]) for kk in range(KC)], reads=[WGt.buf(m * 128), B_h])
                    k.op(ACT, lambda e: e.activation(out=sl_[s2][:], in_=bapg, func=AF.Silu), reads=[bkg], writes=[B_sl[s2]])
                    bku, bapu = next_bank()
                    k.mm(bku, bapu, [(wu[:, kk, mc], hT[:, kk, :]) for kk in range(KC)], reads=[WUp.buf(m * 128), B_h])
                    k.op(DVE, lambda e: e.tensor_tensor(out=ao[s4][:], in0=bapu, in1=sl_[s2][:], op=ALU.mult),
                         reads=[bku, B_sl[s2]], writes=[B_ao[s4]])
                    k.dma(SP, aT_d[mc, t * NT:(t + 1) * NT], ao[s4][:], ds_ao[s4], reads=[B_ao[s4]])
            k.barrier()

        with ExitStack() as pe2:
            wd = sb(pe2, "wd", [128, FC, D], BF16)
            WD = WChunks(wd, w_d[l], FC, 0, D, 256, "wd")
            for i_ in range(WD.n):
                WD.load(i_)
            xts = [sb(pe2, "xtF%d" % i, [128, KC * NT], F32) for i in range(2)]
            B_xt = [Buf("xtF%d" % i) for i in range(2)]
            ds_x = [k.dsem("xF") for _ in range(2)]
            at = [sb(pe2, "at%d" % i, [128, FC, NT], BF16) for i in range(2)]
            B_at = [Buf("at%d" % i) for i in range(2)]
            yt = sb(pe2, "ytF", [128, KC * NT], F32); B_yt = Buf("ytF")
            nctx = NormCtx(pe2, "F")

            ds_at = [k.dsem("at") for _ in range(2)]

            def load_at(t):
                s_ = t % 2
                av = aT_d.rearrange("(k p) n -> p k n", p=128)
                for a in range(0, FC, 11):
                    k.dma(SP, at[s_][:, a:a + 11, :], av[:, a:a + 11, t * NT:(t + 1) * NT], ds_at[s_], writes=[B_at[s_]])

            def load_x(t):
                s_ = t % 2
                k.dma(SP, xt3(xts[s_]), x_view(x_mid, t), ds_x[s_], writes=[B_xt[s_]])

            def postF(t):
                s_ = t % 2
                nctx.residual(yt, B_yt, xts[s_], B_xt[s_], l, 5)
                k.dma(SP, x_view(x_post, t), xt3(xts[s_]), ds_x[s_], reads=[B_xt[s_]])

            load_at(0)
            load_x(0)
            for t in range(NTL):
                s = t % 2
                if t + 1 < NTL:
                    load_at(t + 1)
                for m in range(KC):
                    mc = slice(m * 128, (m + 1) * 128)
                    bk, bap = next_bank()
                    k.mm(bk, bap, [(wd[:, kk, mc], at[s][:, kk, :]) for kk in range(FC)], reads=[WD.buf(m * 128), B_at[s]])
                    if m % 2 == 0:
                        k.op(ACT, lambda e: e.copy(out=yt[:, m * NT:(m + 1) * NT], in_=bap), reads=[bk], writes=[B_yt])
                    else:
                        k.op(DVE, lambda e: e.tensor_copy(out=yt[:, m * NT:(m + 1) * NT], in_=bap), reads=[bk], writes=[B_yt])
                    if m == 0 and t > 0:
                        pass
                if t + 1 < NTL:
                    pass
                postF(t)
                if t + 1 < NTL:
                    load_x(t + 1)
            k.barrier()

    k.final_wait()
    es.close()
    return nc


def prep_shared(inp):
    f = lambda a: np.ascontiguousarray(np.asarray(a, dtype=np.float32))
    sh = {}
    sh["w_ada"] = f(inp["w_ada"])
    sh["b_adaT"] = f(np.asarray(inp["b_ada"]).reshape(L, 48, 128).transpose(0, 2, 1))
    g = np.stack([np.asarray(inp[n]).reshape(L, KC, 128).transpose(0, 2, 1)
                  for n in ("g_pre_mix", "g_post_mix", "g_pre_ffn", "g_post_ffn")], axis=2)
    sh["gT"] = f(g)
    sh["w_in"] = f(inp["w_in"])

    def ep(a):
        a = np.asarray(a)
        rest = a.shape[3:]
        a = a.reshape((L, 16, 2, 64) + rest)
        perm = (0, 2, 3, 1) + tuple(range(4, 4 + len(rest)))
        a = a.transpose(perm)
        return a.reshape((L, 128, 16) + rest)

    lam_re = ep(np.asarray(inp["lam_re"]))
    lam_im = ep(np.asarray(inp["lam_im"]))
    sh["lamT"] = f(np.stack([lam_re, lam_im], axis=2))
    ldt = np.broadcast_to(np.asarray(inp["log_dt"])[:, :, None], (L, 32, 64))
    sh["ldtT"] = f(ep(ldt))
    b_re = ep(np.asarray(inp["b_re"]))
    b_im = ep(np.asarray(inp["b_im"]))
    sh["bT"] = f(np.stack([b_re, b_im], axis=2))
    c_re = ep(np.asarray(inp["c_re"]).transpose(0, 1, 3, 2))
    c_im = ep(np.asarray(inp["c_im"]).transpose(0, 1, 3, 2))
    sh["cTT"] = f(np.stack([c_re, c_im], axis=2))
    sh["dskT"] = f(np.asarray(inp["d_skip"]).reshape(L, 4, 128).transpose(0, 2, 1))
    sh["w_glu"] = f(inp["w_glu"])
    sh["b_gluT"] = f(np.asarray(inp["b_glu"]).reshape(L, 4, 128).transpose(0, 2, 1))
    sh["b_f"] = f(np.asarray(inp["b_f"]).reshape(L, 8, 1))
    sh["w_pa"] = f(inp["w_pa"]); sh["w_pb"] = f(inp["w_pb"]); sh["w_o"] = f(inp["w_o"])
    sh["w_g"] = f(inp["w_ffn_gate"]); sh["w_u"] = f(inp["w_ffn_up"]); sh["w_d"] = f(inp["w_ffn_down"])
    kk = np.arange(128)
    sh["tri"] = f((kk[None, :] >= kk[:, None]).astype(np.float32))
    sh["ident"] = f(np.eye(128, dtype=np.float32))
    sh["bdm"] = f((kk[:, None] // 16 == kk[None, :] // 16).astype(np.float32))
    return sh


_NC_CACHE = {}


def kernel(**inputs):
    x = np.asarray(inputs["x"], dtype=np.float32)
    c = np.asarray(inputs["c"], dtype=np.float32)
    B, S, _ = x.shape
    sh = prep_shared(inputs)
    in_maps = []
    for b in range(B):
        m = dict(sh)
        m["xT"] = np.ascontiguousarray(x[b].T)
        m["cT"] = np.ascontiguousarray(c[b].reshape(KC, 128).T)
        in_maps.append(m)
    if S not in _NC_CACHE:
        _NC_CACHE[S] = build_nc(S)
    nc = _NC_CACHE[S]
    res = run_bass_kernel_spmd(nc, in_maps, core_ids=list(range(B)))
    out = np.stack([np.ascontiguousarray(np.asarray(r["yT"]).T) for r in res.results], axis=0)
    return out.astype(np.float32)
```

```python
import math
from contextlib import ExitStack

import numpy as np
import concourse.bass as bass
import concourse.mybir as mybir
from concourse.bass_utils import run_bass_kernel_spmd

F32 = mybir.dt.float32
BF16 = mybir.dt.bfloat16
I32 = mybir.dt.int32
ALU = mybir.AluOpType
AF = mybir.ActivationFunctionType

D = 1024
KC = 8
L = 2
DFF = 2816
FC = 22
NIN = 4104
NT = 512
EPS = 1e-6
PI = math.pi


class Buf:
    __slots__ = ("name", "w", "r")

    def __init__(self, name):
        self.name = name
        self.w = {}
        self.r = {}


class DSem:
    def __init__(self, sem):
        self.sem = sem
        self.issued = 0


class Eng:
    def __init__(self, name, h, sem):
        self.name = name
        self.h = h
        self.sem = sem
        self.cnt = 0
        self.waited = {}


class K:
    def __init__(self, nc, es):
        self.nc = nc
        self.es = es
        self.nsem = 0
        self.pe = Eng("pe", nc.tensor, self._sem("pe"))
        self.act = Eng("act", nc.scalar, self._sem("act"))
        self.dve = Eng("dve", nc.vector, self._sem("dve"))
        self.pool = Eng("pool", nc.gpsimd, self._sem("pool"))
        self.sp = Eng("sp", nc.sync, None)
        self.engs = [self.pe, self.act, self.dve, self.pool, self.sp]
        self.dsems = []
        self._occ = {}
        self._dcache = {}

    def _sem(self, name):
        self.nsem += 1
        return self.es.enter_context(self.nc.semaphore("s_%s_%d" % (name, self.nsem)))

    def dsem(self, name="d"):
        occ = self._occ.get(name, 0)
        self._occ[name] = occ + 1
        key = (name, occ)
        if key not in self._dcache:
            d = DSem(self._sem(name))
            self.dsems.append(d)
            self._dcache[key] = d
        return self._dcache[key]

    def layer_begin(self):
        self._occ = {}

    def _wait(self, eng, sem, val, is_dma):
        if (not is_dma) and sem is eng.sem and eng is self.pe:
            return
        key = id(sem)
        if eng.waited.get(key, 0) >= val:
            return
        eng.h.wait_ge(sem, val)
        eng.waited[key] = val

    def _deps(self, eng, reads, writes):
        for b in reads:
            for (sem, val, ds) in b.w.values():
                self._wait(eng, sem, ds.issued if ds is not None else val, ds is not None)
        for b in writes:
            for (sem, val, ds) in b.w.values():
                self._wait(eng, sem, ds.issued if ds is not None else val, ds is not None)
            for (sem, val, ds) in b.r.values():
                self._wait(eng, sem, ds.issued if ds is not None else val, ds is not None)

    def _record(self, tok, reads, writes):
        key = id(tok[0])
        for b in reads:
            b.r[key] = tok
        for b in writes:
            b.w = {key: tok}
            b.r = {}

    def op(self, eng, fn, reads=(), writes=()):
        self._deps(eng, reads, writes)
        ins = fn(eng.h)
        eng.cnt += 1
        ins.then_inc(eng.sem, 1)
        self._record((eng.sem, eng.cnt, None), reads, writes)

    def mm(self, out_buf, out_ap, pairs, reads, tile_position=None, start=True, stop=True):
        eng = self.pe
        self._deps(eng, reads, [out_buf])
        n = len(pairs)
        ins = None
        for i, (lhsT, rhs) in enumerate(pairs):
            kw = {}
            if tile_position is not None:
                kw["tile_position"] = tile_position
            ins = eng.h.matmul(out_ap, lhsT=lhsT, rhs=rhs, start=(start and i == 0),
                               stop=(stop and i == n - 1), **kw)
        eng.cnt += 1
        ins.then_inc(eng.sem, 1)
        self._record((eng.sem, eng.cnt, None), reads, [out_buf])

    def mm_multi(self, out_buf, groups, reads):
        eng = self.pe
        self._deps(eng, reads, [out_buf])
        ins = None
        for (out_ap, pairs, tp) in groups:
            n = len(pairs)
            for i, (lhsT, rhs) in enumerate(pairs):
                kw = {}
                if tp is not None:
                    kw["tile_position"] = tp
                ins = eng.h.matmul(out_ap, lhsT=lhsT, rhs=rhs, start=(i == 0), stop=(i == n - 1), **kw)
        eng.cnt += 1
        ins.then_inc(eng.sem, 1)
        self._record((eng.sem, eng.cnt, None), reads, [out_buf])

    def dma(self, eng, out, in_, ds, reads=(), writes=()):
        self._deps(eng, reads, writes)
        eng.h.dma_start(out=out, in_=in_).then_inc(ds.sem, 16)
        ds.issued += 16
        self._record((ds.sem, ds.issued, ds), reads, writes)

    def barrier(self):
        for e in self.engs:
            for o in self.engs:
                if o.sem is not None and o is not e and o.cnt > 0:
                    self._wait(e, o.sem, o.cnt, False)
            for d in self.dsems:
                if d.issued > 0:
                    self._wait(e, d.sem, d.issued, True)

    def final_wait(self):
        self.barrier()


def build_nc(S, debug=False, nl=L):
    NTL = S // NT
    NCH = S // 8
    KT = S // 128
    LV = int(round(math.log2(NCH)))
    assert 2 ** LV == NCH and NCH <= 512

    nc = bass.Bass("TRN2", target_bir_lowering=False)
    es = ExitStack()
    k = K(nc, es)
    PE, ACT, DVE, POOL, SP = k.pe, k.act, k.dve, k.pool, k.sp

    def din(name, shape, dt=F32):
        return nc.dram_tensor(name, list(shape), dt, kind="ExternalInput").ap()

    okind = "ExternalOutput" if debug else "Internal"

    def dscr(name, shape, dt):
        return nc.dram_tensor(name, list(shape), dt, kind=okind).ap()

    xT_in = din("xT", [D, S])
    cT_in = din("cT", [128, KC])
    w_ada = din("w_ada", [L, D, 6 * D])
    b_adaT = din("b_adaT", [L, 128, 48])
    gT = din("gT", [L, 128, 4, KC])
    w_in = din("w_in", [L, D, NIN])
    lamT = din("lamT", [L, 128, 2, 16])
    ldtT = din("ldtT", [L, 128, 16])
    bT = din("bT", [L, 128, 2, 16, 16])
    cTT = din("cTT", [L, 128, 2, 16, 16])
    dskT = din("dskT", [L, 128, 4])
    w_glu = din("w_glu", [L, 512, 512])
    b_gluT = din("b_gluT", [L, 128, 4])
    b_f = din("b_f", [L, 8, 1])
    w_pa = din("w_pa", [L, 512, D])
    w_pb = din("w_pb", [L, 512, D])
    w_o = din("w_o", [L, D, D])
    w_g = din("w_g", [L, D, DFF])
    w_u = din("w_u", [L, D, DFF])
    w_d = din("w_d", [L, DFF, D])
    tri_in = din("tri", [128, 128])
    ident_in = din("ident", [128, 128])
    bdm_in = din("bdm", [128, 128])

    yT_out = nc.dram_tensor("yT", [D, S], F32, kind="ExternalOutput").ap()
    xa_d = dscr("xa_d", [D, S], F32)
    xb_d = dscr("xb_d", [D, S], F32)
    qT_d = dscr("qT_d", [512, S], BF16)
    kT_d = dscr("kT_d", [512, S], BF16)
    v_d = dscr("v_d", [S, 512], BF16)
    cumq_d = dscr("cumq_d", [8, 3, S], BF16)
    cumk_d = dscr("cumk_d", [8, 3, S], BF16)
    yssm_d = dscr("yssm_d", [512, S], BF16)
    yatt_d = dscr("yatt_d", [512, S], BF16)
    aT_d = dscr("aT_d", [DFF, S], BF16)
    Kw_d = dscr("Kw_d", [L, 128, 8 * 4 * 128], BF16)
    WE_d = dscr("WE_d", [L, 128, 8 * 2 * 4 * 128], BF16)
    WI_d = dscr("WI_d", [L, 128, 16 * 8 * 2 * 32], BF16)
    Ad_d = dscr("Ad_d", [L, 128, 3 * LV * 16], F32)

    uid = [0]

    def sb(stack, name, shape, dt):
        uid[0] += 1
        return stack.enter_context(nc.sbuf_tensor("sb%d_%s" % (uid[0], name), list(shape), dt))

    psum = es.enter_context(nc.psum_tensor("psum", [128, 8, 512], F32))
    banks = [Buf("bank%d" % i) for i in range(8)]
    bank_rr = [0]

    def next_bank(lo=0, hi=8):
        n = hi - lo
        i = lo + (bank_rr[0] % n)
        bank_rr[0] += 1
        return banks[i], psum[:, i, :]

    ones_bf = sb(es, "ones_bf", [128, 128], BF16); B_ones = Buf("ones")
    tri_bf = sb(es, "tri_bf", [128, 128], BF16); B_tri = Buf("tri")
    ident_bf = sb(es, "ident_bf", [128, 128], BF16); B_ident = Buf("ident")
    bdm_f = sb(es, "bdm_f", [128, 128], F32); B_bdm = Buf("bdm")
    epsb = sb(es, "epsb", [128, 1], F32); B_eps = Buf("eps")
    vecs = sb(es, "vecs", [128, L, 6, KC], F32); B_vecs = Buf("vecs")
    ds_const = k.dsem("const")
    ds_const_sw = k.dsem("constsw")

    k.op(DVE, lambda e: e.memset(ones_bf[:], 1.0), writes=[B_ones])
    k.op(DVE, lambda e: e.memset(epsb[:], float(D) * EPS), writes=[B_eps])
    k.dma(POOL, tri_bf[:], tri_in, ds_const_sw, writes=[B_tri])
    k.dma(POOL, ident_bf[:], ident_in, ds_const_sw, writes=[B_ident])
    k.dma(SP, bdm_f[:], bdm_in, ds_const, writes=[B_bdm])

    with ExitStack() as ps_:
        cT = sb(ps_, "cT", [128, KC], F32); B_c = Buf("c")
        cond = sb(ps_, "cond", [128, KC], F32); B_cond = Buf("cond")
        modT = sb(ps_, "modT", [128, L, 48], F32); B_mod = Buf("mod")
        badaS = sb(ps_, "badaS", [128, L, 48], F32); B_bada = Buf("bada")
        gS = sb(ps_, "gS", [128, L, 4, KC], F32); B_g = Buf("g")
        tmpv = sb(ps_, "tmpv", [128, KC], F32); B_tmpv = Buf("tmpv")
        wab = [sb(ps_, "wab%d" % i, [128, KC, 768], F32) for i in range(3)]
        B_wab = [Buf("wab%d" % i) for i in range(3)]
        ds_wab = [k.dsem("wab") for _ in range(3)]
        k.dma(SP, cT[:], cT_in, ds_const, writes=[B_c])
        for l in range(L):
            k.dma(SP, badaS[:, l, :], b_adaT[l], ds_const, writes=[B_bada])
            k.dma(SP, gS[:, l, :, :], gT[l], ds_const, writes=[B_g])
        k.op(ACT, lambda e: e.activation(out=cond[:], in_=cT[:], func=AF.Silu), reads=[B_c], writes=[B_cond])

        Kw = sb(ps_, "KwP", [128, 8, 4, 128], BF16); B_Kw = Buf("KwP")
        WE = sb(ps_, "WEP", [128, 8, 2, 4, 128], BF16); B_WE = Buf("WEP")
        WI = sb(ps_, "WIP", [128, 16, 8, 2, 32], BF16); B_WI = Buf("WIP")
        Ad3 = sb(ps_, "Ad3", [128, 3, LV, 16], F32); B_Ad = Buf("AdP")
        Adr, Adi, Adn = Ad3[:, 0, :, :], Ad3[:, 1, :, :], Ad3[:, 2, :, :]
        ds_p = k.dsem("s5p")
        ds_po = k.dsem("s5po")

        def prep_layer(l):
            lam = sb(ps_, "lam", [128, 2, 16], F32); B_lam = Buf("lam")
            ldt = sb(ps_, "ldt", [128, 16], F32); B_ldt = Buf("ldt")
            bS = sb(ps_, "bS", [128, 2, 16, 16], F32); B_bS = Buf("bS")
            cS = sb(ps_, "cS", [128, 2, 16, 16], F32); B_cS = Buf("cS")
            k.dma(SP, lam[:], lamT[l], ds_p, writes=[B_lam])
            k.dma(SP, ldt[:], ldtT[l], ds_p, writes=[B_ldt])
            k.dma(SP, bS[:], bT[l], ds_p, writes=[B_bS])
            k.dma(SP, cS[:], cTT[l], ds_p, writes=[B_cS])
            NS = 16
            sc = sb(ps_, "sc", [128, NS, 16], F32)
            sci = sb(ps_, "sci", [128, 16], I32)
            B_sc = Buf("sc")

            def V(fn, extra_r=(), extra_w=()):
                k.op(DVE, fn, reads=[B_sc] + list(extra_r), writes=[B_sc] + list(extra_w))

            def A_(fn, extra_r=()):
                k.op(ACT, fn, reads=[B_sc] + list(extra_r), writes=[B_sc])

            DT, LR, AR, TH, MAG, T1, T2, SN, CS, LBR, LBI, FR, FI, T3, T4, T5 = [sc[:, i, :] for i in range(NS)]
            LAMRE, LAMIM = lam[:, 0, :], lam[:, 1, :]
            A_(lambda e: e.activation(out=DT, in_=ldt[:], func=AF.Exp), [B_ldt])
            V(lambda e: e.tensor_scalar(out=LR, in0=LAMRE, scalar1=-1e-4, scalar2=None, op0=ALU.min), [B_lam])
            V(lambda e: e.tensor_tensor(out=AR, in0=LR, in1=DT, op=ALU.mult))
            V(lambda e: e.tensor_tensor(out=TH, in0=LAMIM, in1=DT, op=ALU.mult), [B_lam])
            A_(lambda e: e.activation(out=MAG, in_=AR, func=AF.Exp))

            def range_reduce(dst, shift):
                V(lambda e: e.tensor_scalar(out=T1, in0=TH, scalar1=shift, scalar2=1.0 / (2 * PI), op0=ALU.add,
                                            op1=ALU.mult))
                V(lambda e: e.tensor_copy(out=sci[:], in_=T1))
                V(lambda e: e.tensor_copy(out=T1, in_=sci[:]))
                V(lambda e: e.tensor_scalar(out=T2, in0=TH, scalar1=shift, scalar2=None, op0=ALU.add))
                V(lambda e: e.scalar_tensor_tensor(out=T2, in0=T1, scalar=-2 * PI, in1=T2, op0=ALU.mult, op1=ALU.add))
                V(lambda e: e.tensor_scalar(out=T1, in0=T2, scalar1=PI, scalar2=-2 * PI, op0=ALU.is_gt, op1=ALU.mult))
                V(lambda e: e.tensor_tensor(out=T2, in0=T2, in1=T1, op=ALU.add))
                V(lambda e: e.tensor_scalar(out=T1, in0=T2, scalar1=-PI, scalar2=2 * PI, op0=ALU.is_lt, op1=ALU.mult))
                V(lambda e: e.tensor_tensor(out=T2, in0=T2, in1=T1, op=ALU.add))
                V(lambda e: e.tensor_scalar(out=dst, in0=T2, scalar1=-3.141592, scalar2=3.141592, op0=ALU.max,
                                            op1=ALU.min))

            yield
            range_reduce(T3, 0.0)
            yield
            A_(lambda e: e.activation(out=SN, in_=T3, func=AF.Sin))
            range_reduce(T3, PI / 2)
            A_(lambda e: e.activation(out=CS, in_=T3, func=AF.Sin))
            V(lambda e: e.tensor_tensor(out=LBR, in0=MAG, in1=CS, op=ALU.mult))
            V(lambda e: e.tensor_tensor(out=LBI, in0=MAG, in1=SN, op=ALU.mult))
            V(lambda e: e.tensor_tensor(out=T1, in0=LR, in1=LR, op=ALU.mult))
            V(lambda e: e.tensor_tensor(out=T2, in0=LAMIM, in1=LAMIM, op=ALU.mult), [B_lam])
            V(lambda e: e.tensor_tensor(out=T1, in0=T1, in1=T2, op=ALU.add))
            V(lambda e: e.reciprocal(out=T1, in_=T1))
            V(lambda e: e.tensor_scalar(out=T2, in0=LBR, scalar1=-1.0, scalar2=None, op0=ALU.add))
            V(lambda e: e.tensor_tensor(out=T3, in0=T2, in1=LR, op=ALU.mult))
            V(lambda e: e.tensor_tensor(out=T4, in0=LBI, in1=LAMIM, op=ALU.mult), [B_lam])
            V(lambda e: e.tensor_tensor(out=T3, in0=T3, in1=T4, op=ALU.add))
            V(lambda e: e.tensor_tensor(out=FR, in0=T3, in1=T1, op=ALU.mult))
            V(lambda e: e.tensor_tensor(out=T3, in0=LBI, in1=LR, op=ALU.mult))
            V(lambda e: e.tensor_tensor(out=T4, in0=T2, in1=LAMIM, op=ALU.mult), [B_lam])
            V(lambda e: e.tensor_tensor(out=T3, in0=T3, in1=T4, op=ALU.subtract))
            V(lambda e: e.tensor_tensor(out=FI, in0=T3, in1=T1, op=ALU.mult))
            Pr = sb(ps_, "Pr", [128, 9, 16], F32)
            Pi_ = sb(ps_, "Pi", [128, 9, 16], F32)
            V(lambda e: e.memset(Pr[:, 0, :], 1.0))
            V(lambda e: e.memset(Pi_[:, 0, :], 0.0))

            def cmul(o_r, o_i, a_r, a_i, b_r, b_i):
                V(lambda e: e.tensor_tensor(out=T4, in0=a_r, in1=b_r, op=ALU.mult))
                V(lambda e: e.tensor_tensor(out=T5, in0=a_i, in1=b_i, op=ALU.mult))
                V(lambda e: e.tensor_tensor(out=T3, in0=a_r, in1=b_i, op=ALU.mult))
                V(lambda e: e.tensor_tensor(out=T1, in0=a_i, in1=b_r, op=ALU.mult))
                V(lambda e: e.tensor_tensor(out=o_r, in0=T4, in1=T5, op=ALU.subtract))
                V(lambda e: e.tensor_tensor(out=o_i, in0=T3, in1=T1, op=ALU.add))

            for tau in range(8):
                cmul(Pr[:, tau + 1, :], Pi_[:, tau + 1, :], Pr[:, tau, :], Pi_[:, tau, :], LBR, LBI)
                yield
            V(lambda e: e.tensor_copy(out=Adr[:, 0, :], in_=Pr[:, 8, :]), extra_w=[B_Ad])
            V(lambda e: e.tensor_copy(out=Adi[:, 0, :], in_=Pi_[:, 8, :]), extra_w=[B_Ad])
            for lv in range(LV - 1):
                cmul(Adr[:, lv + 1, :], Adi[:, lv + 1, :], Adr[:, lv, :], Adi[:, lv, :], Adr[:, lv, :], Adi[:, lv, :])
                yield
            V(lambda e: e.tensor_scalar(out=Adn[:], in0=Adi[:], scalar1=-1.0, scalar2=None, op0=ALU.mult),
              extra_w=[B_Ad])
            Bbr = sb(ps_, "Bbr", [128, 16, 16], F32)
            Bbi = sb(ps_, "Bbi", [128, 16, 16], F32)
            W1 = sb(ps_, "W1", [128, 16, 16], F32)
            W2 = sb(ps_, "W2", [128, 16, 16], F32)

            def bc(ap2):
                return ap2.unsqueeze(2).broadcast_to([ap2.shape[0], 16, 16])

            V(lambda e: e.tensor_tensor(out=W1[:], in0=bS[:, 0, :, :], in1=bc(FR), op=ALU.mult), [B_bS])
            V(lambda e: e.tensor_tensor(out=W2[:], in0=bS[:, 1, :, :], in1=bc(FI), op=ALU.mult), [B_bS])
            V(lambda e: e.tensor_tensor(out=Bbr[:], in0=W1[:], in1=W2[:], op=ALU.subtract))
            V(lambda e: e.tensor_tensor(out=W1[:], in0=bS[:, 1, :, :], in1=bc(FR), op=ALU.mult), [B_bS])
            V(lambda e: e.tensor_tensor(out=W2[:], in0=bS[:, 0, :, :], in1=bc(FI), op=ALU.mult), [B_bS])
            V(lambda e: e.tensor_tensor(out=Bbi[:], in0=W1[:], in1=W2[:], op=ALU.add))
            Xp = sb(ps_, "Xp", [128, 8, 2, 512], BF16)
            Yp = sb(ps_, "Yp", [128, 2, 512], BF16)
            V(lambda e: e.memset(Xp[:], 0.0))
            V(lambda e: e.memset(Yp[:], 0.0))
            V(lambda e: e.memset(WI[:], 0.0), extra_w=[B_WI])

            def slot(ap_cols, e_):
                return ap_cols.rearrange("p (q e h) -> p q e h", q=16, e=2)[:, :, e_, :]

            for tau in range(8):
                for e_ in range(2):
                    hs = slice(e_ * 64, (e_ + 1) * 64)
                    pr = bc(Pr[hs, tau, :]); pi = bc(Pi_[hs, tau, :])
                    V(lambda e: e.tensor_tensor(out=W1[hs], in0=Bbr[hs], in1=pr, op=ALU.mult))
                    V(lambda e: e.tensor_tensor(out=W2[hs], in0=Bbi[hs], in1=pi, op=ALU.mult))
                    V(lambda e: e.tensor_tensor(out=slot(Xp[hs, tau, 0, :], e_), in0=W1[hs], in1=W2[hs], op=ALU.subtract))
                    V(lambda e: e.tensor_tensor(out=W1[hs], in0=Bbr[hs], in1=pi, op=ALU.mult))
                    V(lambda e: e.tensor_tensor(out=W2[hs], in0=Bbi[hs], in1=pr, op=ALU.mult))
                    V(lambda e: e.tensor_tensor(out=slot(Xp[hs, tau, 1, :], e_), in0=W1[hs], in1=W2[hs], op=ALU.add))
                    yield
            for e_ in range(2):
                hs = slice(e_ * 64, (e_ + 1) * 64)
                V(lambda e: e.tensor_copy(out=slot(Yp[hs, 0, :], e_), in_=cS[hs, 0, :, :]), [B_cS])
                V(lambda e: e.tensor_scalar(out=slot(Yp[hs, 1, :], e_), in0=cS[hs, 1, :, :], scalar1=-1.0,
                                            scalar2=None, op0=ALU.mult), [B_cS])
                for t_ in range(8):
                    pr = bc(Pr[hs, t_ + 1, :]); pi = bc(Pi_[hs, t_ + 1, :])
                    V(lambda e: e.tensor_tensor(out=W1[hs], in0=cS[hs, 0, :, :], in1=pr, op=ALU.mult), [B_cS])
                    V(lambda e: e.tensor_tensor(out=W2[hs], in0=cS[hs, 1, :, :], in1=pi, op=ALU.mult), [B_cS])
                    V(lambda e: e.tensor_tensor(out=WI[hs, :, t_, 0, e_ * 16:(e_ + 1) * 16], in0=W1[hs], in1=W2[hs],
                                                op=ALU.subtract), extra_w=[B_WI])
                    V(lambda e: e.tensor_tensor(out=W1[hs], in0=cS[hs, 0, :, :], in1=pi, op=ALU.mult), [B_cS])
                    V(lambda e: e.tensor_tensor(out=W2[hs], in0=cS[hs, 1, :, :], in1=pr, op=ALU.mult), [B_cS])
                    V(lambda e: e.tensor_tensor(out=W1[hs], in0=W1[hs], in1=W2[hs], op=ALU.add))
                    V(lambda e: e.tensor_scalar(out=WI[hs, :, t_, 1, e_ * 16:(e_ + 1) * 16], in0=W1[hs], scalar1=-1.0,
                                                scalar2=None, op0=ALU.mult), extra_w=[B_WI])
                    yield
            k.dma(ACT, WI_d[l], WI[:].rearrange("p a b c d -> p (a b c d)"), ds_po, reads=[B_WI])
            k.dma(ACT, Ad_d[l], Ad3[:].rearrange("p a b c -> p (a b c)"), ds_po, reads=[B_Ad])
            yield "PE"
            for tau in range(8):
                bk, bap = next_bank(0, 6)
                k.mm_multi(bk, [(bap[:, ct * 128:(ct + 1) * 128],
                                 [(Xp[:, tau, 0, ct * 128:(ct + 1) * 128], Yp[:, 0, ct * 128:(ct + 1) * 128]),
                                  (Xp[:, tau, 1, ct * 128:(ct + 1) * 128], Yp[:, 1, ct * 128:(ct + 1) * 128])], None)
                                for ct in range(4)], reads=[B_sc])
                for ct in range(4):
                    k.op(DVE, lambda e: e.tensor_tensor(out=Kw[:, tau, ct, :], in0=bap[:, ct * 128:(ct + 1) * 128],
                                                        in1=bdm_f[:], op=ALU.mult),
                         reads=[bk, B_bdm], writes=[B_Kw])
            for s_ in range(8):
                for ri in range(2):
                    bk, bap = next_bank(0, 6)
                    k.mm_multi(bk, [(bap[:, ct * 128:(ct + 1) * 128],
                                     [(Xp[:, 7 - s_, ri, ct * 128:(ct + 1) * 128], ident_bf[:])], None)
                                    for ct in range(4)], reads=[B_sc, B_ident])
                    k.op(ACT, lambda e: e.copy(out=WE[:, s_, ri, :, :].rearrange("p c n -> p (c n)"), in_=bap),
                         reads=[bk], writes=[B_WE])
            yield
            k.dma(ACT, Kw_d[l], Kw[:].rearrange("p a b c -> p (a b c)"), ds_po, reads=[B_Kw])
            k.dma(ACT, WE_d[l], WE[:].rearrange("p a b c d -> p (a b c d)"), ds_po, reads=[B_WE])
            yield

        prep_gens = [prep_layer(l_) for l_ in range(nl)]
        prep_parked = []

        def prep_pull():
            while prep_gens:
                u = next(prep_gens[0], "END")
                if u == "PE":
                    prep_parked.append(prep_gens.pop(0))
                    return
                if u == "END":
                    prep_gens.pop(0)
                    continue
                return

        it = 0
        for l in range(L):
            bk, bap = banks[6 + l % 2], psum[:, 6 + l % 2, :]
            wv = w_ada[l].rearrange("(k p) n -> p k n", p=128)
            for jc in range(8):
                s = it % 3
                it += 1
                k.dma(SP, wab[s][:], wv[:, :, jc * 768:(jc + 1) * 768], ds_wab[s], writes=[B_wab[s]])
                for jj in range(6):
                    j = jc * 6 + jj
                    k.mm(bk, bap[:, j:j + 1],
                         [(wab[s][:, kk, jj * 128:(jj + 1) * 128], cond[:, kk:kk + 1]) for kk in range(KC)],
                         reads=[B_wab[s], B_cond])
                    prep_pull()
            k.op(DVE, lambda e: e.tensor_tensor(out=modT[:, l, :], in0=bap[:, 0:48], in1=badaS[:, l, :], op=ALU.add),
                 reads=[bk, B_bada], writes=[B_mod])
            for (o, isc, ig) in ((0, 1, 0), (3, 4, 2)):
                k.op(DVE, lambda e: e.tensor_scalar(out=tmpv[:], in0=modT[:, l, isc * 8:(isc + 1) * 8], scalar1=1.0,
                                                    scalar2=32.0, op0=ALU.add, op1=ALU.mult),
                     reads=[B_mod], writes=[B_tmpv])
                k.op(DVE, lambda e: e.tensor_tensor(out=vecs[:, l, o, :], in0=tmpv[:], in1=gS[:, l, ig, :], op=ALU.mult),
                     reads=[B_tmpv, B_g], writes=[B_vecs])
            for (o, ish) in ((1, 0), (4, 3)):
                k.op(DVE, lambda e: e.tensor_copy(out=vecs[:, l, o, :], in_=modT[:, l, ish * 8:(ish + 1) * 8]),
                     reads=[B_mod], writes=[B_vecs])
            for (o, iga, ig) in ((2, 2, 1), (5, 5, 3)):
                k.op(DVE, lambda e: e.tensor_scalar(out=tmpv[:], in0=modT[:, l, iga * 8:(iga + 1) * 8], scalar1=32.0,
                                                    scalar2=None, op0=ALU.mult),
                     reads=[B_mod], writes=[B_tmpv])
                k.op(DVE, lambda e: e.tensor_tensor(out=vecs[:, l, o, :], in0=tmpv[:], in1=gS[:, l, ig, :], op=ALU.mult),
                     reads=[B_tmpv, B_g], writes=[B_vecs])
        while prep_gens:
            prep_pull()
        for g_ in prep_parked:
            for _ in g_:
                pass
        k.barrier()

    def load_w(dst, src2d, kc, c0, ncols, dsm, buf, dcol0=0):
        v = src2d.rearrange("(k p) n -> p k n", p=128)
        step = 1024
        for a in range(0, ncols, step):
            n = min(step, ncols - a)
            k.dma(POOL, dst[:, 0:kc, dcol0 + a:dcol0 + a + n], v[:, :, c0 + a:c0 + a + n], dsm, writes=[buf])

    class WChunks:
        def __init__(self, tile, src2d, kc, c0, ncols, chunk, name):
            self.tile, self.src2d, self.kc, self.c0, self.chunk, self.name = tile, src2d, kc, c0, chunk, name
            self.n = (ncols + chunk - 1) // chunk
            self.ncols = ncols
            self.bufs = [Buf("%s_c%d" % (name, i)) for i in range(self.n)]
            self.ds = [k.dsem("%s_c" % name) for _ in range(self.n)]

        def load(self, i):
            a = i * self.chunk
            n = min(self.chunk, self.ncols - a)
            load_w(self.tile, self.src2d, self.kc, self.c0 + a, n, self.ds[i], self.bufs[i], dcol0=a)

        def buf(self, col):
            return self.bufs[col // self.chunk]

    class NormCtx:
        def __init__(self, stack, tag):
            self.sq = sb(stack, "sq" + tag, [128, KC * NT], BF16); self.B_sq = Buf("sq")
            self.rs = sb(stack, "rs" + tag, [128, NT], F32); self.B_rs = Buf("rs")
            self.tmp = [sb(stack, "ntmp%d%s" % (i, tag), [128, NT], F32) for i in range(2)]
            self.B_tmp = [Buf("ntmp%d" % i) for i in range(2)]
            self.i = 0

        def rstd(self, src, B_src):
            k.op(ACT, lambda e: e.activation(out=self.sq[:], in_=src, func=AF.Square), reads=[B_src], writes=[self.B_sq])
            bk, bap = next_bank()
            k.mm(bk, bap, [(ones_bf[:], self.sq[:, kk * NT:(kk + 1) * NT]) for kk in range(KC)],
                 reads=[B_ones, self.B_sq])
            k.op(ACT, lambda e: e.activation(out=self.rs[:], in_=bap, func=AF.Sqrt, bias=epsb[:, 0:1], scale=1.0),
                 reads=[bk, B_eps], writes=[self.B_rs])
            k.op(DVE, lambda e: e.reciprocal(out=self.rs[:], in_=self.rs[:]), reads=[self.B_rs], writes=[self.B_rs])

        def modulate(self, xt, B_x, hT, B_h, l, ia, ib):
            self.rstd(xt[:], B_x)
            for kk in range(KC):
                t = self.i % 2
                self.i += 1
                tm, Bt = self.tmp[t], self.B_tmp[t]
                k.op(DVE, lambda e: e.tensor_tensor(out=tm[:], in0=xt[:, kk * NT:(kk + 1) * NT], in1=self.rs[:], op=ALU.mult),
                     reads=[B_x, self.B_rs], writes=[Bt])
                k.op(ACT, lambda e: e.activation(out=hT[:, kk, :], in_=tm[:], func=AF.Identity,
                                                 scale=vecs[:, l, ia, kk:kk + 1], bias=vecs[:, l, ib, kk:kk + 1]),
                     reads=[Bt, B_vecs], writes=[B_h])

        def residual(self, yt, B_y, xt, B_x, l, ig):
            self.rstd(yt[:], B_y)
            for kk in range(KC):
                t = self.i % 2
                self.i += 1
                tm, Bt = self.tmp[t], self.B_tmp[t]
                k.op(DVE, lambda e: e.scalar_tensor_tensor(out=tm[:], in0=yt[:, kk * NT:(kk + 1) * NT],
                                                           scalar=vecs[:, l, ig, kk:kk + 1], in1=self.rs[:],
                                                           op0=ALU.mult, op1=ALU.mult),
                     reads=[B_y, self.B_rs, B_vecs], writes=[Bt])
                k.op(POOL, lambda e: e.tensor_tensor(out=xt[:, kk * NT:(kk + 1) * NT], in0=xt[:, kk * NT:(kk + 1) * NT],
                                                     in1=tm[:], op=ALU.add),
                     reads=[Bt, B_x], writes=[B_x])

    def x_view(xd, t):
        return xd.rearrange("(k p) n -> p k n", p=128)[:, :, t * NT:(t + 1) * NT]

    def xt3(xt):
        return xt[:].rearrange("p (k n) -> p k n", k=KC)

    for l in range(nl):
        k.layer_begin()
        x_pre = xT_in if l == 0 else xb_d
        x_mid = xa_d
        x_post = yT_out if l == nl - 1 else xb_d

        with ExitStack() as pm:
            uT = sb(pm, "uT", [128, 4, S], BF16); B_u = Buf("uT")
            with ExitStack() as pa0:
              fT = sb(pa0, "fT", [8, S], F32); B_f = Buf("fT")
              with ExitStack() as pa:
                wA = sb(pa, "wA", [128, KC, 2048], BF16)
                WA = WChunks(wA, w_in[l], KC, 0, 2048, 512, "wA")
                for i_ in range(WA.n):
                    WA.load(i_)
                wF = sb(pa, "wF", [128, KC, 32], BF16); B_wF = Buf("wF")
                wf32 = sb(pa, "wf32", [128, KC, 8], F32); B_wf32 = Buf("wf32"); ds_wf = k.dsem("wf")
                k.dma(SP, wf32[:], w_in[l].rearrange("(k p) n -> p k n", p=128)[:, :, 2048:2056], ds_wf, writes=[B_wf32])
                k.op(DVE, lambda e: e.memset(wF[:], 0.0), writes=[B_wF])
                k.op(DVE, lambda e: e.tensor_copy(out=wF[:, :, 0:8], in_=wf32[:]), reads=[B_wf32], writes=[B_wF])
                xts = [sb(pa, "xtA%d" % i, [128, KC * NT], F32) for i in range(2)]
                B_xt = [Buf("xtA%d" % i) for i in range(2)]
                ds_x = [k.dsem("xA") for _ in range(2)]
                hT = sb(pa, "hTA", [128, KC, NT], BF16); B_h = Buf("hTA")
                stg = [sb(pa, "stgA%d" % i, [128, NT], BF16) for i in range(4)]
                B_stg = [Buf("stgA%d" % i) for i in range(4)]
                ds_stg = [k.dsem("stgA") for _ in range(4)]
                nctx = NormCtx(pa, "A")
                sti = [0]

                def stage_out(bk, bap, dst, eng_i, scale=None):
                    s_ = sti[0] % 4
                    sti[0] += 1
                    if eng_i % 2 == 0:
                        if scale is None:
                            k.op(ACT, lambda e: e.copy(out=stg[s_][:], in_=bap), reads=[bk], writes=[B_stg[s_]])
                        else:
                            k.op(ACT, lambda e: e.activation(out=stg[s_][:], in_=bap, func=AF.Copy, scale=scale),
                                 reads=[bk], writes=[B_stg[s_]])
                    else:
                        if scale is None:
                            k.op(DVE, lambda e: e.tensor_copy(out=stg[s_][:], in_=bap), reads=[bk], writes=[B_stg[s_]])
                        else:
                            k.op(DVE, lambda e: e.tensor_scalar(out=stg[s_][:], in0=bap, scalar1=scale, scalar2=None,
                                                                op0=ALU.mult), reads=[bk], writes=[B_stg[s_]])
                    k.dma(SP, dst, stg[s_][:], ds_stg[s_], reads=[B_stg[s_]])

                hTs = [hT, sb(pa, "hTA2", [128, KC, NT], BF16)]
                B_hs = [B_h, Buf("hTA2")]
                k.dma(SP, xt3(xts[0]), x_view(x_pre, 0), ds_x[0], writes=[B_xt[0]])
                nctx.modulate(xts[0], B_xt[0], hTs[0], B_hs[0], l, 0, 1)
                for t in range(NTL):
                    s = t % 2
                    hT, B_h = hTs[s], B_hs[s]
                    if t + 1 < NTL:
                        k.dma(SP, xt3(xts[1 - s]), x_view(x_pre, t + 1), ds_x[1 - s], writes=[B_xt[1 - s]])
                    tc = slice(t * NT, (t + 1) * NT)
                    ei = 0
                    for m in range(12):
                        if m == 7 and t + 1 < NTL:
                            nctx.modulate(xts[1 - s], B_xt[1 - s], hTs[1 - s], B_hs[1 - s], l, 0, 1)
                        bk, bap = next_bank()
                        k.mm(bk, bap, [(wA[:, kk, m * 128:(m + 1) * 128], hT[:, kk, :]) for kk in range(KC)],
                             reads=[WA.buf(m * 128), B_h])
                        if m < 4:
                            if m % 2 == 0:
                                k.op(ACT, lambda e: e.copy(out=uT[:, m, tc], in_=bap), reads=[bk], writes=[B_u])
                            else:
                                k.op(DVE, lambda e: e.tensor_copy(out=uT[:, m, tc], in_=bap), reads=[bk], writes=[B_u])
                        elif m < 8:
                            stage_out(bk, bap, qT_d[(m - 4) * 128:(m - 3) * 128, tc], ei, scale=0.125); ei += 1
                        else:
                            stage_out(bk, bap, kT_d[(m - 8) * 128:(m - 7) * 128, tc], ei); ei += 1
                    for sub in range(4):
                        bk, bap = next_bank()
                        k.mm(bk, bap, [(hT[:, kk, sub * 128:(sub + 1) * 128], wA[:, kk, 1536:2048]) for kk in range(KC)],
                             reads=[WA.buf(1536), B_h])
                        stage_out(bk, bap, v_d[t * NT + sub * 128:t * NT + (sub + 1) * 128, :], ei); ei += 1
                    bk, bap = next_bank()
                    k.mm(bk, bap[0:32, :], [(wF[:, kk, :], hT[:, kk, :]) for kk in range(KC)], reads=[B_wF, B_h])
                    k.op(DVE, lambda e: e.tensor_copy(out=fT[:, tc], in_=bap[0:8, :]), reads=[bk], writes=[B_f])
                k.barrier()
              with ExitStack() as pa:
                bfS = sb(pa, "bfS", [8, 1], F32); B_bf = Buf("bf")
                onesS = sb(pa, "onesS", [8, S], F32); B_on = Buf("onesS")
                l1 = sb(pa, "l1", [8, S], F32); B_l1 = Buf("l1")
                ncum = sb(pa, "ncum", [8, S], F32); B_nc = Buf("ncum")
                cb = sb(pa, "cb", [8, 3, S], BF16); B_cb = Buf("cb")
                cbn = sb(pa, "cbn", [8, 3, S], BF16); B_cbn = Buf("cbn")
                ds_c = k.dsem("cum")
                k.dma(SP, bfS[:], b_f[l], ds_c, writes=[B_bf])
                k.op(DVE, lambda e: e.tensor_scalar(out=bfS[:], in0=bfS[:], scalar1=-1.0, scalar2=None, op0=ALU.mult),
                     reads=[B_bf], writes=[B_bf])
                k.op(DVE, lambda e: e.memset(onesS[:], 1.0), writes=[B_on])
                k.op(ACT, lambda e: e.activation(out=l1[:], in_=fT[:], func=AF.Exp, scale=-1.0, bias=bfS[:, 0:1]),
                     reads=[B_f, B_bf], writes=[B_l1])
                k.op(ACT, lambda e: e.activation(out=l1[:], in_=l1[:], func=AF.Ln, bias=1.0), reads=[B_l1], writes=[B_l1])
                k.op(DVE, lambda e: e.tensor_tensor_scan(out=ncum[:], data0=onesS[:], data1=l1[:], initial=0.0,
                                                         op0=ALU.mult, op1=ALU.add),
                     reads=[B_on, B_l1], writes=[B_nc])
                for j in range(3):
                    k.op(DVE, lambda e: e.tensor_copy(out=cb[:, j, :], in_=ncum[:]), reads=[B_nc], writes=[B_cb])
                    if j < 2:
                        k.op(DVE, lambda e: e.tensor_tensor(out=ncum[:], in0=ncum[:], in1=cb[:, j, :], op=ALU.subtract),
                             reads=[B_nc, B_cb], writes=[B_nc])
                k.op(DVE, lambda e: e.tensor_scalar(out=cbn[:], in0=cb[:], scalar1=-1.0, scalar2=None, op0=ALU.mult),
                     reads=[B_cb], writes=[B_cbn])
                k.dma(SP, cumk_d.rearrange("h j s -> h (j s)"), cb[:].rearrange("h j s -> h (j s)"), ds_c, reads=[B_cb])
                k.dma(SP, cumq_d.rearrange("h j s -> h (j s)"), cbn[:].rearrange("h j s -> h (j s)"), ds_c, reads=[B_cbn])
                k.barrier()

            with ExitStack() as pb:
                Kw = sb(pb, "Kw", [128, 8, 4, 128], BF16); B_Kw = Buf("Kw")
                WE = sb(pb, "WE", [128, 8, 2, 4, 128], BF16); B_WE = Buf("WE")
                WI = sb(pb, "WI", [128, 16, 8, 2, 32], BF16); B_WI = Buf("WI")
                Sb = sb(pb, "Sb", [128, 16, 2, NCH], BF16); B_Sb = Buf("Sb")
                Ad3 = sb(pb, "Ad3", [128, 3, LV, 16], F32); B_Ad = Buf("Ad")
                Adr, Adi, Adn = Ad3[:, 0, :, :], Ad3[:, 1, :, :], Ad3[:, 2, :, :]
                dsk = sb(pb, "dsk", [128, 4], F32); B_dsk = Buf("dsk")
                bglu = sb(pb, "bglu", [128, 4], F32); B_bglu = Buf("bglu")
                wglu = sb(pb, "wglu", [128, 4, 512], BF16); B_wglu = Buf("wglu"); ds_wglu = k.dsem("wglu")
                ds_p = k.dsem("s5p")
                load_w(wglu, w_glu[l], 4, 0, 512, ds_wglu, B_wglu)
                k.dma(SP, dsk[:], dskT[l], ds_p, writes=[B_dsk])
                k.dma(SP, bglu[:], b_gluT[l], ds_p, writes=[B_bglu])

                k.dma(SP, Kw[:].rearrange("p a b c -> p (a b c)"), Kw_d[l], ds_p, writes=[B_Kw])
                k.dma(SP, WE[:].rearrange("p a b c d -> p (a b c d)"), WE_d[l], ds_p, writes=[B_WE])
                k.dma(SP, WI[:].rearrange("p a b c d -> p (a b c d)"), WI_d[l], ds_p, writes=[B_WI])
                k.dma(SP, Ad3[:].rearrange("p a b c -> p (a b c)"), Ad_d[l], ds_p, writes=[B_Ad])

                with ExitStack() as pc:
                    NSL = 2
                    stt_ = [[sb(pc, "st%d_%d" % (sl, i), [128, NCH], F32) for i in range(4)] for sl in range(NSL)]
                    B_st = [[Buf("st%d_%d" % (sl, i)) for i in range(4)] for sl in range(NSL)]
                    k.op(DVE, lambda e: e.memset(Sb[:], 0.0), writes=[B_Sb])

                    def scan_units():
                        for q in range(16):
                            ct, ql = q // 4, q % 4
                            sl = q % NSL
                            P_ = (stt_[sl][0], stt_[sl][1]); Q_ = (stt_[sl][2], stt_[sl][3])
                            BP = (B_st[sl][0], B_st[sl][1]); BQ = (B_st[sl][2], B_st[sl][3])
                            rows = slice(32 * ql, 32 * ql + 32)
                            for ri in range(2):
                                bk, bap = next_bank(6, 8)
                                k.mm(bk, bap[:, 0:NCH],
                                     [(WE[rows, s_, ri, ct, :], uT[rows, ct, s_:S:8]) for s_ in range(8)],
                                     reads=[B_WE, B_u], tile_position=(32 * ql, 0))
                                k.op(ACT, lambda e: e.copy(out=P_[ri][:], in_=bap[:, 0:NCH]), reads=[bk], writes=[BP[ri]])
                            yield
                            src, dst, Bs, Bd = P_, Q_, BP, BQ
                            for lv in range(LV):
                                d = 1 << lv
                                n = NCH - d
                                ar = Adr[:, lv, q:q + 1]; ai = Adi[:, lv, q:q + 1]; an = Adn[:, lv, q:q + 1]
                                lo_ = d // 2
                                for ri in range(2):
                                    k.op(DVE, lambda e: e.tensor_copy(out=dst[ri][:, lo_:d], in_=src[ri][:, lo_:d]),
                                         reads=[Bs[ri]], writes=[Bd[ri]])
                                k.op(DVE, lambda e: e.scalar_tensor_tensor(out=dst[0][:, d:NCH], in0=src[0][:, 0:n], scalar=ar,
                                                                           in1=src[0][:, d:NCH], op0=ALU.mult, op1=ALU.add),
                                     reads=[Bs[0], B_Ad], writes=[Bd[0]])
                                yield
                                k.op(DVE, lambda e: e.scalar_tensor_tensor(out=dst[0][:, d:NCH], in0=src[1][:, 0:n], scalar=an,
                                                                           in1=dst[0][:, d:NCH], op0=ALU.mult, op1=ALU.add),
                                     reads=[Bs[1], B_Ad, Bd[0]], writes=[Bd[0]])
                                yield
                                k.op(DVE, lambda e: e.scalar_tensor_tensor(out=dst[1][:, d:NCH], in0=src[1][:, 0:n], scalar=ar,
                                                                           in1=src[1][:, d:NCH], op0=ALU.mult, op1=ALU.add),
                                     reads=[Bs[1], B_Ad], writes=[Bd[1]])
                                yield
                                k.op(DVE, lambda e: e.scalar_tensor_tensor(out=dst[1][:, d:NCH], in0=src[0][:, 0:n], scalar=ai,
                                                                           in1=dst[1][:, d:NCH], op0=ALU.mult, op1=ALU.add),
                                     reads=[Bs[0], B_Ad, Bd[1]], writes=[Bd[1]])
                                yield
                                src, dst, Bs, Bd = dst, src, Bd, Bs
                            for ri in range(2):
                                k.op(POOL, lambda e: e.tensor_copy(out=Sb[:, q, ri, 1:NCH], in_=src[ri][:, 0:NCH - 1]),
                                     reads=[Bs[ri]], writes=[B_Sb])
                            yield

                    scan_gen = scan_units()
                    qa = [sb(pc, "qa%d" % i, [128, S], BF16) for i in range(2)]
                    ka = [sb(pc, "ka%d" % i, [128, S], BF16) for i in range(2)]
                    va = [sb(pc, "va%d" % i, [128, KT, 128], BF16) for i in range(2)]
                    B_qa = [Buf("qa%d" % i) for i in range(2)]
                    B_ka = [Buf("ka%d" % i) for i in range(2)]
                    B_va = [Buf("va%d" % i) for i in range(2)]
                    ds_qkv = [k.dsem("qkv") for _ in range(2)]
                    pT = [sb(pc, "pT%d" % i, [128, NT], BF16) for i in range(4)]
                    B_pT = [Buf("pT%d" % i) for i in range(4)]
                    rden = [sb(pc, "rden%d" % i, [128, NT], F32) for i in range(2)]
                    B_rden = [Buf("rden%d" % i) for i in range(2)]
                    yst = [sb(pc, "yst%d" % i, [128, NT], BF16) for i in range(2)]
                    B_yst = [Buf("yst%d" % i) for i in range(2)]
                    ds_yst = [k.dsem("yst") for _ in range(2)]
                    for i in range(2):
                        k.op(POOL, lambda e: e.memset(qa[i][:], 0.0), writes=[B_qa[i]])
                        k.op(POOL, lambda e: e.memset(ka[i][:], 0.0), writes=[B_ka[i]])
                        k.op(POOL, lambda e: e.memset(qa[i][64:70, :], 1.0), writes=[B_qa[i]])
                        k.op(POOL, lambda e: e.memset(ka[i][64:70, :], 1.0), writes=[B_ka[i]])
                    k.op(POOL, lambda e: e.memset(va[0][:, :, 64:128], 1.0), writes=[B_va[0]])
                    k.op(POOL, lambda e: e.memset(va[1][:, :, 0:64], 1.0), writes=[B_va[1]])

                    def load_head(h):
                        s_ = h % 2
                        hr = slice(h * 64, (h + 1) * 64)
                        k.dma(SP, qa[s_][0:64, :], qT_d[hr, :], ds_qkv[s_], writes=[B_qa[s_]])
                        k.dma(SP, qa[s_][67:70, :], cumq_d[h], ds_qkv[s_], writes=[B_qa[s_]])
                        k.dma(SP, ka[s_][0:64, :], kT_d[hr, :], ds_qkv[s_], writes=[B_ka[s_]])
                        k.dma(SP, ka[s_][64:67, :], cumk_d[h], ds_qkv[s_], writes=[B_ka[s_]])
                        vv = v_d.rearrange("(kt p) c -> p kt c", p=128)
                        co = 0 if s_ == 0 else 64
                        for a in range(0, KT, 8):
                            k.dma(SP, va[s_][:, a:a + 8, co:co + 64], vv[:, a:a + 8, hr], ds_qkv[s_], writes=[B_va[s_]])

                    items = []
                    for h in range(8):
                        for j in range(NTL):
                            for i in range(4 * j + 4):
                                items.append((h, j, i))
                    SB_LO, SB_HI = 0, 4
                    obanks = [(banks[4], psum[:, 4, :]), (banks[5], psum[:, 5, :])]
                    pend = []
                    load_head(0)
                    oi = 0
                    for n in range(len(items) + 2):
                        if n < len(items):
                            h, j, i = items[n]
                            s_ = h % 2
                            if j == 0 and i == 2 and h + 1 < 8:
                                load_head(h + 1)
                            r = i - 4 * j
                            c0 = 128 * r if r > 0 else 0
                            bk, bap = next_bank(SB_LO, SB_HI)
                            pi_ = n % 4
                            k.mm(bk, bap[:, c0:NT], [(ka[s_][:, i * 128:(i + 1) * 128], qa[s_][:, j * NT + c0:(j + 1) * NT])],
                                 reads=[B_ka[s_], B_qa[s_]])
                            k.op(ACT, lambda e: e.activation(out=pT[pi_][:, c0:NT], in_=bap[:, c0:NT], func=AF.Exp),
                                 reads=[bk], writes=[B_pT[pi_]])
                            if r >= 0:
                                k.op(POOL, lambda e: e.tensor_tensor(out=pT[pi_][:, c0:c0 + 128], in0=pT[pi_][:, c0:c0 + 128],
                                                                     in1=tri_bf[:], op=ALU.mult),
                                     reads=[B_pT[pi_], B_tri], writes=[B_pT[pi_]])
                            pend.append((h, j, i, c0, pi_))
                            if (n % 8) in (0, 1, 3, 4, 6):
                                next(scan_gen, None)
                        if n >= 2:
                            h, j, i, c0, pi_ = pend[n - 2]
                            s_ = h % 2
                            last = (i == 4 * j + 3)
                            if i == 0:
                                oi += 1
                            ob, oap = obanks[oi % 2]
                            k.mm(ob, oap[:, c0:NT], [(va[s_][:, i, :], pT[pi_][:, c0:NT])], reads=[B_va[s_], B_pT[pi_]],
                                 start=(i == 0), stop=last)
                            if last:
                                e2 = oi % 2
                                orow = slice(0, 64) if s_ == 0 else slice(64, 128)
                                drow = slice(64, 128) if s_ == 0 else slice(0, 64)
                                k.op(DVE, lambda e: e.reciprocal(out=rden[e2][orow, :], in_=oap[drow, :]), reads=[ob],
                                     writes=[B_rden[e2]])
                                k.op(DVE, lambda e: e.tensor_tensor(out=yst[e2][orow, :], in0=oap[orow, :], in1=rden[e2][orow, :],
                                                                    op=ALU.mult),
                                     reads=[ob, B_rden[e2]], writes=[B_yst[e2]])
                                k.dma(SP, yatt_d[h * 64:(h + 1) * 64, j * NT:(j + 1) * NT], yst[e2][orow, :], ds_yst[e2],
                                      reads=[B_yst[e2]])
                    for _ in scan_gen:
                        pass
                    k.barrier()

                with ExitStack() as pq:
                    zT = sb(pq, "zT", [128, 4, S], BF16); B_z = Buf("zT")
                    isb = [sb(pq, "isb%d" % i, [128, NCH], F32) for i in range(2)]
                    B_isb = [Buf("isb%d" % i) for i in range(2)]
                    y1 = [sb(pq, "y1_%d" % i, [128, NCH], F32) for i in range(2)]
                    B_y1 = [Buf("y1_%d" % i) for i in range(2)]
                    it = 0
                    for t_ in range(8):
                        for ct in range(4):
                            s2 = it % 2
                            it += 1
                            bki, bapi = next_bank()
                            k.mm(bki, bapi[:, 0:NCH],
                                 [(Kw[:, t_ - s_, ct, :], uT[:, ct, s_:S:8]) for s_ in range(t_ + 1)],
                                 reads=[B_Kw, B_u])
                            bke, bape = next_bank()
                            k.mm_multi(bke, [(bape[32 * ql:32 * ql + 32, 0:NCH],
                                              [(WI[:, ct * 4 + ql, t_, 0, :], Sb[:, ct * 4 + ql, 0, :]),
                                               (WI[:, ct * 4 + ql, t_, 1, :], Sb[:, ct * 4 + ql, 1, :])],
                                              (0, 32 * ql)) for ql in range(4)], reads=[B_WI, B_Sb])
                            k.op(ACT, lambda e: e.copy(out=isb[s2][:], in_=bape[:, 0:NCH]), reads=[bke], writes=[B_isb[s2]])
                            k.op(DVE, lambda e: e.scalar_tensor_tensor(out=y1[s2][:], in0=uT[:, ct, t_:S:8],
                                                                       scalar=dsk[:, ct:ct + 1], in1=bapi[:, 0:NCH],
                                                                       op0=ALU.mult, op1=ALU.add),
                                 reads=[B_u, B_dsk, bki], writes=[B_y1[s2]])
                            k.op(POOL, lambda e: e.tensor_tensor(out=y1[s2][:], in0=y1[s2][:], in1=isb[s2][:], op=ALU.add),
                                 reads=[B_y1[s2], B_isb[s2]], writes=[B_y1[s2]])
                            k.op(ACT, lambda e: e.activation(out=zT[:, ct, t_:S:8], in_=y1[s2][:], func=AF.Gelu_apprx_tanh),
                                 reads=[B_y1[s2]], writes=[B_z])
                    sg = [sb(pq, "sg%d" % i, [128, NT], F32) for i in range(2)]
                    B_sg = [Buf("sg%d" % i) for i in range(2)]
                    og = [sb(pq, "og%d" % i, [128, NT], BF16) for i in range(2)]
                    B_og = [Buf("og%d" % i) for i in range(2)]
                    ds_og = [k.dsem("og") for _ in range(2)]
                    it = 0
                    for t in range(NTL):
                        tc = slice(t * NT, (t + 1) * NT)
                        for ct in range(4):
                            s2 = it % 2
                            it += 1
                            bk, bap = next_bank()
                            k.mm(bk, bap, [(wglu[:, kk, ct * 128:(ct + 1) * 128], zT[:, kk, tc]) for kk in range(4)],
                                 reads=[B_wglu, B_z])
                            k.op(ACT, lambda e: e.activation(out=sg[s2][:], in_=bap, func=AF.Sigmoid, bias=bglu[:, ct:ct + 1]),
                                 reads=[bk, B_bglu], writes=[B_sg[s2]])
                            k.op(DVE, lambda e: e.tensor_tensor(out=og[s2][:], in0=sg[s2][:], in1=zT[:, ct, tc], op=ALU.mult),
                                 reads=[B_sg[s2], B_z], writes=[B_og[s2]])
                            k.dma(SP, yssm_d[ct * 128:(ct + 1) * 128, tc], og[s2][:], ds_og[s2], reads=[B_og[s2]])
                    k.barrier()

        with ExitStack() as pd:
            wG = sb(pd, "wG", [128, KC, 2048], BF16)
            wPA = sb(pd, "wPA", [128, 4, D], BF16)
            wPB = sb(pd, "wPB", [128, 4, D], BF16)
            wO = sb(pd, "wO", [128, KC, D], BF16)
            WG = WChunks(wG, w_in[l], KC, 2056, 2048, 512, "wG")
            WPA = WChunks(wPA, w_pa[l], 4, 0, D, 512, "wPA")
            WPB = WChunks(wPB, w_pb[l], 4, 0, D, 512, "wPB")
            WO = WChunks(wO, w_o[l], KC, 0, D, 512, "wO")
            WG.load(0); WG.load(2); WPA.load(0); WPB.load(0)
            WG.load(1); WG.load(3); WPA.load(1); WPB.load(1)
            WO.load(0); WO.load(1)
            xts = [sb(pd, "xtD%d" % i, [128, KC * NT], F32) for i in range(2)]
            B_xt = [Buf("xtD%d" % i) for i in range(2)]
            ds_x = [k.dsem("xD") for _ in range(2)]
            ysa = [sb(pd, "ysa%d" % i, [128, 8, NT], BF16) for i in range(2)]
            B_ysa = [Buf("ysa%d" % i) for i in range(2)]
            hT = sb(pd, "hTD", [128, KC, NT], BF16); B_h = Buf("hTD")
            mg = sb(pd, "mg", [128, KC, NT], BF16); B_mg = Buf("mg")
            yt = sb(pd, "ytD", [128, KC * NT], F32); B_yt = Buf("ytD")
            sga = [sb(pd, "sga%d" % i, [128, NT], F32) for i in range(2)]
            sgb = [sb(pd, "sgb%d" % i, [128, NT], F32) for i in range(2)]
            B_sga = [Buf("sga%d" % i) for i in range(2)]
            B_sgb = [Buf("sgb%d" % i) for i in range(2)]
            nctx = NormCtx(pd, "D")
            nctx2 = NormCtx(pd, "D2")
            hTs = [hT, sb(pd, "hTD2", [128, KC, NT], BF16)]
            B_hs = [B_h, Buf("hTD2")]

            def loadD(t):
                s_ = t % 2
                k.dma(SP, xt3(xts[s_]), x_view(x_pre, t), ds_x[s_], writes=[B_xt[s_]])
                k.dma(SP, ysa[s_][:, 0:4, :], yssm_d.rearrange("(k p) n -> p k n", p=128)[:, :, t * NT:(t + 1) * NT],
                      ds_x[s_], writes=[B_ysa[s_]])
                k.dma(SP, ysa[s_][:, 4:8, :], yatt_d.rearrange("(k p) n -> p k n", p=128)[:, :, t * NT:(t + 1) * NT],
                      ds_x[s_], writes=[B_ysa[s_]])

            def postD(t):
                s_ = t % 2
                nctx2.residual(yt, B_yt, xts[s_], B_xt[s_], l, 2)
                k.dma(SP, x_view(x_mid, t), xt3(xts[s_]), ds_x[s_], reads=[B_xt[s_]])

            def mstepD(t, m):
                s = t % 2
                hT, B_h = hTs[s], B_hs[s]
                s2 = m % 2
                mc = slice(m * 128, (m + 1) * 128)
                bka, bapa = next_bank()
                k.mm(bka, bapa, [(wG[:, kk, mc], hT[:, kk, :]) for kk in range(KC)], reads=[WG.buf(m * 128), B_h])
                k.op(ACT, lambda e: e.activation(out=sga[s2][:], in_=bapa, func=AF.Sigmoid), reads=[bka], writes=[B_sga[s2]])
                bkb, bapb = next_bank()
                k.mm(bkb, bapb, [(wG[:, kk, 1024 + m * 128:1024 + (m + 1) * 128], hT[:, kk, :]) for kk in range(KC)],
                     reads=[WG.buf(1024 + m * 128), B_h])
                k.op(ACT, lambda e: e.activation(out=sgb[s2][:], in_=bapb, func=AF.Sigmoid), reads=[bkb], writes=[B_sgb[s2]])
                bkp, bapp = next_bank()
                k.mm(bkp, bapp, [(wPA[:, kk, mc], ysa[s][:, kk, :]) for kk in range(4)], reads=[WPA.buf(m * 128), B_ysa[s]])
                k.op(DVE, lambda e: e.tensor_tensor(out=sga[s2][:], in0=bapp, in1=sga[s2][:], op=ALU.mult),
                     reads=[bkp, B_sga[s2]], writes=[B_sga[s2]])
                bkq, bapq = next_bank()
                k.mm(bkq, bapq, [(wPB[:, kk, mc], ysa[s][:, 4 + kk, :]) for kk in range(4)], reads=[WPB.buf(m * 128), B_ysa[s]])
                k.op(DVE, lambda e: e.tensor_tensor(out=sgb[s2][:], in0=bapq, in1=sgb[s2][:], op=ALU.mult),
                     reads=[bkq, B_sgb[s2]], writes=[B_sgb[s2]])
                k.op(POOL, lambda e: e.tensor_tensor(out=mg[:, m, :], in0=sga[s2][:], in1=sgb[s2][:], op=ALU.add),
                     reads=[B_sga[s2], B_sgb[s2]], writes=[B_mg])

            def ostepD(t, m):
                mc = slice(m * 128, (m + 1) * 128)
                bk, bap = next_bank()
                k.mm(bk, bap, [(wO[:, kk, mc], mg[:, kk, :]) for kk in range(KC)], reads=[WO.buf(m * 128), B_mg])
                if m % 2 == 0:
                    k.op(ACT, lambda e: e.copy(out=yt[:, m * NT:(m + 1) * NT], in_=bap), reads=[bk], writes=[B_yt])
                else:
                    k.op(DVE, lambda e: e.tensor_copy(out=yt[:, m * NT:(m + 1) * NT], in_=bap), reads=[bk], writes=[B_yt])

            loadD(0)
            nctx.modulate(xts[0], B_xt[0], hTs[0], B_hs[0], l, 0, 1)
            for t in range(NTL):
                for m in range(KC):
                    mstepD(t, m)
                    if m == 1:
                        if t > 0:
                            postD(t - 1)
                        if t + 1 < NTL:
                            loadD(t + 1)
                for m in range(KC):
                    if m == 4 and t + 1 < NTL:
                        s1 = (t + 1) % 2
                        nctx.modulate(xts[s1], B_xt[s1], hTs[s1], B_hs[s1], l, 0, 1)
                    ostepD(t, m)
            postD(NTL - 1)
            k.barrier()

        with ExitStack() as pe1:
            wg = sb(pe1, "wg", [128, KC, DFF], BF16)
            wu = sb(pe1, "wu", [128, KC, DFF], BF16)
            WGt = WChunks(wg, w_g[l], KC, 0, DFF, 512, "wg")
            WUp = WChunks(wu, w_u[l], KC, 0, DFF, 512, "wu")
            for i_ in range(WGt.n):
                WGt.load(i_); WUp.load(i_)
            xts = [sb(pe1, "xtE%d" % i, [128, KC * NT], F32) for i in range(2)]
            B_xt = [Buf("xtE%d" % i) for i in range(2)]
            ds_x = [k.dsem("xE") for _ in range(2)]
            hT = sb(pe1, "hTE", [128, KC, NT], BF16); B_h = Buf("hTE")
            sl_ = [sb(pe1, "sl%d" % i, [128, NT], F32) for i in range(2)]
            B_sl = [Buf("sl%d" % i) for i in range(2)]
            ao = [sb(pe1, "ao%d" % i, [128, NT], BF16) for i in range(4)]
            B_ao = [Buf("ao%d" % i) for i in range(4)]
            ds_ao = [k.dsem("ao") for _ in range(4)]
            nctx = NormCtx(pe1, "E")
            hTs = [hT, sb(pe1, "hTE2", [128, KC, NT], BF16)]
            B_hs = [B_h, Buf("hTE2")]
            k.dma(SP, xt3(xts[0]), x_view(x_mid, 0), ds_x[0], writes=[B_xt[0]])
            nctx.modulate(xts[0], B_xt[0], hTs[0], B_hs[0], l, 3, 4)
            it = 0
            for t in range(NTL):
                s = t % 2
                hT, B_h = hTs[s], B_hs[s]
                if t + 1 < NTL:
                    k.dma(SP, xt3(xts[1 - s]), x_view(x_mid, t + 1), ds_x[1 - s], writes=[B_xt[1 - s]])
                for m in range(FC):
                    if m == 12 and t + 1 < NTL:
                        nctx.modulate(xts[1 - s], B_xt[1 - s], hTs[1 - s], B_hs[1 - s], l, 3, 4)
                    mc = slice(m * 128, (m + 1) * 128)
                    s2 = it % 2
                    s4 = it % 4
                    it += 1
                    bkg, bapg = next_bank()
                    k.mm(bkg, bapg, [(wg[:, kk, mc], hT[:, kk, :]) for kk in range(KC)], reads=[WGt.buf(m * 128), B_h])
                    k.op(ACT, lambda e: e.activation(out=sl_[s2][:], in_=bapg, func=AF.Silu), reads=[bkg], writes=[B_sl[s2]])
                    bku, bapu = next_bank()
                    k.mm(bku, bapu, [(wu[:, kk, mc], hT[:, kk, :]) for kk in range(KC)], reads=[WUp.buf(m * 128), B_h])
                    k.op(DVE, lambda e: e.tensor_tensor(out=ao[s4][:], in0=bapu, in1=sl_[s2][:], op=ALU.mult),
                         reads=[bku, B_sl[s2]], writes=[B_ao[s4]])
                    k.dma(SP, aT_d[mc, t * NT:(t + 1) * NT], ao[s4][:], ds_ao[s4], reads=[B_ao[s4]])
            k.barrier()

        with ExitStack() as pe2:
            wd = sb(pe2, "wd", [128, FC, D], BF16)
            WD = WChunks(wd, w_d[l], FC, 0, D, 256, "wd")
            for i_ in range(WD.n):
                WD.load(i_)
            xts = [sb(pe2, "xtF%d" % i, [128, KC * NT], F32) for i in range(2)]
            B_xt = [Buf("xtF%d" % i) for i in range(2)]
            ds_x = [k.dsem("xF") for _ in range(2)]
            at = [sb(pe2, "at%d" % i, [128, FC, NT], BF16) for i in range(2)]
            B_at = [Buf("at%d" % i) for i in range(2)]
            yt = sb(pe2, "ytF", [128, KC * NT], F32); B_yt = Buf("ytF")
            nctx = NormCtx(pe2, "F")

            ds_at = [k.dsem("at") for _ in range(2)]

            def load_at(t):
                s_ = t % 2
                av = aT_d.rearrange("(k p) n -> p k n", p=128)
                for a in range(0, FC, 11):
                    k.dma(SP, at[s_][:, a:a + 11, :], av[:, a:a + 11, t * NT:(t + 1) * NT], ds_at[s_], writes=[B_at[s_]])

            def load_x(t):
                s_ = t % 2
                k.dma(SP, xt3(xts[s_]), x_view(x_mid, t), ds_x[s_], writes=[B_xt[s_]])

            def postF(t):
                s_ = t % 2
                nctx.residual(yt, B_yt, xts[s_], B_xt[s_], l, 5)
                k.dma(SP, x_view(x_post, t), xt3(xts[s_]), ds_x[s_], reads=[B_xt[s_]])

            load_at(0)
            load_x(0)
            for t in range(NTL):
                s = t % 2
                if t + 1 < NTL:
                    load_at(t + 1)
                for m in range(KC):
                    mc = slice(m * 128, (m + 1) * 128)
                    bk, bap = next_bank()
                    k.mm(bk, bap, [(wd[:, kk, mc], at[s][:, kk, :]) for kk in range(FC)], reads=[WD.buf(m * 128), B_at[s]])
                    if m % 2 == 0:
                        k.op(ACT, lambda e: e.copy(out=yt[:, m * NT:(m + 1) * NT], in_=bap), reads=[bk], writes=[B_yt])
                    else:
                        k.op(DVE, lambda e: e.tensor_copy(out=yt[:, m * NT:(m + 1) * NT], in_=bap), reads=[bk], writes=[B_yt])
                    if m == 0 and t > 0:
                        pass
                if t + 1 < NTL:
                    pass
                postF(t)
                if t + 1 < NTL:
                    load_x(t + 1)
            k.barrier()

    k.final_wait()
    es.close()
    return nc


def prep_shared(inp):
    f = lambda a: np.ascontiguousarray(np.asarray(a, dtype=np.float32))
    sh = {}
    sh["w_ada"] = f(inp["w_ada"])
    sh["b_adaT"] = f(np.asarray(inp["b_ada"]).reshape(L, 48, 128).transpose(0, 2, 1))
    g = np.stack([np.asarray(inp[n]).reshape(L, KC, 128).transpose(0, 2, 1)
                  for n in ("g_pre_mix", "g_post_mix", "g_pre_ffn", "g_post_ffn")], axis=2)
    sh["gT"] = f(g)
    sh["w_in"] = f(inp["w_in"])

    def ep(a):
        a = np.asarray(a)
        rest = a.shape[3:]
        a = a.reshape((L, 16, 2, 64) + rest)
        perm = (0, 2, 3, 1) + tuple(range(4, 4 + len(rest)))
        a = a.transpose(perm)
        return a.reshape((L, 128, 16) + rest)

    lam_re = ep(np.asarray(inp["lam_re"]))
    lam_im = ep(np.asarray(inp["lam_im"]))
    sh["lamT"] = f(np.stack([lam_re, lam_im], axis=2))
    ldt = np.broadcast_to(np.asarray(inp["log_dt"])[:, :, None], (L, 32, 64))
    sh["ldtT"] = f(ep(ldt))
    b_re = ep(np.asarray(inp["b_re"]))
    b_im = ep(np.asarray(inp["b_im"]))
    sh["bT"] = f(np.stack([b_re, b_im], axis=2))
    c_re = ep(np.asarray(inp["c_re"]).transpose(0, 1, 3, 2))
    c_im = ep(np.asarray(inp["c_im"]).transpose(0, 1, 3, 2))
    sh["cTT"] = f(np.stack([c_re, c_im], axis=2))
    sh["dskT"] = f(np.asarray(inp["d_skip"]).reshape(L, 4, 128).transpose(0, 2, 1))
    sh["w_glu"] = f(inp["w_glu"])
    sh["b_gluT"] = f(np.asarray(inp["b_glu"]).reshape(L, 4, 128).transpose(0, 2, 1))
    sh["b_f"] = f(np.asarray(inp["b_f"]).reshape(L, 8, 1))
    sh["w_pa"] = f(inp["w_pa"]); sh["w_pb"] = f(inp["w_pb"]); sh["w_o"] = f(inp["w_o"])
    sh["w_g"] = f(inp["w_ffn_gate"]); sh["w_u"] = f(inp["w_ffn_up"]); sh["w_d"] = f(inp["w_ffn_down"])
    kk = np.arange(128)
    sh["tri"] = f((kk[None, :] >= kk[:, None]).astype(np.float32))
    sh["ident"] = f(np.eye(128, dtype=np.float32))
    sh["bdm"] = f((kk[:, None] // 16 == kk[None, :] // 16).astype(np.float32))
    return sh


_NC_CACHE = {}


def kernel(**inputs):
    x = np.asarray(inputs["x"], dtype=np.float32)
    c = np.asarray(inputs["c"], dtype=np.float32)
    B, S, _ = x.shape
    sh = prep_shared(inputs)
    in_maps = []
    for b in range(B):
        m = dict(sh)
        m["xT"] = np.ascontiguousarray(x[b].T)
        m["cT"] = np.ascontiguousarray(c[b].reshape(KC, 128).T)
        in_maps.append(m)
    if S not in _NC_CACHE:
        _NC_CACHE[S] = build_nc(S)
    nc = _NC_CACHE[S]
    res = run_bass_kernel_spmd(nc, in_maps, core_ids=list(range(B)))
    out = np.stack([np.ascontiguousarray(np.asarray(r["yT"]).T) for r in res.results], axis=0)
    return out.astype(np.float32)
```

```python
import math
from contextlib import ExitStack

import numpy as np
import concourse.bass as bass
import concourse.mybir as mybir
from concourse.bass_utils import run_bass_kernel_spmd

F32 = mybir.dt.float32
BF16 = mybir.dt.bfloat16
I32 = mybir.dt.int32
ALU = mybir.AluOpType
AF = mybir.ActivationFunctionType

D = 1024
KC = 8
L = 2
DFF = 2816
FC = 22
NIN = 4104
NT = 512
EPS = 1e-6
PI = math.pi


class Buf:
    __slots__ = ("name", "w", "r")

    def __init__(self, name):
        self.name = name
        self.w = {}
        self.r = {}


class DSem:
    def __init__(self, sem):
        self.sem = sem
        self.issued = 0


class Eng:
    def __init__(self, name, h, sem):
        self.name = name
        self.h = h
        self.sem = sem
        self.cnt = 0
        self.waited = {}


class K:
    def __init__(self, nc, es):
        self.nc = nc
        self.es = es
        self.nsem = 0
        self.pe = Eng("pe", nc.tensor, self._sem("pe"))
        self.act = Eng("act", nc.scalar, self._sem("act"))
        self.dve = Eng("dve", nc.vector, self._sem("dve"))
        self.pool = Eng("pool", nc.gpsimd, self._sem("pool"))
        self.sp = Eng("sp", nc.sync, None)
        self.engs = [self.pe, self.act, self.dve, self.pool, self.sp]
        self.dsems = []
        self._occ = {}
        self._dcache = {}

    def _sem(self, name):
        self.nsem += 1
        return self.es.enter_context(self.nc.semaphore("s_%s_%d" % (name, self.nsem)))

    def dsem(self, name="d"):
        occ = self._occ.get(name, 0)
        self._occ[name] = occ + 1
        key = (name, occ)
        if key not in self._dcache:
            d = DSem(self._sem(name))
            self.dsems.append(d)
            self._dcache[key] = d
        return self._dcache[key]

    def layer_begin(self):
        self._occ = {}

    def _wait(self, eng, sem, val, is_dma):
        if (not is_dma) and sem is eng.sem and eng is self.pe:
            return
        key = id(sem)
        if eng.waited.get(key, 0) >= val:
            return
        eng.h.wait_ge(sem, val)
        eng.waited[key] = val

    def _deps(self, eng, reads, writes):
        for b in reads:
            for (sem, val, ds) in b.w.values():
                self._wait(eng, sem, ds.issued if ds is not None else val, ds is not None)
        for b in writes:
            for (sem, val, ds) in b.w.values():
                self._wait(eng, sem, ds.issued if ds is not None else val, ds is not None)
            for (sem, val, ds) in b.r.values():
                self._wait(eng, sem, ds.issued if ds is not None else val, ds is not None)

    def _record(self, tok, reads, writes):
        key = id(tok[0])
        for b in reads:
            b.r[key] = tok
        for b in writes:
            b.w = {key: tok}
            b.r = {}

    def op(self, eng, fn, reads=(), writes=()):
        self._deps(eng, reads, writes)
        ins = fn(eng.h)
        eng.cnt += 1
        ins.then_inc(eng.sem, 1)
        self._record((eng.sem, eng.cnt, None), reads, writes)

    def mm(self, out_buf, out_ap, pairs, reads, tile_position=None, start=True, stop=True):
        eng = self.pe
        self._deps(eng, reads, [out_buf])
        n = len(pairs)
        ins = None
        for i, (lhsT, rhs) in enumerate(pairs):
            kw = {}
            if tile_position is not None:
                kw["tile_position"] = tile_position
            ins = eng.h.matmul(out_ap, lhsT=lhsT, rhs=rhs, start=(start and i == 0),
                               stop=(stop and i == n - 1), **kw)
        eng.cnt += 1
        ins.then_inc(eng.sem, 1)
        self._record((eng.sem, eng.cnt, None), reads, [out_buf])

    def mm_multi(self, out_buf, groups, reads):
        eng = self.pe
        self._deps(eng, reads, [out_buf])
        ins = None
        for (out_ap, pairs, tp) in groups:
            n = len(pairs)
            for i, (lhsT, rhs) in enumerate(pairs):
                kw = {}
                if tp is not None:
                    kw["tile_position"] = tp
                ins = eng.h.matmul(out_ap, lhsT=lhsT, rhs=rhs, start=(i == 0), stop=(i == n - 1), **kw)
        eng.cnt += 1
        ins.then_inc(eng.sem, 1)
        self._record((eng.sem, eng.cnt, None), reads, [out_buf])

    def dma(self, eng, out, in_, ds, reads=(), writes=()):
        self._deps(eng, reads, writes)
        eng.h.dma_start(out=out, in_=in_).then_inc(ds.sem, 16)
        ds.issued += 16
        self._record((ds.sem, ds.issued, ds), reads, writes)

    def barrier(self):
        for e in self.engs:
            for o in self.engs:
                if o.sem is not None and o is not e and o.cnt > 0:
                    self._wait(e, o.sem, o.cnt, False)
            for d in self.dsems:
                if d.issued > 0:
                    self._wait(e, d.sem, d.issued, True)

    def final_wait(self):
        self.barrier()


def build_nc(S, debug=False, nl=L):
    NTL = S // NT
    NCH = S // 8
    KT = S // 128
    LV = int(round(math.log2(NCH)))
    assert 2 ** LV == NCH and NCH <= 512

    nc = bass.Bass("TRN2", target_bir_lowering=False)
    es = ExitStack()
    k = K(nc, es)
    PE, ACT, DVE, POOL, SP = k.pe, k.act, k.dve, k.pool, k.sp

    def din(name, shape, dt=F32):
        return nc.dram_tensor(name, list(shape), dt, kind="ExternalInput").ap()

    okind = "ExternalOutput" if debug else "Internal"

    def dscr(name, shape, dt):
        return nc.dram_tensor(name, list(shape), dt, kind=okind).ap()

    xT_in = din("xT", [D, S])
    cT_in = din("cT", [128, KC])
    w_ada = din("w_ada", [L, D, 6 * D])
    b_adaT = din("b_adaT", [L, 128, 48])
    gT = din("gT", [L, 128, 4, KC])
    w_in = din("w_in", [L, D, NIN])
    lamT = din("lamT", [L, 128, 2, 16])
    ldtT = din("ldtT", [L, 128, 16])
    bT = din("bT", [L, 128, 2, 16, 16])
    cTT = din("cTT", [L, 128, 2, 16, 16])
    dskT = din("dskT", [L, 128, 4])
    w_glu = din("w_glu", [L, 512, 512])
    b_gluT = din("b_gluT", [L, 128, 4])
    b_f = din("b_f", [L, 8, 1])
    w_pa = din("w_pa", [L, 512, D])
    w_pb = din("w_pb", [L, 512, D])
    w_o = din("w_o", [L, D, D])
    w_g = din("w_g", [L, D, DFF])
    w_u = din("w_u", [L, D, DFF])
    w_d = din("w_d", [L, DFF, D])
    tri_in = din("tri", [128, 128])
    ident_in = din("ident", [128, 128])
    bdm_in = din("bdm", [128, 128])

    yT_out = nc.dram_tensor("yT", [D, S], F32, kind="ExternalOutput").ap()
    xa_d = dscr("xa_d", [D, S], F32)
    xb_d = dscr("xb_d", [D, S], F32)
    qT_d = dscr("qT_d", [512, S], BF16)
    kT_d = dscr("kT_d", [512, S], BF16)
    v_d = dscr("v_d", [S, 512], BF16)
    cumq_d = dscr("cumq_d", [8, 3, S], BF16)
    cumk_d = dscr("cumk_d", [8, 3, S], BF16)
    yssm_d = dscr("yssm_d", [512, S], BF16)
    yatt_d = dscr("yatt_d", [512, S], BF16)
    aT_d = dscr("aT_d", [DFF, S], BF16)
    Kw_d = dscr("Kw_d", [L, 128, 8 * 4 * 128], BF16)
    WE_d = dscr("WE_d", [L, 128, 8 * 2 * 4 * 128], BF16)
    WI_d = dscr("WI_d", [L, 128, 16 * 8 * 2 * 32], BF16)
    Ad_d = dscr("Ad_d", [L, 128, 3 * LV * 16], F32)

    uid = [0]

    def sb(stack, name, shape, dt):
        uid[0] += 1
        return stack.enter_context(nc.sbuf_tensor("sb%d_%s" % (uid[0], name), list(shape), dt))

    psum = es.enter_context(nc.psum_tensor("psum", [128, 8, 512], F32))
    banks = [Buf("bank%d" % i) for i in range(8)]
    bank_rr = [0]

    def next_bank(lo=0, hi=8):
        n = hi - lo
        i = lo + (bank_rr[0] % n)
        bank_rr[0] += 1
        return banks[i], psum[:, i, :]

    ones_bf = sb(es, "ones_bf", [128, 128], BF16); B_ones = Buf("ones")
    tri_bf = sb(es, "tri_bf", [128, 128], BF16); B_tri = Buf("tri")
    ident_bf = sb(es, "ident_bf", [128, 128], BF16); B_ident = Buf("ident")
    bdm_f = sb(es, "bdm_f", [128, 128], F32); B_bdm = Buf("bdm")
    epsb = sb(es, "epsb", [128, 1], F32); B_eps = Buf("eps")
    vecs = sb(es, "vecs", [128, L, 6, KC], F32); B_vecs = Buf("vecs")
    ds_const = k.dsem("const")
    ds_const_sw = k.dsem("constsw")

    k.op(DVE, lambda e: e.memset(ones_bf[:], 1.0), writes=[B_ones])
    k.op(DVE, lambda e: e.memset(epsb[:], float(D) * EPS), writes=[B_eps])
    k.dma(POOL, tri_bf[:], tri_in, ds_const_sw, writes=[B_tri])
    k.dma(POOL, ident_bf[:], ident_in, ds_const_sw, writes=[B_ident])
    k.dma(SP, bdm_f[:], bdm_in, ds_const, writes=[B_bdm])

    with ExitStack() as ps_:
        cT = sb(ps_, "cT", [128, KC], F32); B_c = Buf("c")
        cond = sb(ps_, "cond", [128, KC], F32); B_cond = Buf("cond")
        modT = sb(ps_, "modT", [128, L, 48], F32); B_mod = Buf("mod")
        badaS = sb(ps_, "badaS", [128, L, 48], F32); B_bada = Buf("bada")
        gS = sb(ps_, "gS", [128, L, 4, KC], F32); B_g = Buf("g")
        tmpv = sb(ps_, "tmpv", [128, KC], F32); B_tmpv = Buf("tmpv")
        wab = [sb(ps_, "wab%d" % i, [128, KC, 768], F32) for i in range(3)]
        B_wab = [Buf("wab%d" % i) for i in range(3)]
        ds_wab = [k.dsem("wab") for _ in range(3)]
        k.dma(SP, cT[:], cT_in, ds_const, writes=[B_c])
        for l in range(L):
            k.dma(SP, badaS[:, l, :], b_adaT[l], ds_const, writes=[B_bada])
            k.dma(SP, gS[:, l, :, :], gT[l], ds_const, writes=[B_g])
        k.op(ACT, lambda e: e.activation(out=cond[:], in_=cT[:], func=AF.Silu), reads=[B_c], writes=[B_cond])

        Kw = sb(ps_, "KwP", [128, 8, 4, 128], BF16); B_Kw = Buf("KwP")
        WE = sb(ps_, "WEP", [128, 8, 2, 4, 128], BF16); B_WE = Buf("WEP")
        WI = sb(ps_, "WIP", [128, 16, 8, 2, 32], BF16); B_WI = Buf("WIP")
        Ad3 = sb(ps_, "Ad3", [128, 3, LV, 16], F32); B_Ad = Buf("AdP")
        Adr, Adi, Adn = Ad3[:, 0, :, :], Ad3[:, 1, :, :], Ad3[:, 2, :, :]
        ds_p = k.dsem("s5p")
        ds_po = k.dsem("s5po")

        def prep_layer(l):
            lam = sb(ps_, "lam", [128, 2, 16], F32); B_lam = Buf("lam")
            ldt = sb(ps_, "ldt", [128, 16], F32); B_ldt = Buf("ldt")
            bS = sb(ps_, "bS", [128, 2, 16, 16], F32); B_bS = Buf("bS")
            cS = sb(ps_, "cS", [128, 2, 16, 16], F32); B_cS = Buf("cS")
            k.dma(SP, lam[:], lamT[l], ds_p, writes=[B_lam])
            k.dma(SP, ldt[:], ldtT[l], ds_p, writes=[B_ldt])
            k.dma(SP, bS[:], bT[l], ds_p, writes=[B_bS])
            k.dma(SP, cS[:], cTT[l], ds_p, writes=[B_cS])
            NS = 16
            sc = sb(ps_, "sc", [128, NS, 16], F32)
            sci = sb(ps_, "sci", [128, 16], I32)
            B_sc = Buf("sc")

            def V(fn, extra_r=(), extra_w=()):
                k.op(DVE, fn, reads=[B_sc] + list(extra_r), writes=[B_sc] + list(extra_w))

            def A_(fn, extra_r=()):
                k.op(ACT, fn, reads=[B_sc] + list(extra_r), writes=[B_sc])

            DT, LR, AR, TH, MAG, T1, T2, SN, CS, LBR, LBI, FR, FI, T3, T4, T5 = [sc[:, i, :] for i in range(NS)]
            LAMRE, LAMIM = lam[:, 0, :], lam[:, 1, :]
            A_(lambda e: e.activation(out=DT, in_=ldt[:], func=AF.Exp), [B_ldt])
            V(lambda e: e.tensor_scalar(out=LR, in0=LAMRE, scalar1=-1e-4, scalar2=None, op0=ALU.min), [B_lam])
            V(lambda e: e.tensor_tensor(out=AR, in0=LR, in1=DT, op=ALU.mult))
            V(lambda e: e.tensor_tensor(out=TH, in0=LAMIM, in1=DT, op=ALU.mult), [B_lam])
            A_(lambda e: e.activation(out=MAG, in_=AR, func=AF.Exp))

            def range_reduce(dst, shift):
                V(lambda e: e.tensor_scalar(out=T1, in0=TH, scalar1=shift, scalar2=1.0 / (2 * PI), op0=ALU.add,
                                            op1=ALU.mult))
                V(lambda e: e.tensor_copy(out=sci[:], in_=T1))
                V(lambda e: e.tensor_copy(out=T1, in_=sci[:]))
                V(lambda e: e.tensor_scalar(out=T2, in0=TH, scalar1=shift, scalar2=None, op0=ALU.add))
                V(lambda e: e.scalar_tensor_tensor(out=T2, in0=T1, scalar=-2 * PI, in1=T2, op0=ALU.mult, op1=ALU.add))
                V(lambda e: e.tensor_scalar(out=T1, in0=T2, scalar1=PI, scalar2=-2 * PI, op0=ALU.is_gt, op1=ALU.mult))
                V(lambda e: e.tensor_tensor(out=T2, in0=T2, in1=T1, op=ALU.add))
                V(lambda e: e.tensor_scalar(out=T1, in0=T2, scalar1=-PI, scalar2=2 * PI, op0=ALU.is_lt, op1=ALU.mult))
                V(lambda e: e.tensor_tensor(out=T2, in0=T2, in1=T1, op=ALU.add))
                V(lambda e: e.tensor_scalar(out=dst, in0=T2, scalar1=-3.141592, scalar2=3.141592, op0=ALU.max,
                                            op1=ALU.min))

            yield
            range_reduce(T3, 0.0)
            yield
            A_(lambda e: e.activation(out=SN, in_=T3, func=AF.Sin))
            range_reduce(T3, PI / 2)
            A_(lambda e: e.activation(out=CS, in_=T3, func=AF.Sin))
            V(lambda e: e.tensor_tensor(out=LBR, in0=MAG, in1=CS, op=ALU.mult))
            V(lambda e: e.tensor_tensor(out=LBI, in0=MAG, in1=SN, op=ALU.mult))
            V(lambda e: e.tensor_tensor(out=T1, in0=LR, in1=LR, op=ALU.mult))
            V(lambda e: e.tensor_tensor(out=T2, in0=LAMIM, in1=LAMIM, op=ALU.mult), [B_lam])
            V(lambda e: e.tensor_tensor(out=T1, in0=T1, in1=T2, op=ALU.add))
            V(lambda e: e.reciprocal(out=T1, in_=T1))
            V(lambda e: e.tensor_scalar(out=T2, in0=LBR, scalar1=-1.0, scalar2=None, op0=ALU.add))
            V(lambda e: e.tensor_tensor(out=T3, in0=T2, in1=LR, op=ALU.mult))
            V(lambda e: e.tensor_tensor(out=T4, in0=LBI, in1=LAMIM, op=ALU.mult), [B_lam])
            V(lambda e: e.tensor_tensor(out=T3, in0=T3, in1=T4, op=ALU.add))
            V(lambda e: e.tensor_tensor(out=FR, in0=T3, in1=T1, op=ALU.mult))
            V(lambda e: e.tensor_tensor(out=T3, in0=LBI, in1=LR, op=ALU.mult))
            V(lambda e: e.tensor_tensor(out=T4, in0=T2, in1=LAMIM, op=ALU.mult), [B_lam])
            V(lambda e: e.tensor_tensor(out=T3, in0=T3, in1=T4, op=ALU.subtract))
            V(lambda e: e.tensor_tensor(out=FI, in0=T3, in1=T1, op=ALU.mult))
            Pr = sb(ps_, "Pr", [128, 9, 16], F32)
            Pi_ = sb(ps_, "Pi", [128, 9, 16], F32)
            V(lambda e: e.memset(Pr[:, 0, :], 1.0))
            V(lambda e: e.memset(Pi_[:, 0, :], 0.0))

            def cmul(o_r, o_i, a_r, a_i, b_r, b_i):
                V(lambda e: e.tensor_tensor(out=T4, in0=a_r, in1=b_r, op=ALU.mult))
                V(lambda e: e.tensor_tensor(out=T5, in0=a_i, in1=b_i, op=ALU.mult))
                V(lambda e: e.tensor_tensor(out=T3, in0=a_r, in1=b_i, op=ALU.mult))
                V(lambda e: e.tensor_tensor(out=T1, in0=a_i, in1=b_r, op=ALU.mult))
                V(lambda e: e.tensor_tensor(out=o_r, in0=T4, in1=T5, op=ALU.subtract))
                V(lambda e: e.tensor_tensor(out=o_i, in0=T3, in1=T1, op=ALU.add))

            for tau in range(8):
                cmul(Pr[:, tau + 1, :], Pi_[:, tau + 1, :], Pr[:, tau, :], Pi_[:, tau, :], LBR, LBI)
                yield
            V(lambda e: e.tensor_copy(out=Adr[:, 0, :], in_=Pr[:, 8, :]), extra_w=[B_Ad])
            V(lambda e: e.tensor_copy(out=Adi[:, 0, :], in_=Pi_[:, 8, :]), extra_w=[B_Ad])
            for lv in range(LV - 1):
                cmul(Adr[:, lv + 1, :], Adi[:, lv + 1, :], Adr[:, lv, :], Adi[:, lv, :], Adr[:, lv, :], Adi[:, lv, :])
                yield
            V(lambda e: e.tensor_scalar(out=Adn[:], in0=Adi[:], scalar1=-1.0, scalar2=None, op0=ALU.mult),
              extra_w=[B_Ad])
            Bbr = sb(ps_, "Bbr", [128, 16, 16], F32)
            Bbi = sb(ps_, "Bbi", [128, 16, 16], F32)
            W1 = sb(ps_, "W1", [128, 16, 16], F32)
            W2 = sb(ps_, "W2", [128, 16, 16], F32)

            def bc(ap2):
                return ap2.unsqueeze(2).broadcast_to([ap2.shape[0], 16, 16])

            V(lambda e: e.tensor_tensor(out=W1[:], in0=bS[:, 0, :, :], in1=bc(FR), op=ALU.mult), [B_bS])
            V(lambda e: e.tensor_tensor(out=W2[:], in0=bS[:, 1, :, :], in1=bc(FI), op=ALU.mult), [B_bS])
            V(lambda e: e.tensor_tensor(out=Bbr[:], in0=W1[:], in1=W2[:], op=ALU.subtract))
            V(lambda e: e.tensor_tensor(out=W1[:], in0=bS[:, 1, :, :], in1=bc(FR), op=ALU.mult), [B_bS])
            V(lambda e: e.tensor_tensor(out=W2[:], in0=bS[:, 0, :, :], in1=bc(FI), op=ALU.mult), [B_bS])
            V(lambda e: e.tensor_tensor(out=Bbi[:], in0=W1[:], in1=W2[:], op=ALU.add))
            Xp = sb(ps_, "Xp", [128, 8, 2, 512], BF16)
            Yp = sb(ps_, "Yp", [128, 2, 512], BF16)
            V(lambda e: e.memset(Xp[:], 0.0))
            V(lambda e: e.memset(Yp[:], 0.0))
            V(lambda e: e.memset(WI[:], 0.0), extra_w=[B_WI])

            def slot(ap_cols, e_):
                return ap_cols.rearrange("p (q e h) -> p q e h", q=16, e=2)[:, :, e_, :]

            for tau in range(8):
                for e_ in range(2):
                    hs = slice(e_ * 64, (e_ + 1) * 64)
                    pr = bc(Pr[hs, tau, :]); pi = bc(Pi_[hs, tau, :])
                    V(lambda e: e.tensor_tensor(out=W1[hs], in0=Bbr[hs], in1=pr, op=ALU.mult))
                    V(lambda e: e.tensor_tensor(out=W2[hs], in0=Bbi[hs], in1=pi, op=ALU.mult))
                    V(lambda e: e.tensor_tensor(out=slot(Xp[hs, tau, 0, :], e_), in0=W1[hs], in1=W2[hs], op=ALU.subtract))
                    V(lambda e: e.tensor_tensor(out=W1[hs], in0=Bbr[hs], in1=pi, op=ALU.mult))
                    V(lambda e: e.tensor_tensor(out=W2[hs], in0=Bbi[hs], in1=pr, op=ALU.mult))
                    V(lambda e: e.tensor_tensor(out=slot(Xp[hs, tau, 1, :], e_), in0=W1[hs], in1=W2[hs], op=ALU.add))
                    yield
            for e_ in range(2):
                hs = slice(e_ * 64, (e_ + 1) * 64)
                V(lambda e: e.tensor_copy(out=slot(Yp[hs, 0, :], e_), in_=cS[hs, 0, :, :]), [B_cS])
                V(lambda e: e.tensor_scalar(out=slot(Yp[hs, 1, :], e_), in0=cS[hs, 1, :, :], scalar1=-1.0,
                                            scalar2=None, op0=ALU.mult), [B_cS])
                for t_ in range(8):
                    pr = bc(Pr[hs, t_ + 1, :]); pi = bc(Pi_[hs, t_ + 1, :])
                    V(lambda e: e.tensor_tensor(out=W1[hs], in0=cS[hs, 0, :, :], in1=pr, op=ALU.mult), [B_cS])
                    V(lambda e: e.tensor_tensor(out=W2[hs], in0=cS[hs, 1, :, :], in1=pi, op=ALU.mult), [B_cS])
                    V(lambda e: e.tensor_tensor(out=WI[hs, :, t_, 0, e_ * 16:(e_ + 1) * 16], in0=W1[hs], in1=W2[hs],
                                                op=ALU.subtract), extra_w=[B_WI])
                    V(lambda e: e.tensor_tensor(out=W1[hs], in0=cS[hs, 0, :, :], in1=pi, op=ALU.mult), [B_cS])
                    V(lambda e: e.tensor_tensor(out=W2[hs], in0=cS[hs, 1, :, :], in1=pr, op=ALU.mult), [B_cS])
                    V(lambda e: e.tensor_tensor(out=W1[hs], in0=W1[hs], in1=W2[hs], op=ALU.add))
                    V(lambda e: e.tensor_scalar(out=WI[hs, :, t_, 1, e_ * 16:(e_ + 1) * 16], in0=W1[hs], scalar1=-1.0,
                                                scalar2=None, op0=ALU.mult), extra_w=[B_WI])
                    yield
            k.dma(ACT, WI_d[l], WI[:].rearrange("p a b c d -> p (a b c d)"), ds_po, reads=[B_WI])
            k.dma(ACT, Ad_d[l], Ad3[:].rearrange("p a b c -> p (a b c)"), ds_po, reads=[B_Ad])
            yield "PE"
            for tau in range(8):
                bk, bap = next_bank(0, 6)
                k.mm_multi(bk, [(bap[:, ct * 128:(ct + 1) * 128],
                                 [(Xp[:, tau, 0, ct * 128:(ct + 1) * 128], Yp[:, 0, ct * 128:(ct + 1) * 128]),
                                  (Xp[:, tau, 1, ct * 128:(ct + 1) * 128], Yp[:, 1, ct * 128:(ct + 1) * 128])], None)
                                for ct in range(4)], reads=[B_sc])
                for ct in range(4):
                    k.op(DVE, lambda e: e.tensor_tensor(out=Kw[:, tau, ct, :], in0=bap[:, ct * 128:(ct + 1) * 128],
                                                        in1=bdm_f[:], op=ALU.mult),
                         reads=[bk, B_bdm], writes=[B_Kw])
            for s_ in range(8):
                for ri in range(2):
                    bk, bap = next_bank(0, 6)
                    k.mm_multi(bk, [(bap[:, ct * 128:(ct + 1) * 128],
                                     [(Xp[:, 7 - s_, ri, ct * 128:(ct + 1) * 128], ident_bf[:])], None)
                                    for ct in range(4)], reads=[B_sc, B_ident])
                    k.op(ACT, lambda e: e.copy(out=WE[:, s_, ri, :, :].rearrange("p c n -> p (c n)"), in_=bap),
                         reads=[bk], writes=[B_WE])
            yield
            k.dma(ACT, Kw_d[l], Kw[:].rearrange("p a b c -> p (a b c)"), ds_po, reads=[B_Kw])
            k.dma(ACT, WE_d[l], WE[:].rearrange("p a b c d -> p (a b c d)"), ds_po, reads=[B_WE])
            yield

        prep_gens = [prep_layer(l_) for l_ in range(nl)]
        prep_parked = []

        def prep_pull():
            while prep_gens:
                u = next(prep_gens[0], "END")
                if u == "PE":
                    prep_parked.append(prep_gens.pop(0))
                    return
                if u == "END":
                    prep_gens.pop(0)
                    continue
                return

        it = 0
        for l in range(L):
            bk, bap = banks[6 + l % 2], psum[:, 6 + l % 2, :]
            wv = w_ada[l].rearrange("(k p) n -> p k n", p=128)
            for jc in range(8):
                s = it % 3
                it += 1
                k.dma(SP, wab[s][:], wv[:, :, jc * 768:(jc + 1) * 768], ds_wab[s], writes=[B_wab[s]])
                for jj in range(6):
                    j = jc * 6 + jj
                    k.mm(bk, bap[:, j:j + 1],
                         [(wab[s][:, kk, jj * 128:(jj + 1) * 128], cond[:, kk:kk + 1]) for kk in range(KC)],
                         reads=[B_wab[s], B_cond])
                    prep_pull()
            k.op(DVE, lambda e: e.tensor_tensor(out=modT[:, l, :], in0=bap[:, 0:48], in1=badaS[:, l, :], op=ALU.add),
                 reads=[bk, B_bada], writes=[B_mod])
            for (o, isc, ig) in ((0, 1, 0), (3, 4, 2)):
                k.op(DVE, lambda e: e.tensor_scalar(out=tmpv[:], in0=modT[:, l, isc * 8:(isc + 1) * 8], scalar1=1.0,
                                                    scalar2=32.0, op0=ALU.add, op1=ALU.mult),
                     reads=[B_mod], writes=[B_tmpv])
                k.op(DVE, lambda e: e.tensor_tensor(out=vecs[:, l, o, :], in0=tmpv[:], in1=gS[:, l, ig, :], op=ALU.mult),
                     reads=[B_tmpv, B_g], writes=[B_vecs])
            for (o, ish) in ((1, 0), (4, 3)):
                k.op(DVE, lambda e: e.tensor_copy(out=vecs[:, l, o, :], in_=modT[:, l, ish * 8:(ish + 1) * 8]),
                     reads=[B_mod], writes=[B_vecs])
            for (o, iga, ig) in ((2, 2, 1), (5, 5, 3)):
                k.op(DVE, lambda e: e.tensor_scalar(out=tmpv[:], in0=modT[:, l, iga * 8:(iga + 1) * 8], scalar1=32.0,
                                                    scalar2=None, op0=ALU.mult),
                     reads=[B_mod], writes=[B_tmpv])
                k.op(DVE, lambda e: e.tensor_tensor(out=vecs[:, l, o, :], in0=tmpv[:], in1=gS[:, l, ig, :], op=ALU.mult),
                     reads=[B_tmpv, B_g], writes=[B_vecs])
        while prep_gens:
            prep_pull()
        for g_ in prep_parked:
            for _ in g_:
                pass
        k.barrier()

    def load_w(dst, src2d, kc, c0, ncols, dsm, buf, dcol0=0):
        v = src2d.rearrange("(k p) n -> p k n", p=128)
        step = 1024
        for a in range(0, ncols, step):
            n = min(step, ncols - a)
            k.dma(POOL, dst[:, 0:kc, dcol0 + a:dcol0 + a + n], v[:, :, c0 + a:c0 + a + n], dsm, writes=[buf])

    class WChunks:
        def __init__(self, tile, src2d, kc, c0, ncols, chunk, name):
            self.tile, self.src2d, self.kc, self.c0, self.chunk, self.name = tile, src2d, kc, c0, chunk, name
            self.n = (ncols + chunk - 1) // chunk
            self.ncols = ncols
            self.bufs = [Buf("%s_c%d" % (name, i)) for i in range(self.n)]
            self.ds = [k.dsem("%s_c" % name) for _ in range(self.n)]

        def load(self, i):
            a = i * self.chunk
            n = min(self.chunk, self.ncols - a)
            load_w(self.tile, self.src2d, self.kc, self.c0 + a, n, self.ds[i], self.bufs[i], dcol0=a)

        def buf(self, col):
            return self.bufs[col // self.chunk]

    class NormCtx:
        def __init__(self, stack, tag):
            self.sq = sb(stack, "sq" + tag, [128, KC * NT], BF16); self.B_sq = Buf("sq")
            self.rs = sb(stack, "rs" + tag, [128, NT], F32); self.B_rs = Buf("rs")
            self.tmp = [sb(stack, "ntmp%d%s" % (i, tag), [128, NT], F32) for i in range(2)]
            self.B_tmp = [Buf("ntmp%d" % i) for i in range(2)]
            self.i = 0

        def rstd(self, src, B_src):
            k.op(ACT, lambda e: e.activation(out=self.sq[:], in_=src, func=AF.Square), reads=[B_src], writes=[self.B_sq])
            bk, bap = next_bank()
            k.mm(bk, bap, [(ones_bf[:], self.sq[:, kk * NT:(kk + 1) * NT]) for kk in range(KC)],
                 reads=[B_ones, self.B_sq])
            k.op(ACT, lambda e: e.activation(out=self.rs[:], in_=bap, func=AF.Sqrt, bias=epsb[:, 0:1], scale=1.0),
                 reads=[bk, B_eps], writes=[self.B_rs])
            k.op(DVE, lambda e: e.reciprocal(out=self.rs[:], in_=self.rs[:]), reads=[self.B_rs], writes=[self.B_rs])

        def modulate(self, xt, B_x, hT, B_h, l, ia, ib):
            self.rstd(xt[:], B_x)
            for kk in range(KC):
                t = self.i % 2
                self.i += 1
                tm, Bt = self.tmp[t], self.B_tmp[t]
                k.op(DVE, lambda e: e.tensor_tensor(out=tm[:], in0=xt[:, kk * NT:(kk + 1) * NT], in1=self.rs[:], op=ALU.mult),
                     reads=[B_x, self.B_rs], writes=[Bt])
                k.op(ACT, lambda e: e.activation(out=hT[:, kk, :], in_=tm[:], func=AF.Identity,
                                                 scale=vecs[:, l, ia, kk:kk + 1], bias=vecs[:, l, ib, kk:kk + 1]),
                     reads=[Bt, B_vecs], writes=[B_h])

        def residual(self, yt, B_y, xt, B_x, l, ig):
            self.rstd(yt[:], B_y)
            for kk in range(KC):
                t = self.i % 2
                self.i += 1
                tm, Bt = self.tmp[t], self.B_tmp[t]
                k.op(DVE, lambda e: e.scalar_tensor_tensor(out=tm[:], in0=yt[:, kk * NT:(kk + 1) * NT],
                                                           scalar=vecs[:, l, ig, kk:kk + 1], in1=self.rs[:],
                                                           op0=ALU.mult, op1=ALU.mult),
                     reads=[B_y, self.B_rs, B_vecs], writes=[Bt])
                k.op(POOL, lambda e: e.tensor_tensor(out=xt[:, kk * NT:(kk + 1) * NT], in0=xt[:, kk * NT:(kk + 1) * NT],
                                                     in1=tm[:], op=ALU.add),
                     reads=[Bt, B_x], writes=[B_x])

    def x_view(xd, t):
        return xd.rearrange("(k p) n -> p k n", p=128)[:, :, t * NT:(t + 1) * NT]

    def xt3(xt):
        return xt[:].rearrange("p (k n) -> p k n", k=KC)

    for l in range(nl):
        k.layer_begin()
        x_pre = xT_in if l == 0 else xb_d
        x_mid = xa_d
        x_post = yT_out if l == nl - 1 else xb_d

        with ExitStack() as pm:
            uT = sb(pm, "uT", [128, 4, S], BF16); B_u = Buf("uT")
            with ExitStack() as pa0:
              fT = sb(pa0, "fT", [8, S], F32); B_f = Buf("fT")
              with ExitStack() as pa:
                wA = sb(pa, "wA", [128, KC, 2048], BF16)
                WA = WChunks(wA, w_in[l], KC, 0, 2048, 512, "wA")
                for i_ in range(WA.n):
                    WA.load(i_)
                wF = sb(pa, "wF", [128, KC, 32], BF16); B_wF = Buf("wF")
                wf32 = sb(pa, "wf32", [128, KC, 8], F32); B_wf32 = Buf("wf32"); ds_wf = k.dsem("wf")
                k.dma(SP, wf32[:], w_in[l].rearrange("(k p) n -> p k n", p=128)[:, :, 2048:2056], ds_wf, writes=[B_wf32])
                k.op(DVE, lambda e: e.memset(wF[:], 0.0), writes=[B_wF])
                k.op(DVE, lambda e: e.tensor_copy(out=wF[:, :, 0:8], in_=wf32[:]), reads=[B_wf32], writes=[B_wF])
                xts = [sb(pa, "xtA%d" % i, [128, KC * NT], F32) for i in range(2)]
                B_xt = [Buf("xtA%d" % i) for i in range(2)]
                ds_x = [k.dsem("xA") for _ in range(2)]
                hT = sb(pa, "hTA", [128, KC, NT], BF16); B_h = Buf("hTA")
                stg = [sb(pa, "stgA%d" % i, [128, NT], BF16) for i in range(4)]
                B_stg = [Buf("stgA%d" % i) for i in range(4)]
                ds_stg = [k.dsem("stgA") for _ in range(4)]
                nctx = NormCtx(pa, "A")
                sti = [0]

                def stage_out(bk, bap, dst, eng_i, scale=None):
                    s_ = sti[0] % 4
                    sti[0] += 1
                    if eng_i % 2 == 0:
                        if scale is None:
                            k.op(ACT, lambda e: e.copy(out=stg[s_][:], in_=bap), reads=[bk], writes=[B_stg[s_]])
                        else:
                            k.op(ACT, lambda e: e.activation(out=stg[s_][:], in_=bap, func=AF.Copy, scale=scale),
                                 reads=[bk], writes=[B_stg[s_]])
                    else:
                        if scale is None:
                            k.op(DVE, lambda e: e.tensor_copy(out=stg[s_][:], in_=bap), reads=[bk], writes=[B_stg[s_]])
                        else:
                            k.op(DVE, lambda e: e.tensor_scalar(out=stg[s_][:], in0=bap, scalar1=scale, scalar2=None,
                                                                op0=ALU.mult), reads=[bk], writes=[B_stg[s_]])
                    k.dma(SP, dst, stg[s_][:], ds_stg[s_], reads=[B_stg[s_]])

                hTs = [hT, sb(pa, "hTA2", [128, KC, NT], BF16)]
                B_hs = [B_h, Buf("hTA2")]
                k.dma(SP, xt3(xts[0]), x_view(x_pre, 0), ds_x[0], writes=[B_xt[0]])
                nctx.modulate(xts[0], B_xt[0], hTs[0], B_hs[0], l, 0, 1)
                for t in range(NTL):
                    s = t % 2
                    hT, B_h = hTs[s], B_hs[s]
                    if t + 1 < NTL:
                        k.dma(SP, xt3(xts[1 - s]), x_view(x_pre, t + 1), ds_x[1 - s], writes=[B_xt[1 - s]])
                    tc = slice(t * NT, (t + 1) * NT)
                    ei = 0
                    for m in range(12):
                        if m == 7 and t + 1 < NTL:
                            nctx.modulate(xts[1 - s], B_xt[1 - s], hTs[1 - s], B_hs[1 - s], l, 0, 1)
                        bk, bap = next_bank()
                        k.mm(bk, bap, [(wA[:, kk, m * 128:(m + 1) * 128], hT[:, kk, :]) for kk in range(KC)],
                             reads=[WA.buf(m * 128), B_h])
                        if m < 4:
                            if m % 2 == 0:
                                k.op(ACT, lambda e: e.copy(out=uT[:, m, tc], in_=bap), reads=[bk], writes=[B_u])
                            else:
                                k.op(DVE, lambda e: e.tensor_copy(out=uT[:, m, tc], in_=bap), reads=[bk], writes=[B_u])
                        elif m < 8:
                            stage_out(bk, bap, qT_d[(m - 4) * 128:(m - 3) * 128, tc], ei, scale=0.125); ei += 1
                        else:
                            stage_out(bk, bap, kT_d[(m - 8) * 128:(m - 7) * 128, tc], ei); ei += 1
                    for sub in range(4):
                        bk, bap = next_bank()
                        k.mm(bk, bap, [(hT[:, kk, sub * 128:(sub + 1) * 128], wA[:, kk, 1536:2048]) for kk in range(KC)],
                             reads=[WA.buf(1536), B_h])
                        stage_out(bk, bap, v_d[t * NT + sub * 128:t * NT + (sub + 1) * 128, :], ei); ei += 1
                    bk, bap = next_bank()
                    k.mm(bk, bap[0:32, :], [(wF[:, kk, :], hT[:, kk, :]) for kk in range(KC)], reads=[B_wF, B_h])
                    k.op(DVE, lambda e: e.tensor_copy(out=fT[:, tc], in_=bap[0:8, :]), reads=[bk], writes=[B_f])
                k.barrier()
              with ExitStack() as pa:
                bfS = sb(pa, "bfS", [8, 1], F32); B_bf = Buf("bf")
                onesS = sb(pa, "onesS", [8, S], F32); B_on = Buf("onesS")
                l1 = sb(pa, "l1", [8, S], F32); B_l1 = Buf("l1")
                ncum = sb(pa, "ncum", [8, S], F32); B_nc = Buf("ncum")
                cb = sb(pa, "cb", [8, 3, S], BF16); B_cb = Buf("cb")
                cbn = sb(pa, "cbn", [8, 3, S], BF16); B_cbn = Buf("cbn")
                ds_c = k.dsem("cum")
                k.dma(SP, bfS[:], b_f[l], ds_c, writes=[B_bf])
                k.op(DVE, lambda e: e.tensor_scalar(out=bfS[:], in0=bfS[:], scalar1=-1.0, scalar2=None, op0=ALU.mult),
                     reads=[B_bf], writes=[B_bf])
                k.op(DVE, lambda e: e.memset(onesS[:], 1.0), writes=[B_on])
                k.op(ACT, lambda e: e.activation(out=l1[:], in_=fT[:], func=AF.Exp, scale=-1.0, bias=bfS[:, 0:1]),
                     reads=[B_f, B_bf], writes=[B_l1])
                k.op(ACT, lambda e: e.activation(out=l1[:], in_=l1[:], func=AF.Ln, bias=1.0), reads=[B_l1], writes=[B_l1])
                k.op(DVE, lambda e: e.tensor_tensor_scan(out=ncum[:], data0=onesS[:], data1=l1[:], initial=0.0,
                                                         op0=ALU.mult, op1=ALU.add),
                     reads=[B_on, B_l1], writes=[B_nc])
                for j in range(3):
                    k.op(DVE, lambda e: e.tensor_copy(out=cb[:, j, :], in_=ncum[:]), reads=[B_nc], writes=[B_cb])
                    if j < 2:
                        k.op(DVE, lambda e: e.tensor_tensor(out=ncum[:], in0=ncum[:], in1=cb[:, j, :], op=ALU.subtract),
                             reads=[B_nc, B_cb], writes=[B_nc])
                k.op(DVE, lambda e: e.tensor_scalar(out=cbn[:], in0=cb[:], scalar1=-1.0, scalar2=None, op0=ALU.mult),
                     reads=[B_cb], writes=[B_cbn])
                k.dma(SP, cumk_d.rearrange("h j s -> h (j s)"), cb[:].rearrange("h j s -> h (j s)"), ds_c, reads=[B_cb])
                k.dma(SP, cumq_d.rearrange("h j s -> h (j s)"), cbn[:].rearrange("h j s -> h (j s)"), ds_c, reads=[B_cbn])
                k.barrier()

            with ExitStack() as pb:
                Kw = sb(pb, "Kw", [128, 8, 4, 128], BF16); B_Kw = Buf("Kw")
                WE = sb(pb, "WE", [128, 8, 2, 4, 128], BF16); B_WE = Buf("WE")
                WI = sb(pb, "WI", [128, 16, 8, 2, 32], BF16); B_WI = Buf("WI")
                Sb = sb(pb, "Sb", [128, 16, 2, NCH], BF16); B_Sb = Buf("Sb")
                Ad3 = sb(pb, "Ad3", [128, 3, LV, 16], F32); B_Ad = Buf("Ad")
                Adr, Adi, Adn = Ad3[:, 0, :, :], Ad3[:, 1, :, :], Ad3[:, 2, :, :]
                dsk = sb(pb, "dsk", [128, 4], F32); B_dsk = Buf("dsk")
                bglu = sb(pb, "bglu", [128, 4], F32); B_bglu = Buf("bglu")
                wglu = sb(pb, "wglu", [128, 4, 512], BF16); B_wglu = Buf("wglu"); ds_wglu = k.dsem("wglu")
                ds_p = k.dsem("s5p")
                load_w(wglu, w_glu[l], 4, 0, 512, ds_wglu, B_wglu)
                k.dma(SP, dsk[:], dskT[l], ds_p, writes=[B_dsk])
                k.dma(SP, bglu[:], b_gluT[l], ds_p, writes=[B_bglu])

                k.dma(SP, Kw[:].rearrange("p a b c -> p (a b c)"), Kw_d[l], ds_p, writes=[B_Kw])
                k.dma(SP, WE[:].rearrange("p a b c d -> p (a b c d)"), WE_d[l], ds_p, writes=[B_WE])
                k.dma(SP, WI[:].rearrange("p a b c d -> p (a b c d)"), WI_d[l], ds_p, writes=[B_WI])
                k.dma(SP, Ad3[:].rearrange("p a b c -> p (a b c)"), Ad_d[l], ds_p, writes=[B_Ad])

                with ExitStack() as pc:
                    NSL = 2
                    stt_ = [[sb(pc, "st%d_%d" % (sl, i), [128, NCH], F32) for i in range(4)] for sl in range(NSL)]
                    B_st = [[Buf("st%d_%d" % (sl, i)) for i in range(4)] for sl in range(NSL)]
                    k.op(DVE, lambda e: e.memset(Sb[:], 0.0), writes=[B_Sb])

                    def scan_units():
                        for q in range(16):
                            ct, ql = q // 4, q % 4
                            sl = q % NSL
                            P_ = (stt_[sl][0], stt_[sl][1]); Q_ = (stt_[sl][2], stt_[sl][3])
                            BP = (B_st[sl][0], B_st[sl][1]); BQ = (B_st[sl][2], B_st[sl][3])
                            rows = slice(32 * ql, 32 * ql + 32)
                            for ri in range(2):
                                bk, bap = next_bank(6, 8)
                                k.mm(bk, bap[:, 0:NCH],
                                     [(WE[rows, s_, ri, ct, :], uT[rows, ct, s_:S:8]) for s_ in range(8)],
                                     reads=[B_WE, B_u], tile_position=(32 * ql, 0))
                                k.op(ACT, lambda e: e.copy(out=P_[ri][:], in_=bap[:, 0:NCH]), reads=[bk], writes=[BP[ri]])
                            yield
                            src, dst, Bs, Bd = P_, Q_, BP, BQ
                            for lv in range(LV):
                                d = 1 << lv
                                n = NCH - d
                                ar = Adr[:, lv, q:q + 1]; ai = Adi[:, lv, q:q + 1]; an = Adn[:, lv, q:q + 1]
                                lo_ = d // 2
                                for ri in range(2):
                                    k.op(DVE, lambda e: e.tensor_copy(out=dst[ri][:, lo_:d], in_=src[ri][:, lo_:d]),
                                         reads=[Bs[ri]], writes=[Bd[ri]])
                                k.op(DVE, lambda e: e.scalar_tensor_tensor(out=dst[0][:, d:NCH], in0=src[0][:, 0:n], scalar=ar,
                                                                           in1=src[0][:, d:NCH], op0=ALU.mult, op1=ALU.add),
                                     reads=[Bs[0], B_Ad], writes=[Bd[0]])
                                yield
                                k.op(DVE, lambda e: e.scalar_tensor_tensor(out=dst[0][:, d:NCH], in0=src[1][:, 0:n], scalar=an,
                                                                           in1=dst[0][:, d:NCH], op0=ALU.mult, op1=ALU.add),
                                     reads=[Bs[1], B_Ad, Bd[0]], writes=[Bd[0]])
                                yield
                                k.op(DVE, lambda e: e.scalar_tensor_tensor(out=dst[1][:, d:NCH], in0=src[1][:, 0:n], scalar=ar,
                                                                           in1=src[1][:, d:NCH], op0=ALU.mult, op1=ALU.add),
                                     reads=[Bs[1], B_Ad], writes=[Bd[1]])
                                yield
                                k.op(DVE, lambda e: e.scalar_tensor_tensor(out=dst[1][:, d:NCH], in0=src[0][:, 0:n], scalar=ai,
                                                                           in1=dst[1][:, d:NCH], op0=ALU.mult, op1=ALU.add),
                                     reads=[Bs[0], B_Ad, Bd[1]], writes=[Bd[1]])
                                yield
                                src, dst, Bs, Bd = dst, src, Bd, Bs
                            for ri in range(2):
                                k.op(POOL, lambda e: e.tensor_copy(out=Sb[:, q, ri, 1:NCH], in_=src[ri][:, 0:NCH - 1]),
                                     reads=[Bs[ri]], writes=[B_Sb])
                            yield

                    scan_gen = scan_units()
                    qa = [sb(pc, "qa%d" % i, [128, S], BF16) for i in range(2)]
                    ka = [sb(pc, "ka%d" % i, [128, S], BF16) for i in range(2)]
                    va = [sb(pc, "va%d" % i, [128, KT, 128], BF16) for i in range(2)]
                    B_qa = [Buf("qa%d" % i) for i in range(2)]
                    B_ka = [Buf("ka%d" % i) for i in range(2)]
                    B_va = [Buf("va%d" % i) for i in range(2)]
                    ds_qkv = [k.dsem("qkv") for _ in range(2)]
                    pT = [sb(pc, "pT%d" % i, [128, NT], BF16) for i in range(4)]
                    B_pT = [Buf("pT%d" % i) for i in range(4)]
                    rden = [sb(pc, "rden%d" % i, [128, NT], F32) for i in range(2)]
                    B_rden = [Buf("rden%d" % i) for i in range(2)]
                    yst = [sb(pc, "yst%d" % i, [128, NT], BF16) for i in range(2)]
                    B_yst = [Buf("yst%d" % i) for i in range(2)]
                    ds_yst = [k.dsem("yst") for _ in range(2)]
                    for i in range(2):
                        k.op(POOL, lambda e: e.memset(qa[i][:], 0.0), writes=[B_qa[i]])
                        k.op(POOL, lambda e: e.memset(ka[i][:], 0.0), writes=[B_ka[i]])
                        k.op(POOL, lambda e: e.memset(qa[i][64:70, :], 1.0), writes=[B_qa[i]])
                        k.op(POOL, lambda e: e.memset(ka[i][64:70, :], 1.0), writes=[B_ka[i]])
                    k.op(POOL, lambda e: e.memset(va[0][:, :, 64:128], 1.0), writes=[B_va[0]])
                    k.op(POOL, lambda e: e.memset(va[1][:, :, 0:64], 1.0), writes=[B_va[1]])

                    def load_head(h):
                        s_ = h % 2
                        hr = slice(h * 64, (h + 1) * 64)
                        k.dma(SP, qa[s_][0:64, :], qT_d[hr, :], ds_qkv[s_], writes=[B_qa[s_]])
                        k.dma(SP, qa[s_][67:70, :], cumq_d[h], ds_qkv[s_], writes=[B_qa[s_]])
                        k.dma(SP, ka[s_][0:64, :], kT_d[hr, :], ds_qkv[s_], writes=[B_ka[s_]])
                        k.dma(SP, ka[s_][64:67, :], cumk_d[h], ds_qkv[s_], writes=[B_ka[s_]])
                        vv = v_d.rearrange("(kt p) c -> p kt c", p=128)
                        co = 0 if s_ == 0 else 64
                        for a in range(0, KT, 8):
                            k.dma(SP, va[s_][:, a:a + 8, co:co + 64], vv[:, a:a + 8, hr], ds_qkv[s_], writes=[B_va[s_]])

                    items = []
                    for h in range(8):
                        for j in range(NTL):
                            for i in range(4 * j + 4):
                                items.append((h, j, i))
                    SB_LO, SB_HI = 0, 4
                    obanks = [(banks[4], psum[:, 4, :]), (banks[5], psum[:, 5, :])]
                    pend = []
                    load_head(0)
                    oi = 0
                    for n in range(len(items) + 2):
                        if n < len(items):
                            h, j, i = items[n]
                            s_ = h % 2
                            if j == 0 and i == 2 and h + 1 < 8:
                                load_head(h + 1)
                            r = i - 4 * j
                            c0 = 128 * r if r > 0 else 0
                            bk, bap = next_bank(SB_LO, SB_HI)
                            pi_ = n % 4
                            k.mm(bk, bap[:, c0:NT], [(ka[s_][:, i * 128:(i + 1) * 128], qa[s_][:, j * NT + c0:(j + 1) * NT])],
                                 reads=[B_ka[s_], B_qa[s_]])
                            k.op(ACT, lambda e: e.activation(out=pT[pi_][:, c0:NT], in_=bap[:, c0:NT], func=AF.Exp),
                                 reads=[bk], writes=[B_pT[pi_]])
                            if r >= 0:
                                k.op(POOL, lambda e: e.tensor_tensor(out=pT[pi_][:, c0:c0 + 128], in0=pT[pi_][:, c0:c0 + 128],
                                                                     in1=tri_bf[:], op=ALU.mult),
                                     reads=[B_pT[pi_], B_tri], writes=[B_pT[pi_]])
                            pend.append((h, j, i, c0, pi_))
                            if (n % 8) != 7:
                                next(scan_gen, None)
                        if n >= 2:
                            h, j, i, c0, pi_ = pend[n - 2]
                            s_ = h % 2
                            last = (i == 4 * j + 3)
                            if i == 0:
                                oi += 1
                            ob, oap = obanks[oi % 2]
                            k.mm(ob, oap[:, c0:NT], [(va[s_][:, i, :], pT[pi_][:, c0:NT])], reads=[B_va[s_], B_pT[pi_]],
                                 start=(i == 0), stop=last)
                            if last:
                                e2 = oi % 2
                                orow = slice(0, 64) if s_ == 0 else slice(64, 128)
                                drow = slice(64, 128) if s_ == 0 else slice(0, 64)
                                k.op(DVE, lambda e: e.reciprocal(out=rden[e2][orow, :], in_=oap[drow, :]), reads=[ob],
                                     writes=[B_rden[e2]])
                                k.op(DVE, lambda e: e.tensor_tensor(out=yst[e2][orow, :], in0=oap[orow, :], in1=rden[e2][orow, :],
                                                                    op=ALU.mult),
                                     reads=[ob, B_rden[e2]], writes=[B_yst[e2]])
                                k.dma(SP, yatt_d[h * 64:(h + 1) * 64, j * NT:(j + 1) * NT], yst[e2][orow, :], ds_yst[e2],
                                      reads=[B_yst[e2]])
                    for _ in scan_gen:
                        pass
                    k.barrier()

                with ExitStack() as pq:
                    zT = sb(pq, "zT", [128, 4, S], BF16); B_z = Buf("zT")
                    isb = [sb(pq, "isb%d" % i, [128, NCH], F32) for i in range(2)]
                    B_isb = [Buf("isb%d" % i) for i in range(2)]
                    y1 = [sb(pq, "y1_%d" % i, [128, NCH], F32) for i in range(2)]
                    B_y1 = [Buf("y1_%d" % i) for i in range(2)]
                    it = 0
                    for t_ in range(8):
                        for ct in range(4):
                            s2 = it % 2
                            it += 1
                            bki, bapi = next_bank()
                            k.mm(bki, bapi[:, 0:NCH],
                                 [(Kw[:, t_ - s_, ct, :], uT[:, ct, s_:S:8]) for s_ in range(t_ + 1)],
                                 reads=[B_Kw, B_u])
                            bke, bape = next_bank()
                            k.mm_multi(bke, [(bape[32 * ql:32 * ql + 32, 0:NCH],
                                              [(WI[:, ct * 4 + ql, t_, 0, :], Sb[:, ct * 4 + ql, 0, :]),
                                               (WI[:, ct * 4 + ql, t_, 1, :], Sb[:, ct * 4 + ql, 1, :])],
                                              (0, 32 * ql)) for ql in range(4)], reads=[B_WI, B_Sb])
                            k.op(ACT, lambda e: e.copy(out=isb[s2][:], in_=bape[:, 0:NCH]), reads=[bke], writes=[B_isb[s2]])
                            k.op(DVE, lambda e: e.scalar_tensor_tensor(out=y1[s2][:], in0=uT[:, ct, t_:S:8],
                                                                       scalar=dsk[:, ct:ct + 1], in1=bapi[:, 0:NCH],
                                                                       op0=ALU.mult, op1=ALU.add),
                                 reads=[B_u, B_dsk, bki], writes=[B_y1[s2]])
                            k.op(POOL, lambda e: e.tensor_tensor(out=y1[s2][:], in0=y1[s2][:], in1=isb[s2][:], op=ALU.add),
                                 reads=[B_y1[s2], B_isb[s2]], writes=[B_y1[s2]])
                            k.op(ACT, lambda e: e.activation(out=zT[:, ct, t_:S:8], in_=y1[s2][:], func=AF.Gelu_apprx_tanh),
                                 reads=[B_y1[s2]], writes=[B_z])
                    sg = [sb(pq, "sg%d" % i, [128, NT], F32) for i in range(2)]
                    B_sg = [Buf("sg%d" % i) for i in range(2)]
                    og = [sb(pq, "og%d" % i, [128, NT], BF16) for i in range(2)]
                    B_og = [Buf("og%d" % i) for i in range(2)]
                    ds_og = [k.dsem("og") for _ in range(2)]
                    it = 0
                    for t in range(NTL):
                        tc = slice(t * NT, (t + 1) * NT)
                        for ct in range(4):
                            s2 = it % 2
                            it += 1
                            bk, bap = next_bank()
                            k.mm(bk, bap, [(wglu[:, kk, ct * 128:(ct + 1) * 128], zT[:, kk, tc]) for kk in range(4)],
                                 reads=[B_wglu, B_z])
                            k.op(ACT, lambda e: e.activation(out=sg[s2][:], in_=bap, func=AF.Sigmoid, bias=bglu[:, ct:ct + 1]),
                                 reads=[bk, B_bglu], writes=[B_sg[s2]])
                            k.op(DVE, lambda e: e.tensor_tensor(out=og[s2][:], in0=sg[s2][:], in1=zT[:, ct, tc], op=ALU.mult),
                                 reads=[B_sg[s2], B_z], writes=[B_og[s2]])
                            k.dma(SP, yssm_d[ct * 128:(ct + 1) * 128, tc], og[s2][:], ds_og[s2], reads=[B_og[s2]])
                    k.barrier()

        with ExitStack() as pd:
            wG = sb(pd, "wG", [128, KC, 2048], BF16)
            wPA = sb(pd, "wPA", [128, 4, D], BF16)
            wPB = sb(pd, "wPB", [128, 4, D], BF16)
            wO = sb(pd, "wO", [128, KC, D], BF16)
            WG = WChunks(wG, w_in[l], KC, 2056, 2048, 512, "wG")
            WPA = WChunks(wPA, w_pa[l], 4, 0, D, 512, "wPA")
            WPB = WChunks(wPB, w_pb[l], 4, 0, D, 512, "wPB")
            WO = WChunks(wO, w_o[l], KC, 0, D, 512, "wO")
            WG.load(0); WG.load(2); WPA.load(0); WPB.load(0)
            WG.load(1); WG.load(3); WPA.load(1); WPB.load(1)
            WO.load(0); WO.load(1)
            xts = [sb(pd, "xtD%d" % i, [128, KC * NT], F32) for i in range(2)]
            B_xt = [Buf("xtD%d" % i) for i in range(2)]
            ds_x = [k.dsem("xD") for _ in range(2)]
            ysa = [sb(pd, "ysa%d" % i, [128, 8, NT], BF16) for i in range(2)]
            B_ysa = [Buf("ysa%d" % i) for i in range(2)]
            hT = sb(pd, "hTD", [128, KC, NT], BF16); B_h = Buf("hTD")
            mg = sb(pd, "mg", [128, KC, NT], BF16); B_mg = Buf("mg")
            yt = sb(pd, "ytD", [128, KC * NT], F32); B_yt = Buf("ytD")
            sga = [sb(pd, "sga%d" % i, [128, NT], F32) for i in range(2)]
            sgb = [sb(pd, "sgb%d" % i, [128, NT], F32) for i in range(2)]
            B_sga = [Buf("sga%d" % i) for i in range(2)]
            B_sgb = [Buf("sgb%d" % i) for i in range(2)]
            nctx = NormCtx(pd, "D")
            nctx2 = NormCtx(pd, "D2")
            hTs = [hT, sb(pd, "hTD2", [128, KC, NT], BF16)]
            B_hs = [B_h, Buf("hTD2")]

            def loadD(t):
                s_ = t % 2
                k.dma(SP, xt3(xts[s_]), x_view(x_pre, t), ds_x[s_], writes=[B_xt[s_]])
                k.dma(SP, ysa[s_][:, 0:4, :], yssm_d.rearrange("(k p) n -> p k n", p=128)[:, :, t * NT:(t + 1) * NT],
                      ds_x[s_], writes=[B_ysa[s_]])
                k.dma(SP, ysa[s_][:, 4:8, :], yatt_d.rearrange("(k p) n -> p k n", p=128)[:, :, t * NT:(t + 1) * NT],
                      ds_x[s_], writes=[B_ysa[s_]])

            def postD(t):
                s_ = t % 2
                nctx2.residual(yt, B_yt, xts[s_], B_xt[s_], l, 2)
                k.dma(SP, x_view(x_mid, t), xt3(xts[s_]), ds_x[s_], reads=[B_xt[s_]])

            def mstepD(t, m):
                s = t % 2
                hT, B_h = hTs[s], B_hs[s]
                s2 = m % 2
                mc = slice(m * 128, (m + 1) * 128)
                bka, bapa = next_bank()
                k.mm(bka, bapa, [(wG[:, kk, mc], hT[:, kk, :]) for kk in range(KC)], reads=[WG.buf(m * 128), B_h])
                k.op(ACT, lambda e: e.activation(out=sga[s2][:], in_=bapa, func=AF.Sigmoid), reads=[bka], writes=[B_sga[s2]])
                bkb, bapb = next_bank()
                k.mm(bkb, bapb, [(wG[:, kk, 1024 + m * 128:1024 + (m + 1) * 128], hT[:, kk, :]) for kk in range(KC)],
                     reads=[WG.buf(1024 + m * 128), B_h])
                k.op(ACT, lambda e: e.activation(out=sgb[s2][:], in_=bapb, func=AF.Sigmoid), reads=[bkb], writes=[B_sgb[s2]])
                bkp, bapp = next_bank()
                k.mm(bkp, bapp, [(wPA[:, kk, mc], ysa[s][:, kk, :]) for kk in range(4)], reads=[WPA.buf(m * 128), B_ysa[s]])
                k.op(DVE, lambda e: e.tensor_tensor(out=sga[s2][:], in0=bapp, in1=sga[s2][:], op=ALU.mult),
                     reads=[bkp, B_sga[s2]], writes=[B_sga[s2]])
                bkq, bapq = next_bank()
                k.mm(bkq, bapq, [(wPB[:, kk, mc], ysa[s][:, 4 + kk, :]) for kk in range(4)], reads=[WPB.buf(m * 128), B_ysa[s]])
                k.op(DVE, lambda e: e.tensor_tensor(out=sgb[s2][:], in0=bapq, in1=sgb[s2][:], op=ALU.mult),
                     reads=[bkq, B_sgb[s2]], writes=[B_sgb[s2]])
                k.op(POOL, lambda e: e.tensor_tensor(out=mg[:, m, :], in0=sga[s2][:], in1=sgb[s2][:], op=ALU.add),
                     reads=[B_sga[s2], B_sgb[s2]], writes=[B_mg])

            def ostepD(t, m):
                mc = slice(m * 128, (m + 1) * 128)
                bk, bap = next_bank()
                k.mm(bk, bap, [(wO[:, kk, mc], mg[:, kk, :]) for kk in range(KC)], reads=[WO.buf(m * 128), B_mg])
                if m % 2 == 0:
                    k.op(ACT, lambda e: e.copy(out=yt[:, m * NT:(m + 1) * NT], in_=bap), reads=[bk], writes=[B_yt])
                else:
                    k.op(DVE, lambda e: e.tensor_copy(out=yt[:, m * NT:(m + 1) * NT], in_=bap), reads=[bk], writes=[B_yt])

            loadD(0)
            nctx.modulate(xts[0], B_xt[0], hTs[0], B_hs[0], l, 0, 1)
            for t in range(NTL):
                for m in range(KC):
                    mstepD(t, m)
                    if m == 1:
                        if t > 0:
                            postD(t - 1)
                        if t + 1 < NTL:
                            loadD(t + 1)
                for m in range(KC):
                    if m == 4 and t + 1 < NTL:
                        s1 = (t + 1) % 2
                        nctx.modulate(xts[s1], B_xt[s1], hTs[s1], B_hs[s1], l, 0, 1)
                    ostepD(t, m)
            postD(NTL - 1)
            k.barrier()

        with ExitStack() as pe1:
            wg = sb(pe1, "wg", [128, KC, DFF], BF16)
            wu = sb(pe1, "wu", [128, KC, DFF], BF16)
            WGt = WChunks(wg, w_g[l], KC, 0, DFF, 512, "wg")
            WUp = WChunks(wu, w_u[l], KC, 0, DFF, 512, "wu")
            for i_ in range(WGt.n):
                WGt.load(i_); WUp.load(i_)
            xts = [sb(pe1, "xtE%d" % i, [128, KC * NT], F32) for i in range(2)]
            B_xt = [Buf("xtE%d" % i) for i in range(2)]
            ds_x = [k.dsem("xE") for _ in range(2)]
            hT = sb(pe1, "hTE", [128, KC, NT], BF16); B_h = Buf("hTE")
            sl_ = [sb(pe1, "sl%d" % i, [128, NT], F32) for i in range(2)]
            B_sl = [Buf("sl%d" % i) for i in range(2)]
            ao = [sb(pe1, "ao%d" % i, [128, NT], BF16) for i in range(4)]
            B_ao = [Buf("ao%d" % i) for i in range(4)]
            ds_ao = [k.dsem("ao") for _ in range(4)]
            nctx = NormCtx(pe1, "E")
            hTs = [hT, sb(pe1, "hTE2", [128, KC, NT], BF16)]
            B_hs = [B_h, Buf("hTE2")]
            k.dma(SP, xt3(xts[0]), x_view(x_mid, 0), ds_x[0], writes=[B_xt[0]])
            nctx.modulate(xts[0], B_xt[0], hTs[0], B_hs[0], l, 3, 4)
            it = 0
            for t in range(NTL):
                s = t % 2
                hT, B_h = hTs[s], B_hs[s]
                if t + 1 < NTL:
                    k.dma(SP, xt3(xts[1 - s]), x_view(x_mid, t + 1), ds_x[1 - s], writes=[B_xt[1 - s]])
                for m in range(FC):
                    if m == 12 and t + 1 < NTL:
                        nctx.modulate(xts[1 - s], B_xt[1 - s], hTs[1 - s], B_hs[1 - s], l, 3, 4)
                    mc = slice(m * 128, (m + 1) * 128)
                    s2 = it % 2
                    s4 = it % 4
                    it += 1
                    bkg, bapg = next_bank()
                    k.mm(bkg, bapg, [(wg[:, kk, mc], hT[:, kk, :]) for kk in range(KC)], reads=[WGt.buf(m * 128), B_h])
                    k.op(ACT, lambda e: e.activation(out=sl_[s2][:], in_=bapg, func=AF.Silu), reads=[bkg], writes=[B_sl[s2]])
                    bku, bapu = next_bank()
                    k.mm(bku, bapu, [(wu[:, kk, mc], hT[:, kk, :]) for kk in range(KC)], reads=[WUp.buf(m * 128), B_h])
                    k.op(DVE, lambda e: e.tensor_tensor(out=ao[s4][:], in0=bapu, in1=sl_[s2][:], op=ALU.mult),
                         reads=[bku, B_sl[s2]], writes=[B_ao[s4]])
                    k.dma(SP, aT_d[mc, t * NT:(t + 1) * NT], ao[s4][:], ds_ao[s4], reads=[B_ao[s4]])
            k.barrier()

        with ExitStack() as pe2:
            wd = sb(pe2, "wd", [128, FC, D], BF16)
            WD = WChunks(wd, w_d[l], FC, 0, D, 256, "wd")
            for i_ in range(WD.n):
                WD.load(i_)
            xts = [sb(pe2, "xtF%d" % i, [128, KC * NT], F32) for i in range(2)]
            B_xt = [Buf("xtF%d" % i) for i in range(2)]
            ds_x = [k.dsem("xF") for _ in range(2)]
            at = [sb(pe2, "at%d" % i, [128, FC, NT], BF16) for i in range(2)]
            B_at = [Buf("at%d" % i) for i in range(2)]
            yt = sb(pe2, "ytF", [128, KC * NT], F32); B_yt = Buf("ytF")
            nctx = NormCtx(pe2, "F")

            ds_at = [k.dsem("at") for _ in range(2)]

            def load_at(t):
                s_ = t % 2
                av = aT_d.rearrange("(k p) n -> p k n", p=128)
                for a in range(0, FC, 11):
                    k.dma(SP, at[s_][:, a:a + 11, :], av[:, a:a + 11, t * NT:(t + 1) * NT], ds_at[s_], writes=[B_at[s_]])

            def load_x(t):
                s_ = t % 2
                k.dma(SP, xt3(xts[s_]), x_view(x_mid, t), ds_x[s_], writes=[B_xt[s_]])

            def postF(t):
                s_ = t % 2
                nctx.residual(yt, B_yt, xts[s_], B_xt[s_], l, 5)
                k.dma(SP, x_view(x_post, t), xt3(xts[s_]), ds_x[s_], reads=[B_xt[s_]])

            load_at(0)
            load_x(0)
            for t in range(NTL):
                s = t % 2
                if t + 1 < NTL:
                    load_at(t + 1)
                for m in range(KC):
                    mc = slice(m * 128, (m + 1) * 128)
                    bk, bap = next_bank()
                    k.mm(bk, bap, [(wd[:, kk, mc], at[s][:, kk, :]) for kk in range(FC)], reads=[WD.buf(m * 128), B_at[s]])
                    if m % 2 == 0:
                        k.op(ACT, lambda e: e.copy(out=yt[:, m * NT:(m + 1) * NT], in_=bap), reads=[bk], writes=[B_yt])
                    else:
                        k.op(DVE, lambda e: e.tensor_copy(out=yt[:, m * NT:(m + 1) * NT], in_=bap), reads=[bk], writes=[B_yt])
                    if m == 0 and t > 0:
                        pass
                if t + 1 < NTL:
                    pass
                postF(t)
                if t + 1 < NTL:
                    load_x(t + 1)
            k.barrier()

    k.final_wait()
    es.close()
    return nc


def prep_shared(inp):
    f = lambda a: np.ascontiguousarray(np.asarray(a, dtype=np.float32))
    sh = {}
    sh["w_ada"] = f(inp["w_ada"])
    sh["b_adaT"] = f(np.asarray(inp["b_ada"]).reshape(L, 48, 128).transpose(0, 2, 1))
    g = np.stack([np.asarray(inp[n]).reshape(L, KC, 128).transpose(0, 2, 1)
                  for n in ("g_pre_mix", "g_post_mix", "g_pre_ffn", "g_post_ffn")], axis=2)
    sh["gT"] = f(g)
    sh["w_in"] = f(inp["w_in"])

    def ep(a):
        a = np.asarray(a)
        rest = a.shape[3:]
        a = a.reshape((L, 16, 2, 64) + rest)
        perm = (0, 2, 3, 1) + tuple(range(4, 4 + len(rest)))
        a = a.transpose(perm)
        return a.reshape((L, 128, 16) + rest)

    lam_re = ep(np.asarray(inp["lam_re"]))
    lam_im = ep(np.asarray(inp["lam_im"]))
    sh["lamT"] = f(np.stack([lam_re, lam_im], axis=2))
    ldt = np.broadcast_to(np.asarray(inp["log_dt"])[:, :, None], (L, 32, 64))
    sh["ldtT"] = f(ep(ldt))
    b_re = ep(np.asarray(inp["b_re"]))
    b_im = ep(np.asarray(inp["b_im"]))
    sh["bT"] = f(np.stack([b_re, b_im], axis=2))
    c_re = ep(np.asarray(inp["c_re"]).transpose(0, 1, 3, 2))
    c_im = ep(np.asarray(inp["c_im"]).transpose(0, 1, 3, 2))
    sh["cTT"] = f(np.stack([c_re, c_im], axis=2))
    sh["dskT"] = f(np.asarray(inp["d_skip"]).reshape(L, 4, 128).transpose(0, 2, 1))
    sh["w_glu"] = f(inp["w_glu"])
    sh["b_gluT"] = f(np.asarray(inp["b_glu"]).reshape(L, 4, 128).transpose(0, 2, 1))
    sh["b_f"] = f(np.asarray(inp["b_f"]).reshape(L, 8, 1))
    sh["w_pa"] = f(inp["w_pa"]); sh["w_pb"] = f(inp["w_pb"]); sh["w_o"] = f(inp["w_o"])
    sh["w_g"] = f(inp["w_ffn_gate"]); sh["w_u"] = f(inp["w_ffn_up"]); sh["w_d"] = f(inp["w_ffn_down"])
    kk = np.arange(128)
    sh["tri"] = f((kk[None, :] >= kk[:, None]).astype(np.float32))
    sh["ident"] = f(np.eye(128, dtype=np.float32))
    sh["bdm"] = f((kk[:, None] // 16 == kk[None, :] // 16).astype(np.float32))
    return sh


_NC_CACHE = {}


def kernel(**inputs):
    x = np.asarray(inputs["x"], dtype=np.float32)
    c = np.asarray(inputs["c"], dtype=np.float32)
    B, S, _ = x.shape
    sh = prep_shared(inputs)
    in_maps = []
    for b in range(B):
        m = dict(sh)
        m["xT"] = np.ascontiguousarray(x[b].T)
        m["cT"] = np.ascontiguousarray(c[b].reshape(KC, 128).T)
        in_maps.append(m)
    if S not in _NC_CACHE:
        _NC_CACHE[S] = build_nc(S)
    nc = _NC_CACHE[S]
    res = run_bass_kernel_spmd(nc, in_maps, core_ids=list(range(B)))
    out = np.stack([np.ascontiguousarray(np.asarray(r["yT"]).T) for r in res.results], axis=0)
    return out.astype(np.float32)
```

```python
import math
from contextlib import ExitStack

import numpy as np
import concourse.bass as bass
import concourse.mybir as mybir
from concourse.bass_utils import run_bass_kernel_spmd

F32 = mybir.dt.float32
BF16 = mybir.dt.bfloat16
I32 = mybir.dt.int32
ALU = mybir.AluOpType
AF = mybir.ActivationFunctionType

D = 1024
KC = 8
L = 2
DFF = 2816
FC = 22
NIN = 4104
NT = 512
EPS = 1e-6
PI = math.pi


class Buf:
    __slots__ = ("name", "w", "r")

    def __init__(self, name):
        self.name = name
        self.w = {}
        self.r = {}


class DSem:
    def __init__(self, sem):
        self.sem = sem
        self.issued = 0


class Eng:
    def __init__(self, name, h, sem):
        self.name = name
        self.h = h
        self.sem = sem
        self.cnt = 0
        self.waited = {}


class K:
    def __init__(self, nc, es):
        self.nc = nc
        self.es = es
        self.nsem = 0
        self.pe = Eng("pe", nc.tensor, self._sem("pe"))
        self.act = Eng("act", nc.scalar, self._sem("act"))
        self.dve = Eng("dve", nc.vector, self._sem("dve"))
        self.pool = Eng("pool", nc.gpsimd, self._sem("pool"))
        self.sp = Eng("sp", nc.sync, None)
        self.engs = [self.pe, self.act, self.dve, self.pool, self.sp]
        self.dsems = []
        self._occ = {}
        self._dcache = {}

    def _sem(self, name):
        self.nsem += 1
        return self.es.enter_context(self.nc.semaphore("s_%s_%d" % (name, self.nsem)))

    def dsem(self, name="d"):
        occ = self._occ.get(name, 0)
        self._occ[name] = occ + 1
        key = (name, occ)
        if key not in self._dcache:
            d = DSem(self._sem(name))
            self.dsems.append(d)
            self._dcache[key] = d
        return self._dcache[key]

    def layer_begin(self):
        self._occ = {}

    def _wait(self, eng, sem, val, is_dma):
        if (not is_dma) and sem is eng.sem and eng is self.pe:
            return
        key = id(sem)
        if eng.waited.get(key, 0) >= val:
            return
        eng.h.wait_ge(sem, val)
        eng.waited[key] = val

    def _deps(self, eng, reads, writes):
        for b in reads:
            for (sem, val, ds) in b.w.values():
                self._wait(eng, sem, ds.issued if ds is not None else val, ds is not None)
        for b in writes:
            for (sem, val, ds) in b.w.values():
                self._wait(eng, sem, ds.issued if ds is not None else val, ds is not None)
            for (sem, val, ds) in b.r.values():
                self._wait(eng, sem, ds.issued if ds is not None else val, ds is not None)

    def _record(self, tok, reads, writes):
        key = id(tok[0])
        for b in reads:
            b.r[key] = tok
        for b in writes:
            b.w = {key: tok}
            b.r = {}

    def op(self, eng, fn, reads=(), writes=()):
        self._deps(eng, reads, writes)
        ins = fn(eng.h)
        eng.cnt += 1
        ins.then_inc(eng.sem, 1)
        self._record((eng.sem, eng.cnt, None), reads, writes)

    def mm(self, out_buf, out_ap, pairs, reads, tile_position=None, start=True, stop=True):
        eng = self.pe
        self._deps(eng, reads, [out_buf])
        n = len(pairs)
        ins = None
        for i, (lhsT, rhs) in enumerate(pairs):
            kw = {}
            if tile_position is not None:
                kw["tile_position"] = tile_position
            ins = eng.h.matmul(out_ap, lhsT=lhsT, rhs=rhs, start=(start and i == 0),
                               stop=(stop and i == n - 1), **kw)
        eng.cnt += 1
        ins.then_inc(eng.sem, 1)
        self._record((eng.sem, eng.cnt, None), reads, [out_buf])

    def mm_multi(self, out_buf, groups, reads):
        eng = self.pe
        self._deps(eng, reads, [out_buf])
        ins = None
        for (out_ap, pairs, tp) in groups:
            n = len(pairs)
            for i, (lhsT, rhs) in enumerate(pairs):
                kw = {}
                if tp is not None:
                    kw["tile_position"] = tp
                ins = eng.h.matmul(out_ap, lhsT=lhsT, rhs=rhs, start=(i == 0), stop=(i == n - 1), **kw)
        eng.cnt += 1
        ins.then_inc(eng.sem, 1)
        self._record((eng.sem, eng.cnt, None), reads, [out_buf])

    def dma(self, eng, out, in_, ds, reads=(), writes=()):
        self._deps(eng, reads, writes)
        eng.h.dma_start(out=out, in_=in_).then_inc(ds.sem, 16)
        ds.issued += 16
        self._record((ds.sem, ds.issued, ds), reads, writes)

    def barrier(self):
        for e in self.engs:
            for o in self.engs:
                if o.sem is not None and o is not e and o.cnt > 0:
                    self._wait(e, o.sem, o.cnt, False)
            for d in self.dsems:
                if d.issued > 0:
                    self._wait(e, d.sem, d.issued, True)

    def final_wait(self):
        self.barrier()


def build_nc(S, debug=False, nl=L):
    NTL = S // NT
    NCH = S // 8
    KT = S // 128
    LV = int(round(math.log2(NCH)))
    assert 2 ** LV == NCH and NCH <= 512

    nc = bass.Bass("TRN2", target_bir_lowering=False)
    es = ExitStack()
    k = K(nc, es)
    PE, ACT, DVE, POOL, SP = k.pe, k.act, k.dve, k.pool, k.sp

    def din(name, shape, dt=F32):
        return nc.dram_tensor(name, list(shape), dt, kind="ExternalInput").ap()

    okind = "ExternalOutput" if debug else "Internal"

    def dscr(name, shape, dt):
        return nc.dram_tensor(name, list(shape), dt, kind=okind).ap()

    xT_in = din("xT", [D, S])
    cT_in = din("cT", [128, KC])
    w_ada = din("w_ada", [L, D, 6 * D])
    b_adaT = din("b_adaT", [L, 128, 48])
    gT = din("gT", [L, 128, 4, KC])
    w_in = din("w_in", [L, D, NIN])
    lamT = din("lamT", [L, 128, 2, 16])
    ldtT = din("ldtT", [L, 128, 16])
    bT = din("bT", [L, 128, 2, 16, 16])
    cTT = din("cTT", [L, 128, 2, 16, 16])
    dskT = din("dskT", [L, 128, 4])
    w_glu = din("w_glu", [L, 512, 512])
    b_gluT = din("b_gluT", [L, 128, 4])
    b_f = din("b_f", [L, 8, 1])
    w_pa = din("w_pa", [L, 512, D])
    w_pb = din("w_pb", [L, 512, D])
    w_o = din("w_o", [L, D, D])
    w_g = din("w_g", [L, D, DFF])
    w_u = din("w_u", [L, D, DFF])
    w_d = din("w_d", [L, DFF, D])
    tri_in = din("tri", [128, 128])
    ident_in = din("ident", [128, 128])
    bdm_in = din("bdm", [128, 128])

    yT_out = nc.dram_tensor("yT", [D, S], F32, kind="ExternalOutput").ap()
    xa_d = dscr("xa_d", [D, S], F32)
    xb_d = dscr("xb_d", [D, S], F32)
    qT_d = dscr("qT_d", [512, S], BF16)
    kT_d = dscr("kT_d", [512, S], BF16)
    v_d = dscr("v_d", [S, 512], BF16)
    cumq_d = dscr("cumq_d", [8, 3, S], BF16)
    cumk_d = dscr("cumk_d", [8, 3, S], BF16)
    yssm_d = dscr("yssm_d", [512, S], BF16)
    yatt_d = dscr("yatt_d", [512, S], BF16)
    aT_d = dscr("aT_d", [DFF, S], BF16)
    Kw_d = dscr("Kw_d", [L, 128, 8 * 4 * 128], BF16)
    WE_d = dscr("WE_d", [L, 128, 8 * 2 * 4 * 128], BF16)
    WI_d = dscr("WI_d", [L, 128, 16 * 8 * 2 * 32], BF16)
    Ad_d = dscr("Ad_d", [L, 128, 3 * LV * 16], F32)

    uid = [0]

    def sb(stack, name, shape, dt):
        uid[0] += 1
        return stack.enter_context(nc.sbuf_tensor("sb%d_%s" % (uid[0], name), list(shape), dt))

    psum = es.enter_context(nc.psum_tensor("psum", [128, 8, 512], F32))
    banks = [Buf("bank%d" % i) for i in range(8)]
    bank_rr = [0]

    def next_bank(lo=0, hi=8):
        n = hi - lo
        i = lo + (bank_rr[0] % n)
        bank_rr[0] += 1
        return banks[i], psum[:, i, :]

    ones_bf = sb(es, "ones_bf", [128, 128], BF16); B_ones = Buf("ones")
    tri_bf = sb(es, "tri_bf", [128, 128], BF16); B_tri = Buf("tri")
    ident_bf = sb(es, "ident_bf", [128, 128], BF16); B_ident = Buf("ident")
    bdm_f = sb(es, "bdm_f", [128, 128], F32); B_bdm = Buf("bdm")
    epsb = sb(es, "epsb", [128, 1], F32); B_eps = Buf("eps")
    vecs = sb(es, "vecs", [128, L, 6, KC], F32); B_vecs = Buf("vecs")
    ds_const = k.dsem("const")
    ds_const_sw = k.dsem("constsw")

    k.op(DVE, lambda e: e.memset(ones_bf[:], 1.0), writes=[B_ones])
    k.op(DVE, lambda e: e.memset(epsb[:], float(D) * EPS), writes=[B_eps])
    k.dma(POOL, tri_bf[:], tri_in, ds_const_sw, writes=[B_tri])
    k.dma(POOL, ident_bf[:], ident_in, ds_const_sw, writes=[B_ident])
    k.dma(SP, bdm_f[:], bdm_in, ds_const, writes=[B_bdm])

    with ExitStack() as ps_:
        cT = sb(ps_, "cT", [128, KC], F32); B_c = Buf("c")
        cond = sb(ps_, "cond", [128, KC], F32); B_cond = Buf("cond")
        modT = sb(ps_, "modT", [128, L, 48], F32); B_mod = Buf("mod")
        badaS = sb(ps_, "badaS", [128, L, 48], F32); B_bada = Buf("bada")
        gS = sb(ps_, "gS", [128, L, 4, KC], F32); B_g = Buf("g")
        tmpv = sb(ps_, "tmpv", [128, KC], F32); B_tmpv = Buf("tmpv")
        wab = [sb(ps_, "wab%d" % i, [128, KC, 768], F32) for i in range(3)]
        B_wab = [Buf("wab%d" % i) for i in range(3)]
        ds_wab = [k.dsem("wab") for _ in range(3)]
        k.dma(SP, cT[:], cT_in, ds_const, writes=[B_c])
        for l in range(L):
            k.dma(SP, badaS[:, l, :], b_adaT[l], ds_const, writes=[B_bada])
            k.dma(SP, gS[:, l, :, :], gT[l], ds_const, writes=[B_g])
        k.op(ACT, lambda e: e.activation(out=cond[:], in_=cT[:], func=AF.Silu), reads=[B_c], writes=[B_cond])

        Kw = sb(ps_, "KwP", [128, 8, 4, 128], BF16); B_Kw = Buf("KwP")
        WE = sb(ps_, "WEP", [128, 8, 2, 4, 128], BF16); B_WE = Buf("WEP")
        WI = sb(ps_, "WIP", [128, 16, 8, 2, 32], BF16); B_WI = Buf("WIP")
        Ad3 = sb(ps_, "Ad3", [128, 3, LV, 16], F32); B_Ad = Buf("AdP")
        Adr, Adi, Adn = Ad3[:, 0, :, :], Ad3[:, 1, :, :], Ad3[:, 2, :, :]
        ds_p = k.dsem("s5p")
        ds_po = k.dsem("s5po")

        def prep_layer(l):
            lam = sb(ps_, "lam", [128, 2, 16], F32); B_lam = Buf("lam")
            ldt = sb(ps_, "ldt", [128, 16], F32); B_ldt = Buf("ldt")
            bS = sb(ps_, "bS", [128, 2, 16, 16], F32); B_bS = Buf("bS")
            cS = sb(ps_, "cS", [128, 2, 16, 16], F32); B_cS = Buf("cS")
            k.dma(SP, lam[:], lamT[l], ds_p, writes=[B_lam])
            k.dma(SP, ldt[:], ldtT[l], ds_p, writes=[B_ldt])
            k.dma(SP, bS[:], bT[l], ds_p, writes=[B_bS])
            k.dma(SP, cS[:], cTT[l], ds_p, writes=[B_cS])
            NS = 16
            sc = sb(ps_, "sc", [128, NS, 16], F32)
            sci = sb(ps_, "sci", [128, 16], I32)
            B_sc = Buf("sc")

            def V(fn, extra_r=(), extra_w=()):
                k.op(DVE, fn, reads=[B_sc] + list(extra_r), writes=[B_sc] + list(extra_w))

            def A_(fn, extra_r=()):
                k.op(ACT, fn, reads=[B_sc] + list(extra_r), writes=[B_sc])

            DT, LR, AR, TH, MAG, T1, T2, SN, CS, LBR, LBI, FR, FI, T3, T4, T5 = [sc[:, i, :] for i in range(NS)]
            LAMRE, LAMIM = lam[:, 0, :], lam[:, 1, :]
            A_(lambda e: e.activation(out=DT, in_=ldt[:], func=AF.Exp), [B_ldt])
            V(lambda e: e.tensor_scalar(out=LR, in0=LAMRE, scalar1=-1e-4, scalar2=None, op0=ALU.min), [B_lam])
            V(lambda e: e.tensor_tensor(out=AR, in0=LR, in1=DT, op=ALU.mult))
            V(lambda e: e.tensor_tensor(out=TH, in0=LAMIM, in1=DT, op=ALU.mult), [B_lam])
            A_(lambda e: e.activation(out=MAG, in_=AR, func=AF.Exp))

            def range_reduce(dst, shift):
                V(lambda e: e.tensor_scalar(out=T1, in0=TH, scalar1=shift, scalar2=1.0 / (2 * PI), op0=ALU.add,
                                            op1=ALU.mult))
                V(lambda e: e.tensor_copy(out=sci[:], in_=T1))
                V(lambda e: e.tensor_copy(out=T1, in_=sci[:]))
                V(lambda e: e.tensor_scalar(out=T2, in0=TH, scalar1=shift, scalar2=None, op0=ALU.add))
                V(lambda e: e.scalar_tensor_tensor(out=T2, in0=T1, scalar=-2 * PI, in1=T2, op0=ALU.mult, op1=ALU.add))
                V(lambda e: e.tensor_scalar(out=T1, in0=T2, scalar1=PI, scalar2=-2 * PI, op0=ALU.is_gt, op1=ALU.mult))
                V(lambda e: e.tensor_tensor(out=T2, in0=T2, in1=T1, op=ALU.add))
                V(lambda e: e.tensor_scalar(out=T1, in0=T2, scalar1=-PI, scalar2=2 * PI, op0=ALU.is_lt, op1=ALU.mult))
                V(lambda e: e.tensor_tensor(out=T2, in0=T2, in1=T1, op=ALU.add))
                V(lambda e: e.tensor_scalar(out=dst, in0=T2, scalar1=-3.141592, scalar2=3.141592, op0=ALU.max,
                                            op1=ALU.min))

            yield
            range_reduce(T3, 0.0)
            yield
            A_(lambda e: e.activation(out=SN, in_=T3, func=AF.Sin))
            range_reduce(T3, PI / 2)
            A_(lambda e: e.activation(out=CS, in_=T3, func=AF.Sin))
            V(lambda e: e.tensor_tensor(out=LBR, in0=MAG, in1=CS, op=ALU.mult))
            V(lambda e: e.tensor_tensor(out=LBI, in0=MAG, in1=SN, op=ALU.mult))
            V(lambda e: e.tensor_tensor(out=T1, in0=LR, in1=LR, op=ALU.mult))
            V(lambda e: e.tensor_tensor(out=T2, in0=LAMIM, in1=LAMIM, op=ALU.mult), [B_lam])
            V(lambda e: e.tensor_tensor(out=T1, in0=T1, in1=T2, op=ALU.add))
            V(lambda e: e.reciprocal(out=T1, in_=T1))
            V(lambda e: e.tensor_scalar(out=T2, in0=LBR, scalar1=-1.0, scalar2=None, op0=ALU.add))
            V(lambda e: e.tensor_tensor(out=T3, in0=T2, in1=LR, op=ALU.mult))
            V(lambda e: e.tensor_tensor(out=T4, in0=LBI, in1=LAMIM, op=ALU.mult), [B_lam])
            V(lambda e: e.tensor_tensor(out=T3, in0=T3, in1=T4, op=ALU.add))
            V(lambda e: e.tensor_tensor(out=FR, in0=T3, in1=T1, op=ALU.mult))
            V(lambda e: e.tensor_tensor(out=T3, in0=LBI, in1=LR, op=ALU.mult))
            V(lambda e: e.tensor_tensor(out=T4, in0=T2, in1=LAMIM, op=ALU.mult), [B_lam])
            V(lambda e: e.tensor_tensor(out=T3, in0=T3, in1=T4, op=ALU.subtract))
            V(lambda e: e.tensor_tensor(out=FI, in0=T3, in1=T1, op=ALU.mult))
            Pr = sb(ps_, "Pr", [128, 9, 16], F32)
            Pi_ = sb(ps_, "Pi", [128, 9, 16], F32)
            V(lambda e: e.memset(Pr[:, 0, :], 1.0))
            V(lambda e: e.memset(Pi_[:, 0, :], 0.0))

            def cmul(o_r, o_i, a_r, a_i, b_r, b_i):
                V(lambda e: e.tensor_tensor(out=T4, in0=a_r, in1=b_r, op=ALU.mult))
                V(lambda e: e.tensor_tensor(out=T5, in0=a_i, in1=b_i, op=ALU.mult))
                V(lambda e: e.tensor_tensor(out=T3, in0=a_r, in1=b_i, op=ALU.mult))
                V(lambda e: e.tensor_tensor(out=T1, in0=a_i, in1=b_r, op=ALU.mult))
                V(lambda e: e.tensor_tensor(out=o_r, in0=T4, in1=T5, op=ALU.subtract))
                V(lambda e: e.tensor_tensor(out=o_i, in0=T3, in1=T1, op=ALU.add))

            for tau in range(8):
                cmul(Pr[:, tau + 1, :], Pi_[:, tau + 1, :], Pr[:, tau, :], Pi_[:, tau, :], LBR, LBI)
                yield
            V(lambda e: e.tensor_copy(out=Adr[:, 0, :], in_=Pr[:, 8, :]), extra_w=[B_Ad])
            V(lambda e: e.tensor_copy(out=Adi[:, 0, :], in_=Pi_[:, 8, :]), extra_w=[B_Ad])
            for lv in range(LV - 1):
                cmul(Adr[:, lv + 1, :], Adi[:, lv + 1, :], Adr[:, lv, :], Adi[:, lv, :], Adr[:, lv, :], Adi[:, lv, :])
                yield
            V(lambda e: e.tensor_scalar(out=Adn[:], in0=Adi[:], scalar1=-1.0, scalar2=None, op0=ALU.mult),
              extra_w=[B_Ad])
            Bbr = sb(ps_, "Bbr", [128, 16, 16], F32)
            Bbi = sb(ps_, "Bbi", [128, 16, 16], F32)
            W1 = sb(ps_, "W1", [128, 16, 16], F32)
            W2 = sb(ps_, "W2", [128, 16, 16], F32)

            def bc(ap2):
                return ap2.unsqueeze(2).broadcast_to([ap2.shape[0], 16, 16])

            V(lambda e: e.tensor_tensor(out=W1[:], in0=bS[:, 0, :, :], in1=bc(FR), op=ALU.mult), [B_bS])
            V(lambda e: e.tensor_tensor(out=W2[:], in0=bS[:, 1, :, :], in1=bc(FI), op=ALU.mult), [B_bS])
            V(lambda e: e.tensor_tensor(out=Bbr[:], in0=W1[:], in1=W2[:], op=ALU.subtract))
            V(lambda e: e.tensor_tensor(out=W1[:], in0=bS[:, 1, :, :], in1=bc(FR), op=ALU.mult), [B_bS])
            V(lambda e: e.tensor_tensor(out=W2[:], in0=bS[:, 0, :, :], in1=bc(FI), op=ALU.mult), [B_bS])
            V(lambda e: e.tensor_tensor(out=Bbi[:], in0=W1[:], in1=W2[:], op=ALU.add))
            Xp = sb(ps_, "Xp", [128, 8, 2, 512], BF16)
            Yp = sb(ps_, "Yp", [128, 2, 512], BF16)
            V(lambda e: e.memset(Xp[:], 0.0))
            V(lambda e: e.memset(Yp[:], 0.0))
            V(lambda e: e.memset(WI[:], 0.0), extra_w=[B_WI])

            def slot(ap_cols, e_):
                return ap_cols.rearrange("p (q e h) -> p q e h", q=16, e=2)[:, :, e_, :]

            for tau in range(8):
                for e_ in range(2):
                    hs = slice(e_ * 64, (e_ + 1) * 64)
                    pr = bc(Pr[hs, tau, :]); pi = bc(Pi_[hs, tau, :])
                    V(lambda e: e.tensor_tensor(out=W1[hs], in0=Bbr[hs], in1=pr, op=ALU.mult))
                    V(lambda e: e.tensor_tensor(out=W2[hs], in0=Bbi[hs], in1=pi, op=ALU.mult))
                    V(lambda e: e.tensor_tensor(out=slot(Xp[hs, tau, 0, :], e_), in0=W1[hs], in1=W2[hs], op=ALU.subtract))
                    V(lambda e: e.tensor_tensor(out=W1[hs], in0=Bbr[hs], in1=pi, op=ALU.mult))
                    V(lambda e: e.tensor_tensor(out=W2[hs], in0=Bbi[hs], in1=pr, op=ALU.mult))
                    V(lambda e: e.tensor_tensor(out=slot(Xp[hs, tau, 1, :], e_), in0=W1[hs], in1=W2[hs], op=ALU.add))
                    yield
            for e_ in range(2):
                hs = slice(e_ * 64, (e_ + 1) * 64)
                V(lambda e: e.tensor_copy(out=slot(Yp[hs, 0, :], e_), in_=cS[hs, 0, :, :]), [B_cS])
                V(lambda e: e.tensor_scalar(out=slot(Yp[hs, 1, :], e_), in0=cS[hs, 1, :, :], scalar1=-1.0,
                                            scalar2=None, op0=ALU.mult), [B_cS])
                for t_ in range(8):
                    pr = bc(Pr[hs, t_ + 1, :]); pi = bc(Pi_[hs, t_ + 1, :])
                    V(lambda e: e.tensor_tensor(out=W1[hs], in0=cS[hs, 0, :, :], in1=pr, op=ALU.mult), [B_cS])
                    V(lambda e: e.tensor_tensor(out=W2[hs], in0=cS[hs, 1, :, :], in1=pi, op=ALU.mult), [B_cS])
                    V(lambda e: e.tensor_tensor(out=WI[hs, :, t_, 0, e_ * 16:(e_ + 1) * 16], in0=W1[hs], in1=W2[hs],
                                                op=ALU.subtract), extra_w=[B_WI])
                    V(lambda e: e.tensor_tensor(out=W1[hs], in0=cS[hs, 0, :, :], in1=pi, op=ALU.mult), [B_cS])
                    V(lambda e: e.tensor_tensor(out=W2[hs], in0=cS[hs, 1, :, :], in1=pr, op=ALU.mult), [B_cS])
                    V(lambda e: e.tensor_tensor(out=W1[hs], in0=W1[hs], in1=W2[hs], op=ALU.add))
                    V(lambda e: e.tensor_scalar(out=WI[hs, :, t_, 1, e_ * 16:(e_ + 1) * 16], in0=W1[hs], scalar1=-1.0,
                                                scalar2=None, op0=ALU.mult), extra_w=[B_WI])
                    yield
            k.dma(ACT, WI_d[l], WI[:].rearrange("p a b c d -> p (a b c d)"), ds_po, reads=[B_WI])
            k.dma(ACT, Ad_d[l], Ad3[:].rearrange("p a b c -> p (a b c)"), ds_po, reads=[B_Ad])
            yield "PE"
            for tau in range(8):
                bk, bap = next_bank(0, 6)
                k.mm_multi(bk, [(bap[:, ct * 128:(ct + 1) * 128],
                                 [(Xp[:, tau, 0, ct * 128:(ct + 1) * 128], Yp[:, 0, ct * 128:(ct + 1) * 128]),
                                  (Xp[:, tau, 1, ct * 128:(ct + 1) * 128], Yp[:, 1, ct * 128:(ct + 1) * 128])], None)
                                for ct in range(4)], reads=[B_sc])
                for ct in range(4):
                    k.op(DVE, lambda e: e.tensor_tensor(out=Kw[:, tau, ct, :], in0=bap[:, ct * 128:(ct + 1) * 128],
                                                        in1=bdm_f[:], op=ALU.mult),
                         reads=[bk, B_bdm], writes=[B_Kw])
            for s_ in range(8):
                for ri in range(2):
                    bk, bap = next_bank(0, 6)
                    k.mm_multi(bk, [(bap[:, ct * 128:(ct + 1) * 128],
                                     [(Xp[:, 7 - s_, ri, ct * 128:(ct + 1) * 128], ident_bf[:])], None)
                                    for ct in range(4)], reads=[B_sc, B_ident])
                    k.op(ACT, lambda e: e.copy(out=WE[:, s_, ri, :, :].rearrange("p c n -> p (c n)"), in_=bap),
                         reads=[bk], writes=[B_WE])
            yield
            k.dma(ACT, Kw_d[l], Kw[:].rearrange("p a b c -> p (a b c)"), ds_po, reads=[B_Kw])
            k.dma(ACT, WE_d[l], WE[:].rearrange("p a b c d -> p (a b c d)"), ds_po, reads=[B_WE])
            yield

        prep_gens = [prep_layer(l_) for l_ in range(nl)]
        prep_parked = []

        def prep_pull():
            while prep_gens:
                u = next(prep_gens[0], "END")
                if u == "PE":
                    prep_parked.append(prep_gens.pop(0))
                    return
                if u == "END":
                    prep_gens.pop(0)
                    continue
                return

        it = 0
        for l in range(L):
            bk, bap = banks[6 + l % 2], psum[:, 6 + l % 2, :]
            wv = w_ada[l].rearrange("(k p) n -> p k n", p=128)
            for jc in range(8):
                s = it % 3
                it += 1
                k.dma(SP, wab[s][:], wv[:, :, jc * 768:(jc + 1) * 768], ds_wab[s], writes=[B_wab[s]])
                for jj in range(6):
                    j = jc * 6 + jj
                    k.mm(bk, bap[:, j:j + 1],
                         [(wab[s][:, kk, jj * 128:(jj + 1) * 128], cond[:, kk:kk + 1]) for kk in range(KC)],
                         reads=[B_wab[s], B_cond])
                    prep_pull()
            k.op(DVE, lambda e: e.tensor_tensor(out=modT[:, l, :], in0=bap[:, 0:48], in1=badaS[:, l, :], op=ALU.add),
                 reads=[bk, B_bada], writes=[B_mod])
            for (o, isc, ig) in ((0, 1, 0), (3, 4, 2)):
                k.op(DVE, lambda e: e.tensor_scalar(out=tmpv[:], in0=modT[:, l, isc * 8:(isc + 1) * 8], scalar1=1.0,
                                                    scalar2=32.0, op0=ALU.add, op1=ALU.mult),
                     reads=[B_mod], writes=[B_tmpv])
                k.op(DVE, lambda e: e.tensor_tensor(out=vecs[:, l, o, :], in0=tmpv[:], in1=gS[:, l, ig, :], op=ALU.mult),
                     reads=[B_tmpv, B_g], writes=[B_vecs])
            for (o, ish) in ((1, 0), (4, 3)):
                k.op(DVE, lambda e: e.tensor_copy(out=vecs[:, l, o, :], in_=modT[:, l, ish * 8:(ish + 1) * 8]),
                     reads=[B_mod], writes=[B_vecs])
            for (o, iga, ig) in ((2, 2, 1), (5, 5, 3)):
                k.op(DVE, lambda e: e.tensor_scalar(out=tmpv[:], in0=modT[:, l, iga * 8:(iga + 1) * 8], scalar1=32.0,
                                                    scalar2=None, op0=ALU.mult),
                     reads=[B_mod], writes=[B_tmpv])
                k.op(DVE, lambda e: e.tensor_tensor(out=vecs[:, l, o, :], in0=tmpv[:], in1=gS[:, l, ig, :], op=ALU.mult),
                     reads=[B_tmpv, B_g], writes=[B_vecs])
        while prep_gens:
            prep_pull()
        for g_ in prep_parked:
            for _ in g_:
                pass
        k.barrier()

    def load_w(dst, src2d, kc, c0, ncols, dsm, buf, dcol0=0):
        v = src2d.rearrange("(k p) n -> p k n", p=128)
        step = 1024
        for a in range(0, ncols, step):
            n = min(step, ncols - a)
            k.dma(POOL, dst[:, 0:kc, dcol0 + a:dcol0 + a + n], v[:, :, c0 + a:c0 + a + n], dsm, writes=[buf])

    class WChunks:
        def __init__(self, tile, src2d, kc, c0, ncols, chunk, name):
            self.tile, self.src2d, self.kc, self.c0, self.chunk, self.name = tile, src2d, kc, c0, chunk, name
            self.n = (ncols + chunk - 1) // chunk
            self.ncols = ncols
            self.bufs = [Buf("%s_c%d" % (name, i)) for i in range(self.n)]
            self.ds = [k.dsem("%s_c" % name) for _ in range(self.n)]

        def load(self, i):
            a = i * self.chunk
            n = min(self.chunk, self.ncols - a)
            load_w(self.tile, self.src2d, self.kc, self.c0 + a, n, self.ds[i], self.bufs[i], dcol0=a)

        def buf(self, col):
            return self.bufs[col // self.chunk]

    class NormCtx:
        def __init__(self, stack, tag):
            self.sq = sb(stack, "sq" + tag, [128, KC * NT], BF16); self.B_sq = Buf("sq")
            self.rs = sb(stack, "rs" + tag, [128, NT], F32); self.B_rs = Buf("rs")
            self.tmp = [sb(stack, "ntmp%d%s" % (i, tag), [128, NT], F32) for i in range(2)]
            self.B_tmp = [Buf("ntmp%d" % i) for i in range(2)]
            self.i = 0

        def sq_part(self, src, B_src):
            k.op(ACT, lambda e: e.activation(out=self.sq[:], in_=src, func=AF.Square), reads=[B_src], writes=[self.B_sq])

        def rstd(self, src, B_src, do_sq=True):
            if do_sq:
                self.sq_part(src, B_src)
            bk, bap = next_bank()
            k.mm(bk, bap, [(ones_bf[:], self.sq[:, kk * NT:(kk + 1) * NT]) for kk in range(KC)],
                 reads=[B_ones, self.B_sq])
            k.op(ACT, lambda e: e.activation(out=self.rs[:], in_=bap, func=AF.Sqrt, bias=epsb[:, 0:1], scale=1.0),
                 reads=[bk, B_eps], writes=[self.B_rs])
            k.op(DVE, lambda e: e.reciprocal(out=self.rs[:], in_=self.rs[:]), reads=[self.B_rs], writes=[self.B_rs])

        def modulate(self, xt, B_x, hT, B_h, l, ia, ib, do_sq=True):
            self.rstd(xt[:], B_x, do_sq)
            for kk in range(KC):
                t = self.i % 2
                self.i += 1
                tm, Bt = self.tmp[t], self.B_tmp[t]
                k.op(DVE, lambda e: e.tensor_tensor(out=tm[:], in0=xt[:, kk * NT:(kk + 1) * NT], in1=self.rs[:], op=ALU.mult),
                     reads=[B_x, self.B_rs], writes=[Bt])
                k.op(ACT, lambda e: e.activation(out=hT[:, kk, :], in_=tm[:], func=AF.Identity,
                                                 scale=vecs[:, l, ia, kk:kk + 1], bias=vecs[:, l, ib, kk:kk + 1]),
                     reads=[Bt, B_vecs], writes=[B_h])

        def residual(self, yt, B_y, xt, B_x, l, ig, do_sq=True):
            self.rstd(yt[:], B_y, do_sq)
            for kk in range(KC):
                t = self.i % 2
                self.i += 1
                tm, Bt = self.tmp[t], self.B_tmp[t]
                k.op(DVE, lambda e: e.scalar_tensor_tensor(out=tm[:], in0=yt[:, kk * NT:(kk + 1) * NT],
                                                           scalar=vecs[:, l, ig, kk:kk + 1], in1=self.rs[:],
                                                           op0=ALU.mult, op1=ALU.mult),
                     reads=[B_y, self.B_rs, B_vecs], writes=[Bt])
                k.op(POOL, lambda e: e.tensor_tensor(out=xt[:, kk * NT:(kk + 1) * NT], in0=xt[:, kk * NT:(kk + 1) * NT],
                                                     in1=tm[:], op=ALU.add),
                     reads=[Bt, B_x], writes=[B_x])

    def x_view(xd, t):
        return xd.rearrange("(k p) n -> p k n", p=128)[:, :, t * NT:(t + 1) * NT]

    def xt3(xt):
        return xt[:].rearrange("p (k n) -> p k n", k=KC)

    for l in range(nl):
        k.layer_begin()
        x_pre = xT_in if l == 0 else xb_d
        x_mid = xa_d
        x_post = yT_out if l == nl - 1 else xb_d

        with ExitStack() as pm:
            uT = sb(pm, "uT", [128, 4, S], BF16); B_u = Buf("uT")
            with ExitStack() as pa0:
              fT = sb(pa0, "fT", [8, S], F32); B_f = Buf("fT")
              with ExitStack() as pa:
                wA = sb(pa, "wA", [128, KC, 2048], BF16)
                WA = WChunks(wA, w_in[l], KC, 0, 2048, 512, "wA")
                for i_ in range(WA.n):
                    WA.load(i_)
                wF = sb(pa, "wF", [128, KC, 32], BF16); B_wF = Buf("wF")
                wf32 = sb(pa, "wf32", [128, KC, 8], F32); B_wf32 = Buf("wf32"); ds_wf = k.dsem("wf")
                k.dma(SP, wf32[:], w_in[l].rearrange("(k p) n -> p k n", p=128)[:, :, 2048:2056], ds_wf, writes=[B_wf32])
                k.op(DVE, lambda e: e.memset(wF[:], 0.0), writes=[B_wF])
                k.op(DVE, lambda e: e.tensor_copy(out=wF[:, :, 0:8], in_=wf32[:]), reads=[B_wf32], writes=[B_wF])
                xts = [sb(pa, "xtA%d" % i, [128, KC * NT], F32) for i in range(2)]
                B_xt = [Buf("xtA%d" % i) for i in range(2)]
                ds_x = [k.dsem("xA") for _ in range(2)]
                hT = sb(pa, "hTA", [128, KC, NT], BF16); B_h = Buf("hTA")
                stg = [sb(pa, "stgA%d" % i, [128, NT], BF16) for i in range(4)]
                B_stg = [Buf("stgA%d" % i) for i in range(4)]
                ds_stg = [k.dsem("stgA") for _ in range(4)]
                nctx = NormCtx(pa, "A")
                sti = [0]

                def stage_out(bk, bap, dst, eng_i, scale=None):
                    s_ = sti[0] % 4
                    sti[0] += 1
                    if eng_i % 2 == 0:
                        if scale is None:
                            k.op(ACT, lambda e: e.copy(out=stg[s_][:], in_=bap), reads=[bk], writes=[B_stg[s_]])
                        else:
                            k.op(ACT, lambda e: e.activation(out=stg[s_][:], in_=bap, func=AF.Copy, scale=scale),
                                 reads=[bk], writes=[B_stg[s_]])
                    else:
                        if scale is None:
                            k.op(DVE, lambda e: e.tensor_copy(out=stg[s_][:], in_=bap), reads=[bk], writes=[B_stg[s_]])
                        else:
                            k.op(DVE, lambda e: e.tensor_scalar(out=stg[s_][:], in0=bap, scalar1=scale, scalar2=None,
                                                                op0=ALU.mult), reads=[bk], writes=[B_stg[s_]])
                    k.dma(SP, dst, stg[s_][:], ds_stg[s_], reads=[B_stg[s_]])

                hTs = [hT, sb(pa, "hTA2", [128, KC, NT], BF16)]
                B_hs = [B_h, Buf("hTA2")]
                k.dma(SP, xt3(xts[0]), x_view(x_pre, 0), ds_x[0], writes=[B_xt[0]])
                nctx.modulate(xts[0], B_xt[0], hTs[0], B_hs[0], l, 0, 1)
                for t in range(NTL):
                    s = t % 2
                    hT, B_h = hTs[s], B_hs[s]
                    if t + 1 < NTL:
                        k.dma(SP, xt3(xts[1 - s]), x_view(x_pre, t + 1), ds_x[1 - s], writes=[B_xt[1 - s]])
                    tc = slice(t * NT, (t + 1) * NT)
                    ei = 0
                    for m in range(12):
                        if m == 7 and t + 1 < NTL:
                            nctx.sq_part(xts[1 - s][:], B_xt[1 - s])
                        if m == 10 and t + 1 < NTL:
                            nctx.modulate(xts[1 - s], B_xt[1 - s], hTs[1 - s], B_hs[1 - s], l, 0, 1, do_sq=False)
                        bk, bap = next_bank()
                        k.mm(bk, bap, [(wA[:, kk, m * 128:(m + 1) * 128], hT[:, kk, :]) for kk in range(KC)],
                             reads=[WA.buf(m * 128), B_h])
                        if m < 4:
                            if m % 2 == 0:
                                k.op(ACT, lambda e: e.copy(out=uT[:, m, tc], in_=bap), reads=[bk], writes=[B_u])
                            else:
                                k.op(DVE, lambda e: e.tensor_copy(out=uT[:, m, tc], in_=bap), reads=[bk], writes=[B_u])
                        elif m < 8:
                            stage_out(bk, bap, qT_d[(m - 4) * 128:(m - 3) * 128, tc], ei, scale=0.125); ei += 1
                        else:
                            stage_out(bk, bap, kT_d[(m - 8) * 128:(m - 7) * 128, tc], ei); ei += 1
                    for sub in range(4):
                        bk, bap = next_bank()
                        k.mm(bk, bap, [(hT[:, kk, sub * 128:(sub + 1) * 128], wA[:, kk, 1536:2048]) for kk in range(KC)],
                             reads=[WA.buf(1536), B_h])
                        stage_out(bk, bap, v_d[t * NT + sub * 128:t * NT + (sub + 1) * 128, :], ei); ei += 1
                    bk, bap = next_bank()
                    k.mm(bk, bap[0:32, :], [(wF[:, kk, :], hT[:, kk, :]) for kk in range(KC)], reads=[B_wF, B_h])
                    k.op(DVE, lambda e: e.tensor_copy(out=fT[:, tc], in_=bap[0:8, :]), reads=[bk], writes=[B_f])
                k.barrier()
              with ExitStack() as pa:
                bfS = sb(pa, "bfS", [8, 1], F32); B_bf = Buf("bf")
                onesS = sb(pa, "onesS", [8, S], F32); B_on = Buf("onesS")
                l1 = sb(pa, "l1", [8, S], F32); B_l1 = Buf("l1")
                ncum = sb(pa, "ncum", [8, S], F32); B_nc = Buf("ncum")
                cb = sb(pa, "cb", [8, 3, S], BF16); B_cb = Buf("cb")
                cbn = sb(pa, "cbn", [8, 3, S], BF16); B_cbn = Buf("cbn")
                ds_c = k.dsem("cum")
                k.dma(SP, bfS[:], b_f[l], ds_c, writes=[B_bf])
                k.op(DVE, lambda e: e.tensor_scalar(out=bfS[:], in0=bfS[:], scalar1=-1.0, scalar2=None, op0=ALU.mult),
                     reads=[B_bf], writes=[B_bf])
                k.op(DVE, lambda e: e.memset(onesS[:], 1.0), writes=[B_on])
                k.op(ACT, lambda e: e.activation(out=l1[:], in_=fT[:], func=AF.Exp, scale=-1.0, bias=bfS[:, 0:1]),
                     reads=[B_f, B_bf], writes=[B_l1])
                k.op(ACT, lambda e: e.activation(out=l1[:], in_=l1[:], func=AF.Ln, bias=1.0), reads=[B_l1], writes=[B_l1])
                k.op(DVE, lambda e: e.tensor_tensor_scan(out=ncum[:], data0=onesS[:], data1=l1[:], initial=0.0,
                                                         op0=ALU.mult, op1=ALU.add),
                     reads=[B_on, B_l1], writes=[B_nc])
                for j in range(3):
                    k.op(DVE, lambda e: e.tensor_copy(out=cb[:, j, :], in_=ncum[:]), reads=[B_nc], writes=[B_cb])
                    if j < 2:
                        k.op(DVE, lambda e: e.tensor_tensor(out=ncum[:], in0=ncum[:], in1=cb[:, j, :], op=ALU.subtract),
                             reads=[B_nc, B_cb], writes=[B_nc])
                k.op(DVE, lambda e: e.tensor_scalar(out=cbn[:], in0=cb[:], scalar1=-1.0, scalar2=None, op0=ALU.mult),
                     reads=[B_cb], writes=[B_cbn])
                k.dma(SP, cumk_d.rearrange("h j s -> h (j s)"), cb[:].rearrange("h j s -> h (j s)"), ds_c, reads=[B_cb])
                k.dma(SP, cumq_d.rearrange("h j s -> h (j s)"), cbn[:].rearrange("h j s -> h (j s)"), ds_c, reads=[B_cbn])
                k.barrier()

            with ExitStack() as pb:
                Kw = sb(pb, "Kw", [128, 8, 4, 128], BF16); B_Kw = Buf("Kw")
                WE = sb(pb, "WE", [128, 8, 2, 4, 128], BF16); B_WE = Buf("WE")
                WI = sb(pb, "WI", [128, 16, 8, 2, 32], BF16); B_WI = Buf("WI")
                Sb = sb(pb, "Sb", [128, 16, 2, NCH], BF16); B_Sb = Buf("Sb")
                Ad3 = sb(pb, "Ad3", [128, 3, LV, 16], F32); B_Ad = Buf("Ad")
                Adr, Adi, Adn = Ad3[:, 0, :, :], Ad3[:, 1, :, :], Ad3[:, 2, :, :]
                dsk = sb(pb, "dsk", [128, 4], F32); B_dsk = Buf("dsk")
                bglu = sb(pb, "bglu", [128, 4], F32); B_bglu = Buf("bglu")
                wglu = sb(pb, "wglu", [128, 4, 512], BF16); B_wglu = Buf("wglu"); ds_wglu = k.dsem("wglu")
                ds_p = k.dsem("s5p")
                load_w(wglu, w_glu[l], 4, 0, 512, ds_wglu, B_wglu)
                k.dma(SP, dsk[:], dskT[l], ds_p, writes=[B_dsk])
                k.dma(SP, bglu[:], b_gluT[l], ds_p, writes=[B_bglu])

                k.dma(SP, Kw[:].rearrange("p a b c -> p (a b c)"), Kw_d[l], ds_p, writes=[B_Kw])
                k.dma(SP, WE[:].rearrange("p a b c d -> p (a b c d)"), WE_d[l], ds_p, writes=[B_WE])
                k.dma(SP, WI[:].rearrange("p a b c d -> p (a b c d)"), WI_d[l], ds_p, writes=[B_WI])
                k.dma(SP, Ad3[:].rearrange("p a b c -> p (a b c)"), Ad_d[l], ds_p, writes=[B_Ad])

                with ExitStack() as pc:
                    NSL = 2
                    stt_ = [[sb(pc, "st%d_%d" % (sl, i), [128, NCH], F32) for i in range(4)] for sl in range(NSL)]
                    B_st = [[Buf("st%d_%d" % (sl, i)) for i in range(4)] for sl in range(NSL)]
                    k.op(DVE, lambda e: e.memset(Sb[:], 0.0), writes=[B_Sb])

                    def scan_units():
                        for q in range(16):
                            ct, ql = q // 4, q % 4
                            sl = q % NSL
                            P_ = (stt_[sl][0], stt_[sl][1]); Q_ = (stt_[sl][2], stt_[sl][3])
                            BP = (B_st[sl][0], B_st[sl][1]); BQ = (B_st[sl][2], B_st[sl][3])
                            rows = slice(32 * ql, 32 * ql + 32)
                            for ri in range(2):
                                bk, bap = next_bank(6, 8)
                                k.mm(bk, bap[:, 0:NCH],
                                     [(WE[rows, s_, ri, ct, :], uT[rows, ct, s_:S:8]) for s_ in range(8)],
                                     reads=[B_WE, B_u], tile_position=(32 * ql, 0))
                                k.op(ACT, lambda e: e.copy(out=P_[ri][:], in_=bap[:, 0:NCH]), reads=[bk], writes=[BP[ri]])
                            yield
                            src, dst, Bs, Bd = P_, Q_, BP, BQ
                            for lv in range(LV):
                                d = 1 << lv
                                n = NCH - d
                                ar = Adr[:, lv, q:q + 1]; ai = Adi[:, lv, q:q + 1]; an = Adn[:, lv, q:q + 1]
                                lo_ = d // 2
                                for ri in range(2):
                                    k.op(DVE, lambda e: e.tensor_copy(out=dst[ri][:, lo_:d], in_=src[ri][:, lo_:d]),
                                         reads=[Bs[ri]], writes=[Bd[ri]])
                                k.op(DVE, lambda e: e.scalar_tensor_tensor(out=dst[0][:, d:NCH], in0=src[0][:, 0:n], scalar=ar,
                                                                           in1=src[0][:, d:NCH], op0=ALU.mult, op1=ALU.add),
                                     reads=[Bs[0], B_Ad], writes=[Bd[0]])
                                yield
                                k.op(DVE, lambda e: e.scalar_tensor_tensor(out=dst[0][:, d:NCH], in0=src[1][:, 0:n], scalar=an,
                                                                           in1=dst[0][:, d:NCH], op0=ALU.mult, op1=ALU.add),
                                     reads=[Bs[1], B_Ad, Bd[0]], writes=[Bd[0]])
                                yield
                                k.op(DVE, lambda e: e.scalar_tensor_tensor(out=dst[1][:, d:NCH], in0=src[1][:, 0:n], scalar=ar,
                                                                           in1=src[1][:, d:NCH], op0=ALU.mult, op1=ALU.add),
                                     reads=[Bs[1], B_Ad], writes=[Bd[1]])
                                yield
                                k.op(DVE, lambda e: e.scalar_tensor_tensor(out=dst[1][:, d:NCH], in0=src[0][:, 0:n], scalar=ai,
                                                                           in1=dst[1][:, d:NCH], op0=ALU.mult, op1=ALU.add),
                                     reads=[Bs[0], B_Ad, Bd[1]], writes=[Bd[1]])
                                yield
                                src, dst, Bs, Bd = dst, src, Bd, Bs
                            for ri in range(2):
                                k.op(POOL, lambda e: e.tensor_copy(out=Sb[:, q, ri, 1:NCH], in_=src[ri][:, 0:NCH - 1]),
                                     reads=[Bs[ri]], writes=[B_Sb])
                            yield

                    scan_gen = scan_units()
                    qa = [sb(pc, "qa%d" % i, [128, S], BF16) for i in range(2)]
                    ka = [sb(pc, "ka%d" % i, [128, S], BF16) for i in range(2)]
                    va = [sb(pc, "va%d" % i, [128, KT, 128], BF16) for i in range(2)]
                    B_qa = [Buf("qa%d" % i) for i in range(2)]
                    B_ka = [Buf("ka%d" % i) for i in range(2)]
                    B_va = [Buf("va%d" % i) for i in range(2)]
                    ds_qkv = [k.dsem("qkv") for _ in range(2)]
                    pT = [sb(pc, "pT%d" % i, [128, NT], BF16) for i in range(4)]
                    B_pT = [Buf("pT%d" % i) for i in range(4)]
                    rden = [sb(pc, "rden%d" % i, [128, NT], F32) for i in range(2)]
                    B_rden = [Buf("rden%d" % i) for i in range(2)]
                    yst = [sb(pc, "yst%d" % i, [128, NT], BF16) for i in range(2)]
                    B_yst = [Buf("yst%d" % i) for i in range(2)]
                    ds_yst = [k.dsem("yst") for _ in range(2)]
                    for i in range(2):
                        k.op(POOL, lambda e: e.memset(qa[i][:], 0.0), writes=[B_qa[i]])
                        k.op(POOL, lambda e: e.memset(ka[i][:], 0.0), writes=[B_ka[i]])
                        k.op(POOL, lambda e: e.memset(qa[i][64:70, :], 1.0), writes=[B_qa[i]])
                        k.op(POOL, lambda e: e.memset(ka[i][64:70, :], 1.0), writes=[B_ka[i]])
                    k.op(POOL, lambda e: e.memset(va[0][:, :, 64:128], 1.0), writes=[B_va[0]])
                    k.op(POOL, lambda e: e.memset(va[1][:, :, 0:64], 1.0), writes=[B_va[1]])

                    def load_head(h):
                        s_ = h % 2
                        hr = slice(h * 64, (h + 1) * 64)
                        k.dma(SP, qa[s_][0:64, :], qT_d[hr, :], ds_qkv[s_], writes=[B_qa[s_]])
                        k.dma(SP, qa[s_][67:70, :], cumq_d[h], ds_qkv[s_], writes=[B_qa[s_]])
                        k.dma(SP, ka[s_][0:64, :], kT_d[hr, :], ds_qkv[s_], writes=[B_ka[s_]])
                        k.dma(SP, ka[s_][64:67, :], cumk_d[h], ds_qkv[s_], writes=[B_ka[s_]])
                        vv = v_d.rearrange("(kt p) c -> p kt c", p=128)
                        co = 0 if s_ == 0 else 64
                        for a in range(0, KT, 8):
                            k.dma(SP, va[s_][:, a:a + 8, co:co + 64], vv[:, a:a + 8, hr], ds_qkv[s_], writes=[B_va[s_]])

                    items = []
                    for h in range(8):
                        for j in range(NTL):
                            for i in range(4 * j + 4):
                                items.append((h, j, i))
                    SB_LO, SB_HI = 0, 4
                    obanks = [(banks[4], psum[:, 4, :]), (banks[5], psum[:, 5, :])]
                    pend = []
                    load_head(0)
                    oi = 0
                    for n in range(len(items) + 2):
                        if n < len(items):
                            h, j, i = items[n]
                            s_ = h % 2
                            if j == 0 and i == 2 and h + 1 < 8:
                                load_head(h + 1)
                            r = i - 4 * j
                            c0 = 128 * r if r > 0 else 0
                            bk, bap = next_bank(SB_LO, SB_HI)
                            pi_ = n % 4
                            k.mm(bk, bap[:, c0:NT], [(ka[s_][:, i * 128:(i + 1) * 128], qa[s_][:, j * NT + c0:(j + 1) * NT])],
                                 reads=[B_ka[s_], B_qa[s_]])
                            k.op(ACT, lambda e: e.activation(out=pT[pi_][:, c0:NT], in_=bap[:, c0:NT], func=AF.Exp),
                                 reads=[bk], writes=[B_pT[pi_]])
                            if r >= 0:
                                k.op(POOL, lambda e: e.tensor_tensor(out=pT[pi_][:, c0:c0 + 128], in0=pT[pi_][:, c0:c0 + 128],
                                                                     in1=tri_bf[:], op=ALU.mult),
                                     reads=[B_pT[pi_], B_tri], writes=[B_pT[pi_]])
                            pend.append((h, j, i, c0, pi_))
                            if (n % 8) != 7:
                                next(scan_gen, None)
                        if n >= 2:
                            h, j, i, c0, pi_ = pend[n - 2]
                            s_ = h % 2
                            last = (i == 4 * j + 3)
                            if i == 0:
                                oi += 1
                            ob, oap = obanks[oi % 2]
                            k.mm(ob, oap[:, c0:NT], [(va[s_][:, i, :], pT[pi_][:, c0:NT])], reads=[B_va[s_], B_pT[pi_]],
                                 start=(i == 0), stop=last)
                            if last:
                                e2 = oi % 2
                                orow = slice(0, 64) if s_ == 0 else slice(64, 128)
                                drow = slice(64, 128) if s_ == 0 else slice(0, 64)
                                k.op(DVE, lambda e: e.reciprocal(out=rden[e2][orow, :], in_=oap[drow, :]), reads=[ob],
                                     writes=[B_rden[e2]])
                                k.op(DVE, lambda e: e.tensor_tensor(out=yst[e2][orow, :], in0=oap[orow, :], in1=rden[e2][orow, :],
                                                                    op=ALU.mult),
                                     reads=[ob, B_rden[e2]], writes=[B_yst[e2]])
                                k.dma(SP, yatt_d[h * 64:(h + 1) * 64, j * NT:(j + 1) * NT], yst[e2][orow, :], ds_yst[e2],
                                      reads=[B_yst[e2]])
                    for _ in scan_gen:
                        pass
                    k.barrier()

                with ExitStack() as pq:
                    zT = sb(pq, "zT", [128, 4, S], BF16); B_z = Buf("zT")
                    isb = [sb(pq, "isb%d" % i, [128, NCH], F32) for i in range(2)]
                    B_isb = [Buf("isb%d" % i) for i in range(2)]
                    y1 = [sb(pq, "y1_%d" % i, [128, NCH], F32) for i in range(2)]
                    B_y1 = [Buf("y1_%d" % i) for i in range(2)]
                    it = 0
                    for t_ in range(8):
                        for ct in range(4):
                            s2 = it % 2
                            it += 1
                            bki, bapi = next_bank()
                            k.mm(bki, bapi[:, 0:NCH],
                                 [(Kw[:, t_ - s_, ct, :], uT[:, ct, s_:S:8]) for s_ in range(t_ + 1)],
                                 reads=[B_Kw, B_u])
                            bke, bape = next_bank()
                            k.mm_multi(bke, [(bape[32 * ql:32 * ql + 32, 0:NCH],
                                              [(WI[:, ct * 4 + ql, t_, 0, :], Sb[:, ct * 4 + ql, 0, :]),
                                               (WI[:, ct * 4 + ql, t_, 1, :], Sb[:, ct * 4 + ql, 1, :])],
                                              (0, 32 * ql)) for ql in range(4)], reads=[B_WI, B_Sb])
                            k.op(ACT, lambda e: e.copy(out=isb[s2][:], in_=bape[:, 0:NCH]), reads=[bke], writes=[B_isb[s2]])
                            k.op(DVE, lambda e: e.scalar_tensor_tensor(out=y1[s2][:], in0=uT[:, ct, t_:S:8],
                                                                       scalar=dsk[:, ct:ct + 1], in1=bapi[:, 0:NCH],
                                                                       op0=ALU.mult, op1=ALU.add),
                                 reads=[B_u, B_dsk, bki], writes=[B_y1[s2]])
                            k.op(POOL, lambda e: e.tensor_tensor(out=y1[s2][:], in0=y1[s2][:], in1=isb[s2][:], op=ALU.add),
                                 reads=[B_y1[s2], B_isb[s2]], writes=[B_y1[s2]])
                            k.op(ACT, lambda e: e.activation(out=zT[:, ct, t_:S:8], in_=y1[s2][:], func=AF.Gelu_apprx_tanh),
                                 reads=[B_y1[s2]], writes=[B_z])
                    sg = [sb(pq, "sg%d" % i, [128, NT], F32) for i in range(2)]
                    B_sg = [Buf("sg%d" % i) for i in range(2)]
                    og = [sb(pq, "og%d" % i, [128, NT], BF16) for i in range(2)]
                    B_og = [Buf("og%d" % i) for i in range(2)]
                    ds_og = [k.dsem("og") for _ in range(2)]
                    it = 0
                    for t in range(NTL):
                        tc = slice(t * NT, (t + 1) * NT)
                        for ct in range(4):
                            s2 = it % 2
                            it += 1
                            bk, bap = next_bank()
                            k.mm(bk, bap, [(wglu[:, kk, ct * 128:(ct + 1) * 128], zT[:, kk, tc]) for kk in range(4)],
                                 reads=[B_wglu, B_z])
                            k.op(ACT, lambda e: e.activation(out=sg[s2][:], in_=bap, func=AF.Sigmoid, bias=bglu[:, ct:ct + 1]),
                                 reads=[bk, B_bglu], writes=[B_sg[s2]])
                            k.op(DVE, lambda e: e.tensor_tensor(out=og[s2][:], in0=sg[s2][:], in1=zT[:, ct, tc], op=ALU.mult),
                                 reads=[B_sg[s2], B_z], writes=[B_og[s2]])
                            k.dma(SP, yssm_d[ct * 128:(ct + 1) * 128, tc], og[s2][:], ds_og[s2], reads=[B_og[s2]])
                    k.barrier()

        with ExitStack() as pd:
            wG = sb(pd, "wG", [128, KC, 2048], BF16)
            wPA = sb(pd, "wPA", [128, 4, D], BF16)
            wPB = sb(pd, "wPB", [128, 4, D], BF16)
            wO = sb(pd, "wO", [128, KC, D], BF16)
            WG = WChunks(wG, w_in[l], KC, 2056, 2048, 512, "wG")
            WPA = WChunks(wPA, w_pa[l], 4, 0, D, 512, "wPA")
            WPB = WChunks(wPB, w_pb[l], 4, 0, D, 512, "wPB")
            WO = WChunks(wO, w_o[l], KC, 0, D, 512, "wO")
            WG.load(0); WG.load(2); WPA.load(0); WPB.load(0)
            WG.load(1); WG.load(3); WPA.load(1); WPB.load(1)
            WO.load(0); WO.load(1)
            xts = [sb(pd, "xtD%d" % i, [128, KC * NT], F32) for i in range(2)]
            B_xt = [Buf("xtD%d" % i) for i in range(2)]
            ds_x = [k.dsem("xD") for _ in range(2)]
            ysa = [sb(pd, "ysa%d" % i, [128, 8, NT], BF16) for i in range(2)]
            B_ysa = [Buf("ysa%d" % i) for i in range(2)]
            hT = sb(pd, "hTD", [128, KC, NT], BF16); B_h = Buf("hTD")
            mg = sb(pd, "mg", [128, KC, NT], BF16); B_mg = Buf("mg")
            yt = sb(pd, "ytD", [128, KC * NT], F32); B_yt = Buf("ytD")
            sga = [sb(pd, "sga%d" % i, [128, NT], F32) for i in range(2)]
            sgb = [sb(pd, "sgb%d" % i, [128, NT], F32) for i in range(2)]
            B_sga = [Buf("sga%d" % i) for i in range(2)]
            B_sgb = [Buf("sgb%d" % i) for i in range(2)]
            nctx = NormCtx(pd, "D")
            nctx2 = NormCtx(pd, "D2")
            hTs = [hT, sb(pd, "hTD2", [128, KC, NT], BF16)]
            B_hs = [B_h, Buf("hTD2")]

            def loadD(t):
                s_ = t % 2
                k.dma(SP, xt3(xts[s_]), x_view(x_pre, t), ds_x[s_], writes=[B_xt[s_]])
                k.dma(SP, ysa[s_][:, 0:4, :], yssm_d.rearrange("(k p) n -> p k n", p=128)[:, :, t * NT:(t + 1) * NT],
                      ds_x[s_], writes=[B_ysa[s_]])
                k.dma(SP, ysa[s_][:, 4:8, :], yatt_d.rearrange("(k p) n -> p k n", p=128)[:, :, t * NT:(t + 1) * NT],
                      ds_x[s_], writes=[B_ysa[s_]])

            def postD(t):
                s_ = t % 2
                nctx2.residual(yt, B_yt, xts[s_], B_xt[s_], l, 2, do_sq=False)
                k.dma(SP, x_view(x_mid, t), xt3(xts[s_]), ds_x[s_], reads=[B_xt[s_]])

            def mstepD(t, m):
                s = t % 2
                hT, B_h = hTs[s], B_hs[s]
                s2 = m % 2
                mc = slice(m * 128, (m + 1) * 128)
                bka, bapa = next_bank()
                k.mm(bka, bapa, [(wG[:, kk, mc], hT[:, kk, :]) for kk in range(KC)], reads=[WG.buf(m * 128), B_h])
                k.op(ACT, lambda e: e.activation(out=sga[s2][:], in_=bapa, func=AF.Sigmoid), reads=[bka], writes=[B_sga[s2]])
                bkb, bapb = next_bank()
                k.mm(bkb, bapb, [(wG[:, kk, 1024 + m * 128:1024 + (m + 1) * 128], hT[:, kk, :]) for kk in range(KC)],
                     reads=[WG.buf(1024 + m * 128), B_h])
                k.op(ACT, lambda e: e.activation(out=sgb[s2][:], in_=bapb, func=AF.Sigmoid), reads=[bkb], writes=[B_sgb[s2]])
                bkp, bapp = next_bank()
                k.mm(bkp, bapp, [(wPA[:, kk, mc], ysa[s][:, kk, :]) for kk in range(4)], reads=[WPA.buf(m * 128), B_ysa[s]])
                k.op(DVE, lambda e: e.tensor_tensor(out=sga[s2][:], in0=bapp, in1=sga[s2][:], op=ALU.mult),
                     reads=[bkp, B_sga[s2]], writes=[B_sga[s2]])
                bkq, bapq = next_bank()
                k.mm(bkq, bapq, [(wPB[:, kk, mc], ysa[s][:, 4 + kk, :]) for kk in range(4)], reads=[WPB.buf(m * 128), B_ysa[s]])
                k.op(DVE, lambda e: e.tensor_tensor(out=sgb[s2][:], in0=bapq, in1=sgb[s2][:], op=ALU.mult),
                     reads=[bkq, B_sgb[s2]], writes=[B_sgb[s2]])
                k.op(POOL, lambda e: e.tensor_tensor(out=mg[:, m, :], in0=sga[s2][:], in1=sgb[s2][:], op=ALU.add),
                     reads=[B_sga[s2], B_sgb[s2]], writes=[B_mg])

            def ostepD(t, m):
                mc = slice(m * 128, (m + 1) * 128)
                bk, bap = next_bank()
                k.mm(bk, bap, [(wO[:, kk, mc], mg[:, kk, :]) for kk in range(KC)], reads=[WO.buf(m * 128), B_mg])
                if m % 2 == 0:
                    k.op(ACT, lambda e: e.copy(out=yt[:, m * NT:(m + 1) * NT], in_=bap), reads=[bk], writes=[B_yt])
                else:
                    k.op(DVE, lambda e: e.tensor_copy(out=yt[:, m * NT:(m + 1) * NT], in_=bap), reads=[bk], writes=[B_yt])

            loadD(0)
            nctx.modulate(xts[0], B_xt[0], hTs[0], B_hs[0], l, 0, 1)
            for t in range(NTL):
                for m in range(KC):
                    mstepD(t, m)
                    if m == 1:
                        if t > 0:
                            postD(t - 1)
                        if t + 1 < NTL:
                            loadD(t + 1)
                    if m == 6 and t + 1 < NTL:
                        nctx.sq_part(xts[(t + 1) % 2][:], B_xt[(t + 1) % 2])
                for m in range(KC):
                    if m == 2 and t + 1 < NTL:
                        s1 = (t + 1) % 2
                        nctx.modulate(xts[s1], B_xt[s1], hTs[s1], B_hs[s1], l, 0, 1, do_sq=False)
                    ostepD(t, m)
                nctx2.sq_part(yt[:], B_yt)
            postD(NTL - 1)
            k.barrier()

        with ExitStack() as pe1:
            wg = sb(pe1, "wg", [128, KC, DFF], BF16)
            wu = sb(pe1, "wu", [128, KC, DFF], BF16)
            WGt = WChunks(wg, w_g[l], KC, 0, DFF, 512, "wg")
            WUp = WChunks(wu, w_u[l], KC, 0, DFF, 512, "wu")
            for i_ in range(WGt.n):
                WGt.load(i_); WUp.load(i_)
            xts = [sb(pe1, "xtE%d" % i, [128, KC * NT], F32) for i in range(2)]
            B_xt = [Buf("xtE%d" % i) for i in range(2)]
            ds_x = [k.dsem("xE") for _ in range(2)]
            hT = sb(pe1, "hTE", [128, KC, NT], BF16); B_h = Buf("hTE")
            sl_ = [sb(pe1, "sl%d" % i, [128, NT], F32) for i in range(2)]
            B_sl = [Buf("sl%d" % i) for i in range(2)]
            ao = [sb(pe1, "ao%d" % i, [128, NT], BF16) for i in range(4)]
            B_ao = [Buf("ao%d" % i) for i in range(4)]
            ds_ao = [k.dsem("ao") for _ in range(4)]
            nctx = NormCtx(pe1, "E")
            hTs = [hT, sb(pe1, "hTE2", [128, KC, NT], BF16)]
            B_hs = [B_h, Buf("hTE2")]
            k.dma(SP, xt3(xts[0]), x_view(x_mid, 0), ds_x[0], writes=[B_xt[0]])
            nctx.modulate(xts[0], B_xt[0], hTs[0], B_hs[0], l, 3, 4)
            it = 0
            for t in range(NTL):
                s = t % 2
                hT, B_h = hTs[s], B_hs[s]
                if t + 1 < NTL:
                    k.dma(SP, xt3(xts[1 - s]), x_view(x_mid, t + 1), ds_x[1 - s], writes=[B_xt[1 - s]])
                for m in range(FC):
                    if m == 9 and t + 1 < NTL:
                        nctx.sq_part(xts[1 - s][:], B_xt[1 - s])
                    if m == 12 and t + 1 < NTL:
                        nctx.modulate(xts[1 - s], B_xt[1 - s], hTs[1 - s], B_hs[1 - s], l, 3, 4, do_sq=False)
                    mc = slice(m * 128, (m + 1) * 128)
                    s2 = it % 2
                    s4 = it % 4
                    it += 1
                    bkg, bapg = next_bank()
                    k.mm(bkg, bapg, [(wg[:, kk, mc], hT[:, kk, :]) for kk in range(KC)], reads=[WGt.buf(m * 128), B_h])
                    k.op(ACT, lambda e: e.activation(out=sl_[s2][:], in_=bapg, func=AF.Silu), reads=[bkg], writes=[B_sl[s2]])
                    bku, bapu = next_bank()
                    k.mm(bku, bapu, [(wu[:, kk, mc], hT[:, kk, :]) for kk in range(KC)], reads=[WUp.buf(m * 128), B_h])
                    k.op(DVE, lambda e: e.tensor_tensor(out=ao[s4][:], in0=bapu, in1=sl_[s2][:], op=ALU.mult),
                         reads=[bku, B_sl[s2]], writes=[B_ao[s4]])
                    k.dma(SP, aT_d[mc, t * NT:(t + 1) * NT], ao[s4][:], ds_ao[s4], reads=[B_ao[s4]])
            k.barrier()

        with ExitStack() as pe2:
            wd = sb(pe2, "wd", [128, FC, D], BF16)
            WD = WChunks(wd, w_d[l], FC, 0, D, 256, "wd")
            for i_ in range(WD.n):
                WD.load(i_)
            xts = [sb(pe2, "xtF%d" % i, [128, KC * NT], F32) for i in range(2)]
            B_xt = [Buf("xtF%d" % i) for i in range(2)]
            ds_x = [k.dsem("xF") for _ in range(2)]
            at = [sb(pe2, "at%d" % i, [128, FC, NT], BF16) for i in range(2)]
            B_at = [Buf("at%d" % i) for i in range(2)]
            yt2 = [sb(pe2, "ytF%d" % i, [128, KC * NT], F32) for i in range(2)]
            B_yt2 = [Buf("ytF%d" % i) for i in range(2)]
            nctx = NormCtx(pe2, "F")

            ds_at = [k.dsem("at") for _ in range(2)]

            def load_at(t):
                s_ = t % 2
                av = aT_d.rearrange("(k p) n -> p k n", p=128)
                for a in range(0, FC, 11):
                    k.dma(SP, at[s_][:, a:a + 11, :], av[:, a:a + 11, t * NT:(t + 1) * NT], ds_at[s_], writes=[B_at[s_]])

            def load_x(t):
                s_ = t % 2
                k.dma(SP, xt3(xts[s_]), x_view(x_mid, t), ds_x[s_], writes=[B_xt[s_]])

            def postF(t):
                s_ = t % 2
                nctx.residual(yt2[s_], B_yt2[s_], xts[s_], B_xt[s_], l, 5, do_sq=False)
                k.dma(SP, x_view(x_post, t), xt3(xts[s_]), ds_x[s_], reads=[B_xt[s_]])

            load_at(0)
            load_x(0)
            if NTL > 1:
                load_x(1)
            for t in range(NTL):
                s = t % 2
                if t + 1 < NTL:
                    load_at(t + 1)
                for m in range(KC):
                    mc = slice(m * 128, (m + 1) * 128)
                    bk, bap = next_bank()
                    k.mm(bk, bap, [(wd[:, kk, mc], at[s][:, kk, :]) for kk in range(FC)], reads=[WD.buf(m * 128), B_at[s]])
                    if m % 2 == 0:
                        k.op(ACT, lambda e: e.copy(out=yt2[s][:, m * NT:(m + 1) * NT], in_=bap), reads=[bk], writes=[B_yt2[s]])
                    else:
                        k.op(DVE, lambda e: e.tensor_copy(out=yt2[s][:, m * NT:(m + 1) * NT], in_=bap), reads=[bk],
                             writes=[B_yt2[s]])
                    if m == 1 and t > 0:
                        postF(t - 1)
                        if t + 1 < NTL:
                            load_x(t + 1)
                nctx.sq_part(yt2[s][:], B_yt2[s])
            postF(NTL - 1)
            k.barrier()

    k.final_wait()
    es.close()
    return nc


def prep_shared(inp):
    f = lambda a: np.ascontiguousarray(np.asarray(a, dtype=np.float32))
    sh = {}
    sh["w_ada"] = f(inp["w_ada"])
    sh["b_adaT"] = f(np.asarray(inp["b_ada"]).reshape(L, 48, 128).transpose(0, 2, 1))
    g = np.stack([np.asarray(inp[n]).reshape(L, KC, 128).transpose(0, 2, 1)
                  for n in ("g_pre_mix", "g_post_mix", "g_pre_ffn", "g_post_ffn")], axis=2)
    sh["gT"] = f(g)
    sh["w_in"] = f(inp["w_in"])

    def ep(a):
        a = np.asarray(a)
        rest = a.shape[3:]
        a = a.reshape((L, 16, 2, 64) + rest)
        perm = (0, 2, 3, 1) + tuple(range(4, 4 + len(rest)))
        a = a.transpose(perm)
        return a.reshape((L, 128, 16) + rest)

    lam_re = ep(np.asarray(inp["lam_re"]))
    lam_im = ep(np.asarray(inp["lam_im"]))
    sh["lamT"] = f(np.stack([lam_re, lam_im], axis=2))
    ldt = np.broadcast_to(np.asarray(inp["log_dt"])[:, :, None], (L, 32, 64))
    sh["ldtT"] = f(ep(ldt))
    b_re = ep(np.asarray(inp["b_re"]))
    b_im = ep(np.asarray(inp["b_im"]))
    sh["bT"] = f(np.stack([b_re, b_im], axis=2))
    c_re = ep(np.asarray(inp["c_re"]).transpose(0, 1, 3, 2))
    c_im = ep(np.asarray(inp["c_im"]).transpose(0, 1, 3, 2))
    sh["cTT"] = f(np.stack([c_re, c_im], axis=2))
    sh["dskT"] = f(np.asarray(inp["d_skip"]).reshape(L, 4, 128).transpose(0, 2, 1))
    sh["w_glu"] = f(inp["w_glu"])
    sh["b_gluT"] = f(np.asarray(inp["b_glu"]).reshape(L, 4, 128).transpose(0, 2, 1))
    sh["b_f"] = f(np.asarray(inp["b_f"]).reshape(L, 8, 1))
    sh["w_pa"] = f(inp["w_pa"]); sh["w_pb"] = f(inp["w_pb"]); sh["w_o"] = f(inp["w_o"])
    sh["w_g"] = f(inp["w_ffn_gate"]); sh["w_u"] = f(inp["w_ffn_up"]); sh["w_d"] = f(inp["w_ffn_down"])
    kk = np.arange(128)
    sh["tri"] = f((kk[None, :] >= kk[:, None]).astype(np.float32))
    sh["ident"] = f(np.eye(128, dtype=np.float32))
    sh["bdm"] = f((kk[:, None] // 16 == kk[None, :] // 16).astype(np.float32))
    return sh


_NC_CACHE = {}


def kernel(**inputs):
    x = np.asarray(inputs["x"], dtype=np.float32)
    c = np.asarray(inputs["c"], dtype=np.float32)
    B, S, _ = x.shape
    sh = prep_shared(inputs)
    in_maps = []
    for b in range(B):
        m = dict(sh)
        m["xT"] = np.ascontiguousarray(x[b].T)
        m["cT"] = np.ascontiguousarray(c[b].reshape(KC, 128).T)
        in_maps.append(m)
    if S not in _NC_CACHE:
        _NC_CACHE[S] = build_nc(S)
    nc = _NC_CACHE[S]
    res = run_bass_kernel_spmd(nc, in_maps, core_ids=list(range(B)))
    out = np.stack([np.ascontiguousarray(np.asarray(r["yT"]).T) for r in res.results], axis=0)
    return out.astype(np.float32)
```
